# Optimizing a Trainium2 kernel written in Bass

```python
import jax, jax.numpy as jnp
from jax import lax
import numpy as np

D_MODEL = 4096
BATCH = 2
SEQ = 8192
DEPTH = 1

N_HEADS = 16
HEAD_DIM = 128
N_KV_GROUPS = 4
HEADS_PER_GROUP = N_HEADS // N_KV_GROUPS
NSA_WIDTH = N_HEADS * HEAD_DIM
KV_WIDTH = N_KV_GROUPS * HEAD_DIM
N_NSA_BRANCHES = 3
ROPE_DIM = HEAD_DIM // 4
ROPE_THETA = 500000.0
CMP_BLOCK = 32
CMP_STRIDE = 16
CMP_HIDDEN = 256
SLC_BLOCK = 64
SLC_TOPK = 16
WINDOW = 512
Q_BLOCK = 64
LRU_WIDTH = D_MODEL // 2
LRU_BLOCKS = 16
LRU_BLOCK_DIM = LRU_WIDTH // LRU_BLOCKS
CONV_WIDTH = 4
LRU_C = 8.0
N_BRANCHES = 2
FFN_HIDDEN = ((8 * D_MODEL + 2) // 3 + 255) // 256 * 256
NORM_EPS = 1e-6
NEG_INF = -1e30
POS_BIG = 1e30
IN_SIZES = (NSA_WIDTH, 6 * KV_WIDTH, N_HEADS * N_NSA_BRANCHES, LRU_WIDTH, LRU_WIDTH, N_BRANCHES * D_MODEL)
IN_WIDTH = int(sum(IN_SIZES))
IN_SPLITS = tuple(int(v) for v in np.cumsum(IN_SIZES)[:-1])

kernel_name = "hybrid_nsa_rglru_gated_block"


def _rmsnorm(x, g):
    xf = x.astype(jnp.float32)
    y = xf * lax.rsqrt(jnp.mean(xf * xf, axis=-1, keepdims=True) + NORM_EPS)
    return (y * g.astype(jnp.float32)).astype(x.dtype)


def _rope_partial(x, pos):
    half = ROPE_DIM // 2
    inv = 1.0 / (ROPE_THETA ** (jnp.arange(half, dtype=jnp.float32) * 2.0 / ROPE_DIM))
    ang = pos.astype(jnp.float32)[:, None] * inv[None, :]
    cos, sin = jnp.cos(ang), jnp.sin(ang)
    xr = x[..., :ROPE_DIM].astype(jnp.float32)
    x1, x2 = xr[..., :half], xr[..., half:]
    rot = jnp.concatenate([x1 * cos - x2 * sin, x2 * cos + x1 * sin], axis=-1)
    return jnp.concatenate([rot.astype(x.dtype), x[..., ROPE_DIM:]], axis=-1)


def _compress(kv, pos_emb, w1, b1, w2, b2):
    S = kv.shape[2]
    nc = (S - CMP_BLOCK) // CMP_STRIDE + 1
    idx = jnp.arange(nc)[:, None] * CMP_STRIDE + jnp.arange(CMP_BLOCK)[None, :]
    blocks = kv[:, :, idx, :] + pos_emb.astype(kv.dtype)
    flat = blocks.reshape(blocks.shape[:3] + (CMP_BLOCK * HEAD_DIM,))
    hid = jax.nn.gelu(flat @ w1 + b1)
    return hid @ w2 + b2


def _nsa(q, kv, gate_logits, q_norm_g, k_norm_g, cmp_pos, cmp_w1, cmp_b1, cmp_w2, cmp_b2):
    B, S, _ = q.shape
    dt = q.dtype
    f32 = jnp.float32
    pos = jnp.arange(S)
    scale = HEAD_DIM ** -0.5
    q = q.reshape(B, S, N_KV_GROUPS, HEADS_PER_GROUP, HEAD_DIM).transpose(0, 2, 3, 1, 4)
    q = _rope_partial(_rmsnorm(q, q_norm_g), pos)
    kv = kv.reshape(B, S, 6, N_KV_GROUPS, HEAD_DIM).transpose(2, 0, 3, 1, 4)
    k_cmp_raw, v_cmp_raw, k_slc, v_slc, k_win, v_win = kv[0], kv[1], kv[2], kv[3], kv[4], kv[5]

    nc = (S - CMP_BLOCK) // CMP_STRIDE + 1
    cmp_start = jnp.arange(nc) * CMP_STRIDE
    cmp_end = cmp_start + CMP_BLOCK - 1
    k_cmp = _compress(k_cmp_raw, cmp_pos[0], cmp_w1[0], cmp_b1[0], cmp_w2[0], cmp_b2[0])
    v_cmp = _compress(v_cmp_raw, cmp_pos[1], cmp_w1[1], cmp_b1[1], cmp_w2[1], cmp_b2[1])
    k_cmp = _rope_partial(_rmsnorm(k_cmp, k_norm_g[0]), cmp_end)
    k_slc = _rope_partial(_rmsnorm(k_slc, k_norm_g[1]), pos)
    k_win = _rope_partial(_rmsnorm(k_win, k_norm_g[2]), pos)

    ns = S // SLC_BLOCK
    topk = min(SLC_TOPK, ns)
    blk_ids = jnp.arange(ns)
    slc_start = blk_ids * SLC_BLOCK
    overlap = ((cmp_start[:, None] <= slc_start[None, :] + SLC_BLOCK - 1)
               & (cmp_end[:, None] >= slc_start[None, :])).astype(f32)
    k_slc_b = k_slc.reshape(B, N_KV_GROUPS, ns, SLC_BLOCK, HEAD_DIM)
    v_slc_b = v_slc.reshape(B, N_KV_GROUPS, ns, SLC_BLOCK, HEAD_DIM)
    pad = ((0, 0), (0, 0), (WINDOW, 0), (0, 0))
    k_win_p = jnp.pad(k_win, pad)
    v_win_p = jnp.pad(v_win, pad)
    gather_blocks = jax.vmap(jax.vmap(lambda kb, ix: kb[ix]))

    nq = S // Q_BLOCK
    q_blocks = q.reshape(B, N_KV_GROUPS, HEADS_PER_GROUP, nq, Q_BLOCK, HEAD_DIM).transpose(3, 0, 1, 2, 4, 5)
    g = jax.nn.sigmoid(gate_logits.astype(f32)).reshape(B, S, N_KV_GROUPS, HEADS_PER_GROUP, N_NSA_BRANCHES)
    g_blocks = g.transpose(0, 2, 3, 1, 4).reshape(B, N_KV_GROUPS, HEADS_PER_GROUP, nq, Q_BLOCK, N_NSA_BRANCHES)
    g_blocks = g_blocks.transpose(3, 0, 1, 2, 4, 5)

    def block(args):
        qb, gb, i = args
        q0 = i * Q_BLOCK
        t = q0 + jnp.arange(Q_BLOCK)
        s = jnp.einsum('bghqd,bgcd->bghqc', qb, k_cmp).astype(f32) * scale
        m = cmp_end[None, :] <= t[:, None]
        p_cmp = jax.nn.softmax(jnp.where(m, s, NEG_INF), axis=-1) * m
        o_cmp = jnp.einsum('bghqc,bgcd->bghqd', p_cmp.astype(dt), v_cmp).astype(f32)
        imp = jnp.einsum('bghqc,cn->bgqn', p_cmp, overlap)
        cur = t // SLC_BLOCK
        valid = slc_start[None, :] <= t[:, None]
        forced = ((blk_ids[None, :] == 0) | (blk_ids[None, :] == cur[:, None])
                  | (blk_ids[None, :] == cur[:, None] - 1))
        imp = jnp.where(forced, POS_BIG, jnp.where(valid, imp, NEG_INF))
        _, sel = lax.top_k(imp, topk)
        ks = gather_blocks(k_slc_b, sel)
        vs = gather_blocks(v_slc_b, sel)
        tok = sel[..., None] * SLC_BLOCK + jnp.arange(SLC_BLOCK)
        msk = tok <= t[:, None, None]
        s = jnp.einsum('bghqd,bgqnkd->bghqnk', qb, ks).astype(f32) * scale
        s = jnp.where(msk[:, :, None], s, NEG_INF).reshape(s.shape[:4] + (topk * SLC_BLOCK,))
        p = jax.nn.softmax(s, axis=-1).reshape(s.shape[:4] + (topk, SLC_BLOCK))
        o_slc = jnp.einsum('bghqnk,bgqnkd->bghqd', p.astype(dt), vs).astype(f32)
        kw = lax.dynamic_slice_in_dim(k_win_p, q0, WINDOW + Q_BLOCK, axis=2)
        vw = lax.dynamic_slice_in_dim(v_win_p, q0, WINDOW + Q_BLOCK, axis=2)
        kpos = q0 - WINDOW + jnp.arange(WINDOW + Q_BLOCK)
        mw = (kpos[None, :] >= 0) & (kpos[None, :] <= t[:, None]) & (kpos[None, :] > t[:, None] - WINDOW)
        s = jnp.einsum('bghqd,bgkd->bghqk', qb, kw).astype(f32) * scale
        p = jax.nn.softmax(jnp.where(mw, s, NEG_INF), axis=-1)
        o_win = jnp.einsum('bghqk,bgkd->bghqd', p.astype(dt), vw).astype(f32)
        o = gb[..., 0:1] * o_cmp + gb[..., 1:2] * o_slc + gb[..., 2:3] * o_win
        return o.astype(dt)

    o = lax.map(block, (q_blocks, g_blocks, jnp.arange(nq)))
    return o.transpose(1, 0, 4, 2, 3, 5).reshape(B, S, NSA_WIDTH)


def _lin_comb(left, right):
    a1, b1 = left
    a2, b2 = right
    return a1 * a2, a2 * b1 + b2


def _rglru(xb, yb, conv_w, conv_b, w_gates, b_gates, lam):
    B, S, W = xb.shape
    f32 = jnp.float32
    xc = lax.conv_general_dilated(xb, conv_w[:, None, :].astype(xb.dtype), window_strides=(1,),
                                  padding=((CONV_WIDTH - 1, 0),), dimension_numbers=('NWC', 'WIO', 'NWC'),
                                  feature_group_count=W) + conv_b
    xh = xc.reshape(B, S, LRU_BLOCKS, LRU_BLOCK_DIM)
    gates = jnp.einsum('bshc,ghcd->gbshd', xh, w_gates).reshape(2, B, S, W) + b_gates[:, None, None, :]
    r = jax.nn.sigmoid(gates[0].astype(f32))
    i = jax.nn.sigmoid(gates[1].astype(f32))
    log_a = -LRU_C * r * jax.nn.softplus(-lam.astype(f32))
    a = jnp.exp(log_a)
    b = jnp.sqrt(-jnp.expm1(2.0 * log_a)) * (i * xc.astype(f32))
    _, h = lax.associative_scan(_lin_comb, (a, b), axis=1)
    return (h * jax.nn.gelu(yb.astype(f32))).astype(xb.dtype)


def setup_inputs(seed: int = 0) -> dict:
    key = jax.random.key(seed)
    ks = jax.random.split(key, 24)
    f32 = jnp.float32
    L = DEPTH

    def nrm(k, shape, scale):
        return jax.random.normal(k, shape, f32) * scale

    u = jax.random.uniform(ks[14], (L, LRU_WIDTH), f32, 0.9, 0.999)
    a0 = u ** (1.0 / LRU_C)
    return {
        "x": nrm(ks[0], (BATCH, SEQ, D_MODEL), 1.0),
        "norm1_g": 1.0 + nrm(ks[1], (L, D_MODEL), 0.02),
        "w_in": nrm(ks[2], (L, D_MODEL, IN_WIDTH), D_MODEL ** -0.5),
        "q_norm_g": 1.0 + nrm(ks[3], (L, HEAD_DIM), 0.02),
        "k_norm_g": 1.0 + nrm(ks[4], (L, N_NSA_BRANCHES, HEAD_DIM), 0.02),
        "cmp_pos": nrm(ks[5], (L, 2, CMP_BLOCK, HEAD_DIM), 0.1),
        "cmp_w1": nrm(ks[6], (L, 2, CMP_BLOCK * HEAD_DIM, CMP_HIDDEN), (CMP_BLOCK * HEAD_DIM) ** -0.5),
        "cmp_b1": nrm(ks[7], (L, 2, CMP_HIDDEN), 0.01),
        "cmp_w2": nrm(ks[8], (L, 2, CMP_HIDDEN, HEAD_DIM), CMP_HIDDEN ** -0.5),
        "cmp_b2": nrm(ks[9], (L, 2, HEAD_DIM), 0.01),
        "conv_w": nrm(ks[10], (L, CONV_WIDTH, LRU_WIDTH), CONV_WIDTH ** -0.5),
        "conv_b": nrm(ks[11], (L, LRU_WIDTH), 0.01),
        "lru_w_gates": nrm(ks[12], (L, 2, LRU_BLOCKS, LRU_BLOCK_DIM, LRU_BLOCK_DIM), LRU_BLOCK_DIM ** -0.5),
        "lru_b_gates": nrm(ks[13], (L, 2, LRU_WIDTH), 0.01),
        "lru_lambda": jnp.log(a0) - jnp.log1p(-a0),
        "w_branch_a": nrm(ks[15], (L, NSA_WIDTH, D_MODEL), NSA_WIDTH ** -0.5),
        "w_branch_b": nrm(ks[16], (L, LRU_WIDTH, D_MODEL), LRU_WIDTH ** -0.5),
        "w_out": nrm(ks[17], (L, D_MODEL, D_MODEL), D_MODEL ** -0.5),
        "norm2_g": 1.0 + nrm(ks[18], (L, D_MODEL), 0.02),
        "w_ffn_in": nrm(ks[19], (L, D_MODEL, 2 * FFN_HIDDEN), D_MODEL ** -0.5),
        "w_ffn_out": nrm(ks[20], (L, FFN_HIDDEN, D_MODEL), FFN_HIDDEN ** -0.5),
    }


def reference(x, norm1_g, w_in, q_norm_g, k_norm_g, cmp_pos, cmp_w1, cmp_b1, cmp_w2, cmp_b2,
              conv_w, conv_b, lru_w_gates, lru_b_gates, lru_lambda, w_branch_a, w_branch_b,
              w_out, norm2_g, w_ffn_in, w_ffn_out):
    for l in range(DEPTH):
        h = _rmsnorm(x, norm1_g[l])
        proj = h @ w_in[l]
        q, kv, nsa_g, lru_x, lru_y, merge_g = jnp.split(proj, list(IN_SPLITS), axis=-1)
        y_a = _nsa(q, kv, nsa_g, q_norm_g[l], k_norm_g[l], cmp_pos[l], cmp_w1[l], cmp_b1[l],
                   cmp_w2[l], cmp_b2[l]) @ w_branch_a[l]
        y_b = _rglru(lru_x, lru_y, conv_w[l], conv_b[l], lru_w_gates[l], lru_b_gates[l],
                     lru_lambda[l]) @ w_branch_b[l]
        g = jax.nn.sigmoid(merge_g.astype(jnp.float32)).astype(x.dtype)
        mixed = g[..., :D_MODEL] * y_a + g[..., D_MODEL:] * y_b
        x = x + mixed @ w_out[l]
        h = _rmsnorm(x, norm2_g[l])
        gate, up = jnp.split(h @ w_ffn_in[l], 2, axis=-1)
        x = x + (jax.nn.silu(gate) * up) @ w_ffn_out[l]
    return x
```

```python
import math
from contextlib import ExitStack
import numpy as np
import concourse.bass as bass
import concourse.mybir as mybir
from concourse.bass_utils import run_bass_kernel_spmd

F32, BF16 = mybir.dt.float32, mybir.dt.bfloat16
AF = mybir.ActivationFunctionType
ALU = mybir.AluOpType
AX = mybir.AxisListType
NDMA = 16
P = 128
NEGB = -30000.0
EPS = 1e-6


class Ctx:
    def __init__(self, nc, es):
        self.nc = nc
        self.E = {'pe': nc.tensor, 'act': nc.scalar, 'dve': nc.vector, 'pool': nc.gpsimd, 'sp': nc.sync}
        self.sem = {k: es.enter_context(nc.semaphore('s_' + k)) for k in ('pe', 'act', 'dve', 'pool')}
        self.dsem = [es.enter_context(nc.semaphore('s_d%d' % i)) for i in range(NDMA)]
        self.cnt = {k: 0 for k in self.sem}
        self.pend = {k: False for k in self.sem}
        self.duse = [0] * NDMA
        self.di = 0
        self.waited = {k: {} for k in self.E}
        self.tiles = {}
        self.nwait = 0

    def _semh(self, key):
        return self.sem[key] if isinstance(key, str) else self.dsem[key[1]]

    def _wait(self, X, evs):
        need = {}
        for (k, v) in evs:
            if v > need.get(k, 0):
                need[k] = v
        for k, v in need.items():
            if X == 'pe' and k == 'pe':
                continue
            if self.waited[X].get(k, 0) >= v:
                continue
            self.E[X].wait_ge(self._semh(k), v)
            self.waited[X][k] = v
            self.nwait += 1

    def _deps(self, reads, writes):
        evs = []
        for k in reads:
            t = self.tiles.get(k)
            if t and t[0]:
                evs.append(t[0])
        for k in writes:
            t = self.tiles.get(k)
            if t:
                if t[0]:
                    evs.append(t[0])
                evs.extend(t[1].items())
        return evs

    def _record(self, ev, reads, writes):
        for k in reads:
            t = self.tiles.setdefault(k, [None, {}])
            if ev[1] > t[1].get(ev[0], 0):
                t[1][ev[0]] = ev[1]
        for k in writes:
            self.tiles[k] = [ev, {}]

    def op(self, X, emit, reads=(), writes=(), inc=True):
        self._wait(X, self._deps(reads, writes))
        ins = emit(self.E[X])
        if inc:
            self.cnt[X] += 1
            ins.then_inc(self.sem[X], 1)
            ev = (X, self.cnt[X])
            self.pend[X] = False
        else:
            ev = (X, self.cnt[X] + 1)
            self.pend[X] = True
        self._record(ev, reads, writes)
        return ins

    def dma(self, Q, out, in_, reads=(), writes=(), **kw):
        i = self.di % NDMA
        self.di += 1
        n = self.duse[i]
        evs = self._deps(reads, writes)
        if n > 0:
            evs.append((('d', i), 16 * n))
        self._wait(Q, evs)
        self.E[Q].dma_start(out=out, in_=in_, **kw).then_inc(self.dsem[i], 16)
        self.duse[i] = n + 1
        self._record((('d', i), 16 * (n + 1)), reads, writes)

    def barrier(self):
        for k in self.pend:
            assert not self.pend[k], k
        evs = [(k, self.cnt[k]) for k in self.cnt if self.cnt[k] > 0]
        evs += [(('d', i), 16 * self.duse[i]) for i in range(NDMA) if self.duse[i] > 0]
        for X in self.E:
            self._wait(X, evs)
        self.tiles = {}


class Ring:
    def __init__(self, name, t, n, base=0):
        self.name, self.t, self.n, self.i, self.base = name, t, n, 0, base

    def next(self):
        s = self.base + self.i % self.n
        self.i += 1
        return self.t[:, s], (self.name, s)


def bcast_rows(ap1d, n):
    return ap1d.partition_broadcast(P)


def build_program(cfg, debug_outs=()):
    DM, TCTX, TOWN, LW, FH = cfg['DM'], cfg['TCTX'], cfg['TOWN'], cfg['LW'], cfg['FH']
    KT = DM // P
    NB = TCTX // 64
    NCP = TCTX // 16
    NC = NCP - 1
    NCT = NCP // P
    NCHC = TCTX // 512
    NCHO = TOWN // 512
    OWN0 = TCTX - TOWN
    NLT = LW // P
    INW = 2048 + 3072 + 48 + 2 * LW + 2 * DM
    C_Q, C_KV, C_G = 0, 2048, 5120
    C_LX = 5168
    C_LY = C_LX + LW
    C_MG = C_LY + LW

    nc = bass.Bass("TRN2", target_bir_lowering=False)

    def din(name, shape, dt=F32):
        return nc.dram_tensor(name, list(shape), dt, kind="ExternalInput").ap()

    def dscr(name, shape, dt):
        kind = "ExternalOutput" if name in debug_outs else "Internal"
        return nc.dram_tensor(name, list(shape), dt, kind=kind).ap()

    x_ctx = din("x_ctx", [TCTX, DM])
    norm1_g = din("norm1_g", [DM]); norm2_g = din("norm2_g", [DM])
    w_in = din("w_in", [DM, INW])
    q_norm_g = din("q_norm_g", [128]); k_norm_g = din("k_norm_g", [3, 128])
    cmp_pos = din("cmp_pos", [2, 32, 128]); cmp_w1 = din("cmp_w1", [2, 4096, 256])
    cmp_b1 = din("cmp_b1", [2, 256]); cmp_w2 = din("cmp_w2", [2, 256, 128]); cmp_b2 = din("cmp_b2", [2, 128])
    conv_w = din("conv_w", [4, LW]); conv_b = din("conv_b", [LW])
    lru_wg = din("lru_w_gates", [2, NLT, 128, 128]); lru_bg = din("lru_b_gates", [2, LW])
    lru_lam = din("lru_lambda", [LW])
    w_bra = din("w_branch_a", [2048, DM]); w_brb = din("w_branch_b", [LW, DM])
    w_out = din("w_out", [DM, DM])
    w_ffi = din("w_ffn_in", [DM, 2 * FH]); w_ffo = din("w_ffn_out", [FH, DM])
    cs_tok = din("cs_tok", [TCTX, 32]); cs_cmp = din("cs_cmp", [NCP, 32])
    t_cmpmask = din("t_cmpmask", [P, NCHO, NCT, 512])
    t_overlap = din("t_overlap", [NCP, NB])
    t_E = din("t_E", [NB, TCTX])
    t_slcdiag = din("t_slcdiag", [P, 4, 512])
    t_winmask = din("t_winmask", [P, 2, 8, 512])
    t_topA = din("t_topA", [TOWN, NB]); t_topB = din("t_topB", [TOWN, NB])
    t_vchunk = din("t_vchunk", [P, NCHC])
    out = nc.dram_tensor("out", [TOWN, DM], F32, kind="ExternalOutput").ap()

    hT_d = dscr("hT_d", [DM, TCTX], BF16)
    kcmpT_d = dscr("kcmpT_d", [4, P, TCTX], BF16); vcmpT_d = dscr("vcmpT_d", [4, P, TCTX], BF16)
    kslcT_d = dscr("kslcT_d", [4, P, TCTX], BF16); kwinT_d = dscr("kwinT_d", [4, P, TCTX], BF16)
    vslc_d = dscr("vslc_d", [TCTX, 512], BF16); vwin_d = dscr("vwin_d", [TCTX, 512], BF16)
    hlru_d = dscr("hlru_d", [LW, TOWN], F32)
    qT_d = dscr("qT_d", [16, P, TOWN], BF16)
    lruoT_d = dscr("lruoT_d", [LW, TOWN], BF16)
    gT_d = dscr("gT_d", [2 * DM, TOWN], BF16)
    attnT_d = dscr("attnT_d", [2048, TOWN], BF16)
    mixT_d = dscr("mixT_d", [DM, TOWN], BF16)
    x1_d = dscr("x1_d", [TOWN, DM], F32)
    h2T_d = dscr("h2T_d", [DM, TOWN], BF16)
    actT_d = dscr("actT_d", [FH, TOWN], BF16)

    es = ExitStack()
    with es:
        C = Ctx(nc, es)

        def sb(st, name, shape, dt):
            return st.enter_context(nc.sbuf_tensor(name, list(shape), dt))[:]

        ident = sb(es, "ident", [P, P], BF16)
        identf = sb(es, "identf", [P, P], F32)
        gate_sb = sb(es, "gate_sb", [P, TOWN // P, 48], F32)
        psF = es.enter_context(nc.psum_tensor("psF", [P, 6, 512], F32))[:]
        psB = es.enter_context(nc.psum_tensor("psB", [P, 2, 1024], BF16))[:]
        PSF = Ring("psF", psF, 6)
        PSB = Ring("psB", psB, 2)

        C.op('pool', lambda e: e.memset(identf[:], 1.0), writes=['identf'])
        C.op('pool', lambda e: e.affine_select(out=identf[:], in_=identf[:], pattern=[[-1, P]],
                                               compare_op=ALU.is_equal, fill=0.0, base=0, channel_multiplier=1),
             reads=['identf'], writes=['identf'])
        C.op('dve', lambda e: e.tensor_copy(out=ident[:], in_=identf[:]), reads=['identf'], writes=['ident'])

        def phase_norm(tag, src_d, ntok, g_d, dst_d):
            with ExitStack() as st:
                gb = sb(st, tag + "gb", [P, DM], F32)
                xr = sb(st, tag + "x", [P, 2, DM], F32)
                junk = sb(st, tag + "junk", [P, DM], BF16)
                xn = sb(st, tag + "xn", [P, 2, DM], BF16)
                ss = sb(st, tag + "ss", [P, 4, 2], F32)
                stg = sb(st, tag + "stg", [P, 2, KT, 512], BF16)
                XR = Ring(tag + "x", xr, 2); XN = Ring(tag + "xn", xn, 2); SS = Ring(tag + "ss", ss, 4)
                STG = Ring(tag + "stg", stg, 2)
                C.dma('sp', gb[:], g_d.partition_broadcast(P), writes=[tag + 'gb'])
                for ch in range(ntok // 512):
                    sg, sgk = STG.next()
                    for sub in range(4):
                        t0 = ch * 512 + sub * P
                        xt, xk = XR.next()
                        C.dma('sp', xt, src_d[t0:t0 + P, :], writes=[xk])
                        s_, sk = SS.next()
                        C.op('act', lambda e: e.activation(out=junk[:], in_=xt, func=AF.Square, accum_out=s_[:, 0:1]),
                             reads=[xk], writes=[tag + 'junk', sk])
                        C.op('dve', lambda e: e.tensor_scalar(out=s_[:, 1:2], in0=s_[:, 0:1], scalar1=1.0 / DM, scalar2=EPS,
                                                              op0=ALU.mult, op1=ALU.add), reads=[sk], writes=[sk])
                        C.op('act', lambda e: e.sqrt(out=s_[:, 1:2], in_=s_[:, 1:2]), reads=[sk], writes=[sk])
                        C.op('dve', lambda e: e.reciprocal(out=s_[:, 1:2], in_=s_[:, 1:2]), reads=[sk], writes=[sk])
                        xb, xbk = XN.next()
                        C.op('dve', lambda e: e.scalar_tensor_tensor(out=xb, in0=xt, scalar=s_[:, 1:2], in1=gb[:],
                                                                     op0=ALU.mult, op1=ALU.mult),
                             reads=[xk, sk, tag + 'gb'], writes=[xbk])
                        nb8 = min(8, KT)
                        for k8 in range(KT // nb8):
                            pt, pk = PSB.next()
                            for j in range(nb8):
                                kt = k8 * nb8 + j
                                C.op('pe', lambda e: e.transpose(out=pt[:, j * P:(j + 1) * P], in_=xb[:, kt * P:(kt + 1) * P],
                                                                 identity=ident[:]),
                                     reads=[xbk, 'ident'], writes=[pk], inc=(j == nb8 - 1))
                            dst = sg[:, k8 * nb8:(k8 + 1) * nb8, sub * P:(sub + 1) * P]
                            src = pt[:, 0:nb8 * P].rearrange("p (k t) -> p k t", k=nb8)
                            eng = 'act' if (k8 % 2 == 0) else 'dve'
                            if eng == 'act':
                                C.op('act', lambda e: e.copy(out=dst, in_=src), reads=[pk], writes=[sgk])
                            else:
                                C.op('dve', lambda e: e.tensor_copy(out=dst, in_=src), reads=[pk], writes=[sgk])
                    dv = dst_d.rearrange("(k p) t -> p k t", p=P)
                    half = KT // 2
                    C.dma('sp', dv[:, 0:half, ch * 512:(ch + 1) * 512], sg[:, 0:half, :], reads=[sgk])
                    C.dma('sp', dv[:, half:KT, ch * 512:(ch + 1) * 512], sg[:, half:KT, :], reads=[sgk])
            C.barrier()

        def gemm(tag, form, colblocks, tok0, ntok, TC, epilogue, wslots=2, aslots=2, resident=False, pre=None, lq='act'):
            ktot = max(sum(g[1] for g in blk) for blk in colblocks)
            wmax = max(sum(pc[2] for pc in g[2]) for blk in colblocks for g in blk)
            nch = ntok // TC
            if resident:
                aslots = nch
            with ExitStack() as st:
                wb = sb(st, tag + "w", [P, wslots, ktot, wmax], BF16)
                ab = sb(st, tag + "a", [P, aslots, ktot, TC], BF16)
                WB = Ring(tag + "w", wb, wslots); AB = Ring(tag + "a", ab, aslots)
                work = [(bi, ci) for bi in range(len(colblocks)) for ci in range(nch)]
                wstate = {}
                astate = {}

                def load_w(bi):
                    if bi >= len(colblocks) or bi in wstate:
                        return
                    blk = colblocks[bi]
                    wt, wk = WB.next()
                    wkeys = []
                    k0 = 0
                    for (act_d, ktn, pieces) in blk:
                        c0 = 0
                        for (w_d, col0, width) in pieces:
                            wv = w_d.rearrange("(k p) n -> p k n", p=P)
                            step = 8
                            for ks in range(0, ktn, step):
                                ke = min(ktn, ks + step)
                                wkeys.append((wk, k0 + ks, c0))
                                C.dma('pool', wt[:, k0 + ks:k0 + ke, c0:c0 + width], wv[:, ks:ke, col0:col0 + width],
                                      writes=[wkeys[-1]])
                            c0 += width
                        k0 += ktn
                    wstate[bi] = (wt, wkeys)

                def load_a(idx):
                    if idx >= len(work) or idx in astate:
                        return
                    bi, ci = work[idx]
                    blk = colblocks[bi]
                    t0 = tok0 + ci * TC
                    if resident and bi > 0:
                        at, akeys, _ = astate[ci]
                    else:
                        at, ak = AB.next()
                        akeys = []
                        k0 = 0
                        for (act_d, ktn, pieces) in blk:
                            av = act_d.rearrange("(k p) t -> p k t", p=P)
                            step = 16
                            for ks in range(0, ktn, step):
                                ke = min(ktn, ks + step)
                                akeys.append((ak, k0 + ks))
                                C.dma(lq, at[:, k0 + ks:k0 + ke, :], av[:, ks:ke, t0:t0 + TC], writes=[akeys[-1]])
                            k0 += ktn
                    pr = pre(bi, ci, t0) if pre is not None else None
                    astate[idx] = (at, akeys, pr)

                load_w(0)
                load_a(0)
                for idx, (bi, ci) in enumerate(work):
                    blk = colblocks[bi]
                    t0 = tok0 + ci * TC
                    if ci == 0:
                        load_w(bi)
                        if wslots >= 2:
                            load_w(bi + 1)
                    load_a(idx + 1)
                    wt, wkeys = wstate[bi]
                    at, akeys, pr = astate[idx]
                    pts = []
                    if form == 'b':
                        k0 = 0
                        for (act_d, ktn, pieces) in blk:
                            wsum = sum(pc[2] for pc in pieces)
                            for ct in range(wsum // P):
                                pt, pk = PSF.next()
                                for kt in range(ktn):
                                    C.op('pe', lambda e: e.matmul(pt[:, 0:TC], lhsT=wt[:, k0 + kt, ct * P:(ct + 1) * P],
                                                                  rhs=at[:, k0 + kt, :], start=(kt == 0), stop=(kt == ktn - 1)),
                                         reads=wkeys + akeys, writes=[pk], inc=(kt == ktn - 1))
                                pts.append((pt, pk))
                            k0 += ktn
                    else:
                        nw = sum(pc[2] for pc in blk[0][2])
                        for sub in range(TC // P):
                            pt, pk = PSF.next()
                            for kt in range(ktot):
                                C.op('pe', lambda e: e.matmul(pt[:, 0:nw], lhsT=at[:, kt, sub * P:(sub + 1) * P],
                                                              rhs=wt[:, kt, 0:nw], start=(kt == 0), stop=(kt == ktot - 1)),
                                     reads=wkeys + akeys, writes=[pk], inc=(kt == ktot - 1))
                            pts.append((pt, pk))
                    if pre is not None:
                        epilogue(bi, ci, t0, pts, pr)
                    else:
                        epilogue(bi, ci, t0, pts)
                    if not resident:
                        del astate[idx]
                    if ci == nch - 1:
                        del wstate[bi]
            C.barrier()

        SCALE = 128.0 ** -0.5
        stages = cfg.get('stages', 99)

        def evac_copy(i, dst, src, rk, wk):
            if i % 2 == 0:
                C.op('act', lambda e: e.copy(out=dst, in_=src), reads=rk, writes=wk)
            else:
                C.op('dve', lambda e: e.tensor_copy(out=dst, in_=src), reads=rk, writes=wk)

        def normrope(T, xf, xfk, nh, gb_ap, gbk, cs, csk, outb, outk):
            sq, sqk = T['sq'].next(); stt, stk = T['st'].next(); tr, trk = T['tr'].next()
            W = nh * 128
            x3 = xf[:, 0:W].rearrange("p (h d) -> p h d", h=nh)
            o3 = outb[:, 0:W].rearrange("p (h d) -> p h d", h=nh)
            C.op('act', lambda e: e.activation(out=sq[:, 0:W], in_=xf[:, 0:W], func=AF.Square), reads=[xfk], writes=[sqk])
            C.op('dve', lambda e: e.tensor_reduce(out=stt[:, 0:nh], in_=sq[:, 0:W].rearrange("p (h d) -> p h d", h=nh),
                                                  axis=AX.X, op=ALU.add), reads=[sqk], writes=[stk])
            C.op('dve', lambda e: e.tensor_scalar(out=stt[:, 0:nh], in0=stt[:, 0:nh], scalar1=1.0 / 128, scalar2=EPS,
                                                  op0=ALU.mult, op1=ALU.add), reads=[stk], writes=[stk])
            C.op('act', lambda e: e.sqrt(out=stt[:, 0:nh], in_=stt[:, 0:nh]), reads=[stk], writes=[stk])
            C.op('dve', lambda e: e.reciprocal(out=stt[:, 0:nh], in_=stt[:, 0:nh]), reads=[stk], writes=[stk])
            C.op('dve', lambda e: e.tensor_tensor(out=x3, in0=x3, in1=stt[:, 0:nh].unsqueeze(2).to_broadcast([P, nh, 128]),
                                                  op=ALU.mult), reads=[xfk, stk], writes=[xfk])
            C.op('dve', lambda e: e.tensor_tensor(out=x3, in0=x3, in1=gb_ap.unsqueeze(1).to_broadcast([P, nh, 128]),
                                                  op=ALU.mult), reads=[xfk, gbk], writes=[xfk])
            C.op('act', lambda e: e.copy(out=outb[:, 0:W], in_=xf[:, 0:W]), reads=[xfk], writes=[outk])
            x1 = x3[:, :, 0:16]; x2 = x3[:, :, 16:32]
            cb = cs[:, 0:16].unsqueeze(1).to_broadcast([P, nh, 16]); sbb = cs[:, 16:32].unsqueeze(1).to_broadcast([P, nh, 16])
            C.op('dve', lambda e: e.tensor_tensor(out=tr[:, 0, 0:nh, :], in0=x1, in1=cb, op=ALU.mult), reads=[xfk, csk], writes=[trk])
            C.op('dve', lambda e: e.tensor_tensor(out=tr[:, 1, 0:nh, :], in0=x2, in1=sbb, op=ALU.mult), reads=[xfk, csk], writes=[trk])
            C.op('dve', lambda e: e.tensor_tensor(out=tr[:, 2, 0:nh, :], in0=x2, in1=cb, op=ALU.mult), reads=[xfk, csk], writes=[trk])
            C.op('dve', lambda e: e.tensor_tensor(out=tr[:, 3, 0:nh, :], in0=x1, in1=sbb, op=ALU.mult), reads=[xfk, csk], writes=[trk])
            C.op('dve', lambda e: e.tensor_tensor(out=o3[:, :, 0:16], in0=tr[:, 0, 0:nh, :], in1=tr[:, 1, 0:nh, :], op=ALU.subtract),
                 reads=[trk], writes=[outk])
            C.op('dve', lambda e: e.tensor_tensor(out=o3[:, :, 16:32], in0=tr[:, 2, 0:nh, :], in1=tr[:, 3, 0:nh, :], op=ALU.add),
                 reads=[trk], writes=[outk])

        def nr_temps(st, tag):
            sq = sb(st, tag + "sq", [P, 2, 512], F32); stt = sb(st, tag + "st", [P, 2, 8], F32)
            tr = sb(st, tag + "tr", [P, 2, 4, 4, 16], F32)
            return dict(sq=Ring(tag + "sq", sq, 2), st=Ring(tag + "st", stt, 2), tr=Ring(tag + "tr", tr, 2))

        phase_norm("n1", x_ctx, TCTX, norm1_g, hT_d)

        if stages >= 1:
            with ExitStack() as st:
                stg = sb(st, "g1a_stg", [P, 2, 4, 512], BF16); STG = Ring("g1a_stg", stg, 2)

                def epi(bi, ci, t0, pts):
                    sg, sgk = STG.next()
                    for j, (pt, pk) in enumerate(pts):
                        evac_copy(j, sg[:, j, :], pt[:, 0:512], [pk], [sgk])
                    dst = (kcmpT_d if bi == 0 else vcmpT_d)[:, :, t0:t0 + 512].rearrange("g p t -> p g t")
                    C.dma('sp', dst, sg, reads=[sgk])
                gemm("g1a", 'b', [[(hT_d, KT, [(w_in, 2048, 512)])], [(hT_d, KT, [(w_in, 2560, 512)])]], 0, TCTX, 512, epi)

        if stages >= 2:
            with ExitStack() as st:
                T = nr_temps(st, "g1b")
                gkb = sb(st, "g1b_gk", [P, 2, 128], F32)
                xf = sb(st, "g1b_xf", [P, 2, 512], F32); XF = Ring("g1b_xf", xf, 2)
                kb = sb(st, "g1b_kb", [P, 2, 512], BF16); KB = Ring("g1b_kb", kb, 2)
                cst = sb(st, "g1b_cs", [P, TCTX // P, 32], F32)
                C.dma('sp', cst, cs_tok.rearrange("(t p) c -> p t c", p=P), writes=['g1b_cs'])
                stg = sb(st, "g1b_stg", [P, 2, 4, 512], BF16); STG = Ring("g1b_stg", stg, 2)
                C.dma('sp', gkb[:, 0, :], k_norm_g[1].partition_broadcast(P), writes=['g1b_gk'])
                C.dma('sp', gkb[:, 1, :], k_norm_g[2].partition_broadcast(P), writes=['g1b_gk'])

                def epi(bi, ci, t0, pts):
                    sg, sgk = STG.next()
                    if bi in (1, 3):
                        for j, (pt, pk) in enumerate(pts):
                            evac_copy(j, sg[:, j, :], pt[:, 0:512], [pk], [sgk])
                        dst = (vslc_d if bi == 1 else vwin_d)[t0:t0 + 512, :].rearrange("(s p) c -> p s c", p=P)
                        C.dma('sp', dst, sg, reads=[sgk])
                    else:
                        for j, (pt, pk) in enumerate(pts):
                            x_, xk = XF.next()
                            C.op('act', lambda e: e.copy(out=x_, in_=pt[:, 0:512]), reads=[pk], writes=[xk])
                            c_, ck = cst[:, t0 // P + j, :], 'g1b_cs'
                            k_, kk = KB.next()
                            normrope(T, x_, xk, 4, gkb[:, 0 if bi == 0 else 1, :], 'g1b_gk', c_, ck, k_, kk)
                            p2, p2k = PSB.next()
                            for g in range(4):
                                C.op('pe', lambda e: e.transpose(out=p2[:, g * P:(g + 1) * P], in_=k_[:, g * P:(g + 1) * P],
                                                                 identity=ident[:]), reads=[kk, 'ident'], writes=[p2k], inc=(g == 3))
                            evac_copy(j, sg[:, :, j * P:(j + 1) * P], p2[:, 0:512].rearrange("p (g t) -> p g t", g=4), [p2k], [sgk])
                        dst = (kslcT_d if bi == 0 else kwinT_d)[:, :, t0:t0 + 512].rearrange("g p t -> p g t")
                        C.dma('sp', dst, sg, reads=[sgk])
                gemm("g1b", 'a', [[(hT_d, KT, [(w_in, 3072 + 512 * b_, 512)])] for b_ in range(4)], 0, TCTX, 512, epi)

        if stages >= 3:
            with ExitStack() as st:
                cw = sb(st, "l_cw", [P, 4, NLT], F32); cbias = sb(st, "l_cb", [P, NLT], F32)
                bg = sb(st, "l_bg", [P, 2, NLT], F32); lam = sb(st, "l_lam", [P, NLT], F32)
                sc = sb(st, "l_sc", [P, NLT], F32); vch = sb(st, "l_vch", [P, NCHC], F32)
                wg = sb(st, "l_wg", [P, 2, NLT, 128], BF16)
                xprev = sb(st, "l_xprev", [P, NLT, 4], F32); hprev = sb(st, "l_hprev", [P, NLT], F32)
                xbuf = sb(st, "l_xbuf", [P, 4, 516], F32); XB = Ring("l_xbuf", xbuf, 4)
                xc = sb(st, "l_xc", [P, 2, 512], F32); XC = Ring("l_xc", xc, 2)
                xcb = sb(st, "l_xcb", [P, 2, 512], BF16); XCB = Ring("l_xcb", xcb, 2)
                rr = sb(st, "l_r", [P, 2, 512], F32); RR = Ring("l_r", rr, 2)
                ii = sb(st, "l_i", [P, 2, 512], F32); II = Ring("l_i", ii, 2)
                aa = sb(st, "l_a", [P, 2, 512], F32); AA = Ring("l_a", aa, 2)
                mm = sb(st, "l_m", [P, 2, 512], F32); MM = Ring("l_m", mm, 2)
                hh_ = sb(st, "l_h", [P, 2, 512], F32); HH = Ring("l_h", hh_, 2)
                C.dma('sp', cw, conv_w.rearrange("k (t p) -> p k t", p=P), writes=['l_par'], allow_slow_non_contiguous=True)
                C.dma('sp', cbias, conv_b.rearrange("(t p) -> p t", p=P), writes=['l_par'], allow_slow_non_contiguous=True)
                C.dma('sp', bg, lru_bg.rearrange("k (t p) -> p k t", p=P), writes=['l_par'], allow_slow_non_contiguous=True)
                C.dma('sp', lam, lru_lam.rearrange("(t p) -> p t", p=P), writes=['l_par'], allow_slow_non_contiguous=True)
                C.dma('sp', vch, t_vchunk[:, :], writes=['l_par'])
                C.dma('pool', wg, lru_wg.rearrange("k t c d -> c k t d"), writes=['l_wg'])
                C.op('act', lambda e: e.activation(out=sc, in_=lam, func=AF.Exp, scale=-1.0), reads=['l_par'], writes=['l_sc'])
                C.op('act', lambda e: e.activation(out=sc, in_=sc, func=AF.Ln, bias=1.0), reads=['l_sc'], writes=['l_sc'])
                C.op('dve', lambda e: e.tensor_scalar(out=sc, in0=sc, scalar1=-8.0, scalar2=None, op0=ALU.mult), reads=['l_sc'], writes=['l_sc'])
                C.op('dve', lambda e: e.memset(xprev, 0.0), writes=['l_xprev'])
                C.op('dve', lambda e: e.memset(hprev, 0.0), writes=['l_hprev'])
                BW = min(512, LW)

                def epi(bi, ci, t0, pts):
                    xbs = []
                    for j, (pt, pk) in enumerate(pts):
                        xb_, xbk = XB.next()
                        C.op('act', lambda e: e.copy(out=xb_[:, 3:515], in_=pt[:, 0:512]), reads=[pk], writes=[xbk])
                        xbs.append((xb_, xbk))
                    for j, (pt, pk) in enumerate(pts):
                        ct = bi * (BW // P) + j
                        xb_, xbk = xbs[j]
                        C.op('dve', lambda e: e.tensor_copy(out=xb_[:, 0:3], in_=xprev[:, ct, 0:3]), reads=['l_xprev', xbk], writes=[xbk])
                        xc_, xck = XC.next()
                        C.op('dve', lambda e: e.tensor_scalar(out=xc_, in0=xb_[:, 3:515], scalar1=cw[:, 3, ct:ct + 1],
                                                              scalar2=cbias[:, ct:ct + 1], op0=ALU.mult, op1=ALU.add),
                             reads=[xbk, 'l_par'], writes=[xck])
                        for k in range(3):
                            C.op('dve', lambda e: e.scalar_tensor_tensor(out=xc_, in0=xb_[:, k:k + 512], scalar=cw[:, k, ct:ct + 1],
                                                                         in1=xc_, op0=ALU.mult, op1=ALU.add),
                                 reads=[xbk, xck, 'l_par'], writes=[xck])
                        C.op('dve', lambda e: e.tensor_copy(out=xprev[:, ct, 0:3], in_=xb_[:, 512:515]), reads=[xbk], writes=['l_xprev'])
                        xcb_, xcbk = XCB.next()
                        C.op('act', lambda e: e.copy(out=xcb_, in_=xc_), reads=[xck], writes=[xcbk])
                        pr, prk = PSF.next()
                        C.op('pe', lambda e: e.matmul(pr[:, 0:512], lhsT=wg[:, 0, ct, :], rhs=xcb_, start=True, stop=True),
                             reads=['l_wg', xcbk], writes=[prk])
                        pi, pik = PSF.next()
                        C.op('pe', lambda e: e.matmul(pi[:, 0:512], lhsT=wg[:, 1, ct, :], rhs=xcb_, start=True, stop=True),
                             reads=['l_wg', xcbk], writes=[pik])
                        r_, rk = RR.next(); i_, ik = II.next(); a_, ak = AA.next(); m_, mk = MM.next(); h_, hk = HH.next()
                        C.op('act', lambda e: e.activation(out=r_, in_=pr[:, 0:512], func=AF.Sigmoid, bias=bg[:, 0, ct:ct + 1]),
                             reads=[prk, 'l_par'], writes=[rk])
                        C.op('act', lambda e: e.activation(out=i_, in_=pi[:, 0:512], func=AF.Sigmoid, bias=bg[:, 1, ct:ct + 1]),
                             reads=[pik, 'l_par'], writes=[ik])
                        C.op('act', lambda e: e.activation(out=a_, in_=r_, func=AF.Exp, scale=sc[:, ct:ct + 1]),
                             reads=[rk, 'l_sc'], writes=[ak])
                        C.op('dve', lambda e: e.tensor_tensor(out=m_, in0=a_, in1=a_, op=ALU.mult), reads=[ak], writes=[mk])
                        C.op('act', lambda e: e.activation(out=m_, in_=m_, func=AF.Sqrt, scale=-1.0, bias=1.0), reads=[mk], writes=[mk])
                        C.op('dve', lambda e: e.tensor_tensor(out=m_, in0=m_, in1=i_, op=ALU.mult), reads=[mk, ik], writes=[mk])
                        C.op('dve', lambda e: e.scalar_tensor_tensor(out=m_, in0=m_, scalar=vch[:, ci:ci + 1], in1=xc_,
                                                                     op0=ALU.mult, op1=ALU.mult), reads=[mk, xck, 'l_par'], writes=[mk])
                        C.op('dve', lambda e: e.tensor_tensor_scan(out=h_, data0=a_, data1=m_, initial=hprev[:, ct:ct + 1],
                                                                   op0=ALU.mult, op1=ALU.add), reads=[ak, mk, 'l_hprev'], writes=[hk])
                        C.op('dve', lambda e: e.tensor_copy(out=hprev[:, ct:ct + 1], in_=h_[:, 511:512]), reads=[hk], writes=['l_hprev'])
                        if t0 >= OWN0:
                            C.dma('sp', hlru_d[ct * P:(ct + 1) * P, t0 - OWN0:t0 - OWN0 + 512], h_, reads=[hk])
                gemm("g1c", 'b', [[(hT_d, KT, [(w_in, C_LX + BW * b_, BW)])] for b_ in range(LW // BW)], 0, TCTX, 512, epi)

        if stages >= 4:
            with ExitStack() as st:
                T = nr_temps(st, "g2a")
                gq = sb(st, "g2a_gq", [P, 128], F32)
                xf = sb(st, "g2a_xf", [P, 2, 512], F32); XF = Ring("g2a_xf", xf, 2)
                kb = sb(st, "g2a_kb", [P, 2, 512], BF16); KB = Ring("g2a_kb", kb, 2)
                cst = sb(st, "g2a_cs", [P, TOWN // P, 32], F32)
                C.dma('sp', cst, cs_tok[OWN0:TCTX, :].rearrange("(t p) c -> p t c", p=P), writes=['g2a_cs'])
                stg = sb(st, "g2a_stg", [P, 2, 4, 512], BF16); STG = Ring("g2a_stg", stg, 2)
                C.dma('sp', gq, q_norm_g.partition_broadcast(P), writes=['g2a_gq'])

                def epi(bi, ci, t0, pts):
                    if bi == 4:
                        for j, (pt, pk) in enumerate(pts):
                            tile_i = (t0 - OWN0) // P + j
                            C.op('act', lambda e: e.activation(out=gate_sb[:, tile_i, :], in_=pt[:, 0:48], func=AF.Sigmoid),
                                 reads=[pk], writes=['gate_sb'])
                        return
                    sg, sgk = STG.next()
                    for j, (pt, pk) in enumerate(pts):
                        x_, xk = XF.next()
                        C.op('act', lambda e: e.copy(out=x_, in_=pt[:, 0:512]), reads=[pk], writes=[xk])
                        c_, ck = cst[:, (t0 - OWN0) // P + j, :], 'g2a_cs'
                        k_, kk = KB.next()
                        normrope(T, x_, xk, 4, gq, 'g2a_gq', c_, ck, k_, kk)
                        p2, p2k = PSB.next()
                        for g in range(4):
                            C.op('pe', lambda e: e.transpose(out=p2[:, g * P:(g + 1) * P], in_=k_[:, g * P:(g + 1) * P],
                                                             identity=ident[:]), reads=[kk, 'ident'], writes=[p2k], inc=(g == 3))
                        evac_copy(j, sg[:, :, j * P:(j + 1) * P], p2[:, 0:512].rearrange("p (g t) -> p g t", g=4), [p2k], [sgk])
                    dst = qT_d[4 * bi:4 * bi + 4, :, t0 - OWN0:t0 - OWN0 + 512].rearrange("g p t -> p g t")
                    C.dma('sp', dst, sg, reads=[sgk])
                blocks = [[(hT_d, KT, [(w_in, 512 * b_, 512)])] for b_ in range(4)] + [[(hT_d, KT, [(w_in, C_G, 48)])]]
                gemm("g2a", 'a', blocks, OWN0, TOWN, 512, epi)

        if stages >= 5:
            with ExitStack() as st:
                yx = sb(st, "g2b_y", [P, 2, 512], F32); YX = Ring("g2b_y", yx, 2)
                uu = sb(st, "g2b_u", [P, 2, 512], F32); UU = Ring("g2b_u", uu, 2)
                hl = sb(st, "g2b_h", [P, 8, 512], F32); HL = Ring("g2b_h", hl, 8)
                ob = sb(st, "g2b_o", [P, 3, 512], BF16); OB = Ring("g2b_o", ob, 3)
                BW = min(256, LW)
                nlb = LW // BW

                def pre(bi, ci, t0):
                    if bi >= nlb:
                        return None
                    to = t0 - OWN0
                    res = []
                    for j in range(BW // P):
                        ct = bi * (BW // P) + j
                        h_, hk = HL.next()
                        C.dma('act', h_, hlru_d[ct * P:(ct + 1) * P, to:to + 512], writes=[hk])
                        res.append((h_, hk))
                    return res

                def epi(bi, ci, t0, pts, pr):
                    to = t0 - OWN0
                    for j, (pt, pk) in enumerate(pts):
                        o_, ok = OB.next()
                        if bi < nlb:
                            ct = bi * (BW // P) + j
                            y_, yk = YX.next(); u_, uk = UU.next(); h_, hk = pr[j]
                            C.op('act', lambda e: e.copy(out=y_, in_=pt[:, 0:512]), reads=[pk], writes=[yk])
                            C.op('dve', lambda e: e.tensor_tensor(out=u_, in0=y_, in1=y_, op=ALU.mult), reads=[yk], writes=[uk])
                            C.op('dve', lambda e: e.tensor_scalar(out=u_, in0=u_, scalar1=0.044715, scalar2=1.0, op0=ALU.mult, op1=ALU.add),
                                 reads=[uk], writes=[uk])
                            C.op('dve', lambda e: e.tensor_tensor(out=u_, in0=u_, in1=y_, op=ALU.mult), reads=[uk, yk], writes=[uk])
                            C.op('act', lambda e: e.activation(out=u_, in_=u_, func=AF.Sigmoid, scale=1.5957691216057308), reads=[uk], writes=[uk])
                            C.op('dve', lambda e: e.tensor_tensor(out=u_, in0=u_, in1=y_, op=ALU.mult), reads=[uk, yk], writes=[uk])
                            C.op('dve', lambda e: e.tensor_tensor(out=o_, in0=u_, in1=h_, op=ALU.mult), reads=[uk, hk], writes=[ok])
                            C.dma('sp', lruoT_d[ct * P:(ct + 1) * P, to:to + 512], o_, reads=[ok])
                        else:
                            row = (bi - nlb) * 256 + j * P
                            C.op('act', lambda e: e.activation(out=o_, in_=pt[:, 0:512], func=AF.Sigmoid), reads=[pk], writes=[ok])
                            C.dma('sp', gT_d[row:row + P, to:to + 512], o_, reads=[ok])
                blocks = [[(hT_d, KT, [(w_in, C_LY + BW * b_, BW)])] for b_ in range(nlb)]
                blocks += [[(hT_d, KT, [(w_in, C_MG + 256 * b_, 256)])] for b_ in range(2 * DM // 256)]
                gemm("g2b", 'b', blocks, OWN0, TOWN, 512, epi, resident=True, pre=pre)

        AW = 129 + NB
        if stages >= 6:
            kcT = sb(es, "kcT", [P, 4, NCP], BF16)
            vca = sb(es, "vca", [P, 4, NCT, AW], BF16)
            with ExitStack() as st:
                T = nr_temps(st, "cp")
                w1 = sb(st, "cp_w1", [P, 2, 32, 256], BF16); posT = sb(st, "cp_pos", [P, 2, 32], BF16)
                b1 = sb(st, "cp_b1", [P, 2, 2], F32); w2 = sb(st, "cp_w2", [P, 2, 2, 128], BF16)
                b2b = sb(st, "cp_b2", [P, 2, 128], F32); gk = sb(st, "cp_gk", [P, 128], F32)
                csc = sb(st, "cp_cs", [P, NCT, 32], F32)
                src = sb(st, "cp_src", [P, 2, TCTX], BF16); SRC = Ring("cp_src", src, 2)
                hx = sb(st, "cp_hx", [P, 2, 512], F32); HX = Ring("cp_hx", hx, 2)
                hu = sb(st, "cp_hu", [P, 2, 512], F32); HU = Ring("cp_hu", hu, 2)
                hid = sb(st, "cp_hid", [P, 2, 2, NCP], BF16); HID = Ring("cp_hid", hid, 2)
                hb = sb(st, "cp_hb", [P, 2, 2], F32)
                xf = sb(st, "cp_xf", [P, 2, 512], F32); XF = Ring("cp_xf", xf, 2)
                kb = sb(st, "cp_kb", [P, 2, 512], BF16); KB = Ring("cp_kb", kb, 2)
                for kv in range(2):
                    C.dma('pool', w1[:, kv], cmp_w1[kv].rearrange("(l d) h -> d l h", d=128), writes=['cp_w1'])
                    C.dma('pool', posT[:, kv], cmp_pos[kv].rearrange("l d -> d l"), writes=['cp_pos'], allow_slow_non_contiguous=True)
                    C.dma('sp', b1[:, kv], cmp_b1[kv].rearrange("(t p) -> p t", p=P), writes=['cp_b1'], allow_slow_non_contiguous=True)
                    C.dma('pool', w2[:, kv], cmp_w2[kv].rearrange("(t p) d -> p t d", p=P), writes=['cp_w2'])
                    C.dma('sp', b2b[:, kv], cmp_b2[kv].partition_broadcast(P), writes=['cp_b2'])
                C.dma('sp', gk, k_norm_g[0].partition_broadcast(P), writes=['cp_gk'])
                C.dma('sp', csc, cs_cmp.rearrange("(t p) c -> p t c", p=P), writes=['cp_cs'])
                for g in range(4):
                    C.dma('pool', vca[:, g, :, 129:129 + NB], t_overlap.rearrange("(t p) n -> p t n", p=P), writes=['vca'])
                C.op('dve', lambda e: e.memset(vca[:, :, :, 128:129], 1.0), reads=[], writes=['vca'])
                C.op('dve', lambda e: e.memset(hid, 0.0), writes=[('cp_hid', 0), ('cp_hid', 1)])
                for kv in range(2):
                    for ht in range(2):
                        pt, pk = PSF.next()
                        for l in range(32):
                            C.op('pe', lambda e: e.matmul(pt[:, 0:1], lhsT=w1[:, kv, l, ht * P:(ht + 1) * P], rhs=posT[:, kv, l:l + 1],
                                                          start=(l == 0), stop=(l == 31)),
                                 reads=['cp_w1', 'cp_pos'], writes=[pk], inc=(l == 31))
                        C.op('dve', lambda e: e.tensor_tensor(out=hb[:, kv, ht:ht + 1], in0=pt[:, 0:1], in1=b1[:, kv, ht:ht + 1], op=ALU.add),
                             reads=[pk, 'cp_b1'], writes=['cp_hb'])
                for g in range(4):
                    for kv in range(2):
                        s_, sk = SRC.next()
                        C.dma('sp', s_, (kcmpT_d if kv == 0 else vcmpT_d)[g], writes=[sk])
                        sv = s_.rearrange("p (c s) -> p c s", s=16)
                        hd, hdk = HID.next()
                        for ht in range(2):
                            pt, pk = PSF.next()
                            for l in range(32):
                                rhs = sv[:, 0:NC, l] if l < 16 else sv[:, 1:NC + 1, l - 16]
                                C.op('pe', lambda e: e.matmul(pt[:, 0:NC], lhsT=w1[:, kv, l, ht * P:(ht + 1) * P], rhs=rhs,
                                                              start=(l == 0), stop=(l == 31)),
                                     reads=['cp_w1', sk], writes=[pk], inc=(l == 31))
                            x_, xk = HX.next(); u_, uk = HU.next()
                            C.op('act', lambda e: e.activation(out=x_[:, 0:NC], in_=pt[:, 0:NC], func=AF.Identity, bias=hb[:, kv, ht:ht + 1]),
                                 reads=[pk, 'cp_hb'], writes=[xk])
                            C.op('dve', lambda e: e.tensor_tensor(out=u_[:, 0:NC], in0=x_[:, 0:NC], in1=x_[:, 0:NC], op=ALU.mult), reads=[xk], writes=[uk])
                            C.op('dve', lambda e: e.tensor_scalar(out=u_[:, 0:NC], in0=u_[:, 0:NC], scalar1=0.044715, scalar2=1.0,
                                                                  op0=ALU.mult, op1=ALU.add), reads=[uk], writes=[uk])
                            C.op('dve', lambda e: e.tensor_tensor(out=u_[:, 0:NC], in0=u_[:, 0:NC], in1=x_[:, 0:NC], op=ALU.mult), reads=[uk, xk], writes=[uk])
                            C.op('act', lambda e: e.activation(out=u_[:, 0:NC], in_=u_[:, 0:NC], func=AF.Sigmoid, scale=1.5957691216057308),
                                 reads=[uk], writes=[uk])
                            C.op('dve', lambda e: e.tensor_tensor(out=hd[:, ht, 0:NC], in0=u_[:, 0:NC], in1=x_[:, 0:NC], op=ALU.mult),
                                 reads=[uk, xk], writes=[hdk])
                        for ct in range(NCT):
                            pt, pk = PSF.next()
                            for ht in range(2):
                                C.op('pe', lambda e: e.matmul(pt[:, 0:128], lhsT=hd[:, ht, ct * P:(ct + 1) * P], rhs=w2[:, kv, ht, :],
                                                              start=(ht == 0), stop=(ht == 1)), reads=[hdk, 'cp_w2'], writes=[pk], inc=(ht == 1))
                            if kv == 1:
                                C.op('dve', lambda e: e.tensor_tensor(out=vca[:, g, ct, 0:128], in0=pt[:, 0:128], in1=b2b[:, 1, :], op=ALU.add),
                                     reads=[pk, 'cp_b2'], writes=['vca'])
                            else:
                                x_, xk = XF.next(); k_, kk = KB.next()
                                C.op('dve', lambda e: e.tensor_tensor(out=x_[:, 0:128], in0=pt[:, 0:128], in1=b2b[:, 0, :], op=ALU.add),
                                     reads=[pk, 'cp_b2'], writes=[xk])
                                normrope(T, x_, xk, 1, gk, 'cp_gk', csc[:, ct, :], 'cp_cs', k_, kk)
                                p2, p2k = PSB.next()
                                C.op('pe', lambda e: e.transpose(out=p2[:, 0:P], in_=k_[:, 0:P], identity=ident[:]), reads=[kk, 'ident'], writes=[p2k])
                                C.op('act', lambda e: e.copy(out=kcT[:, g, ct * P:(ct + 1) * P], in_=p2[:, 0:P]), reads=[p2k], writes=['kcT'])
            C.barrier()

        if stages >= 7:
            with ExitStack() as st:
                Eb = sb(st, "at_E", [P, TCTX], BF16)
                cmask = sb(st, "at_cm", [P, NCHO, NCT, 512], BF16)
                sdiag = sb(st, "at_sd", [P, 4, 512], BF16); wmask = sb(st, "at_wm", [P, 2, 8, 512], BF16)
                kS = sb(st, "at_kS", [P, TCTX], BF16); kW = sb(st, "at_kW", [P, TCTX], BF16)
                NKT = TCTX // P
                vS = sb(st, "at_vS", [P, NKT, 129], BF16); vW = sb(st, "at_vW", [P, NKT, 129], BF16)
                qt = sb(st, "at_q", [P, 2, 4, 512], BF16); QT = Ring("at_q", qt, 2)
                pT = sb(st, "at_p", [P, 3, 512], BF16); PT = Ring("at_p", pT, 3)
                nsT = sb(st, "at_ns", [P, 512], BF16)
                tA = sb(st, "at_tA", [P, 2, NB], F32); TA = Ring("at_tA", tA, 2)
                tB = sb(st, "at_tB", [P, 2, NB], F32); TB = Ring("at_tB", tB, 2)
                imp = sb(st, "at_imp", [P, 4, NB], F32)
                wv = sb(st, "at_wv", [P, 2, NB], F32); WV = Ring("at_wv", wv, 2)
                wr = sb(st, "at_wr", [P, 2, NB], F32); WR = Ring("at_wr", wr, 2)
                m8 = sb(st, "at_m8", [P, 2, 16], F32); M8 = Ring("at_m8", m8, 2)
                selb = sb(st, "at_sel", [P, 2, NB], BF16); SEL = Ring("at_sel", selb, 2)
                rs = sb(st, "at_rs", [P, 4, 2], F32); RS = Ring("at_rs", rs, 4)
                oacc = sb(st, "at_o", [P, 4, 512], F32)
                obf = sb(st, "at_ob", [P, 4, 512], BF16)
                oT = sb(st, "at_oT", [P, 2, 4, 512], BF16); OT = Ring("at_oT", oT, 2)
                PS_S = Ring("psF", psF, 2, base=0); PS_A = Ring("psF", psF, 4, base=2)
                C.dma('pool', Eb[0:NB, :], t_E[:, :], writes=['at_E'])
                C.dma('pool', cmask, t_cmpmask, writes=['at_cm'])
                C.dma('pool', sdiag, t_slcdiag, writes=['at_sd'])
                C.dma('pool', wmask, t_winmask, writes=['at_wm'])
                C.op('dve', lambda e: e.memset(vS[:, :, 128:129], 1.0), writes=['at_vS'])
                C.op('dve', lambda e: e.memset(vW[:, :, 128:129], 1.0), writes=['at_vW'])
                JQ0 = OWN0 // P

                def evac(accs, hh, gidx, first, with_imp):
                    for sub in range(4):
                        pa, pak = accs[sub]
                        r_, rk = RS.next()
                        tile_i = None
                        C.op('dve', lambda e: e.tensor_scalar(out=r_[:, 0:1], in0=pa[:, 128:129], scalar1=1e-30, scalar2=None, op0=ALU.max),
                             reads=[pak], writes=[rk])
                        C.op('dve', lambda e: e.reciprocal(out=r_[:, 0:1], in_=r_[:, 0:1]), reads=[rk], writes=[rk])
                        C.op('dve', lambda e: e.tensor_tensor(out=r_[:, 1:2], in0=r_[:, 0:1], in1=gate_sb[:, evac.tile0 + sub, gidx:gidx + 1], op=ALU.mult),
                             reads=[rk, 'gate_sb'], writes=[rk])
                        od = oacc[:, sub, hh * P:(hh + 1) * P]
                        ok = ('at_o', sub, hh)
                        if first:
                            C.op('dve', lambda e: e.tensor_scalar(out=od, in0=pa[:, 0:128], scalar1=r_[:, 1:2], scalar2=None, op0=ALU.mult),
                                 reads=[pak, rk], writes=[ok])
                        else:
                            C.op('dve', lambda e: e.scalar_tensor_tensor(out=od, in0=pa[:, 0:128], scalar=r_[:, 1:2], in1=od, op0=ALU.mult, op1=ALU.add),
                                 reads=[pak, rk, ok], writes=[ok])
                        if with_imp:
                            ik = ('at_imp', sub)
                            if hh == 0:
                                C.op('dve', lambda e: e.tensor_scalar(out=imp[:, sub, :], in0=pa[:, 129:129 + NB], scalar1=r_[:, 0:1], scalar2=None, op0=ALU.mult),
                                     reads=[pak, rk], writes=[ik])
                            else:
                                C.op('dve', lambda e: e.scalar_tensor_tensor(out=imp[:, sub, :], in0=pa[:, 129:129 + NB], scalar=r_[:, 0:1], in1=imp[:, sub, :],
                                                                             op0=ALU.mult, op1=ALU.add), reads=[pak, rk, ik], writes=[ik])

                for g in range(4):
                    C.dma('sp', kS, kslcT_d[g], writes=['at_kS'])
                    C.dma('sp', kW, kwinT_d[g], writes=['at_kW'])
                    C.dma('sp', vS[:, :, 0:128], vslc_d[:, g * P:(g + 1) * P].rearrange("(j p) d -> p j d", p=P), writes=['at_vS'])
                    C.dma('sp', vW[:, :, 0:128], vwin_d[:, g * P:(g + 1) * P].rearrange("(j p) d -> p j d", p=P), writes=['at_vW'])
                    for i in range(NCHO):
                        q_, qk = QT.next()
                        C.dma('sp', q_, qT_d[4 * g:4 * g + 4, :, i * 512:(i + 1) * 512].rearrange("h p t -> p h t"), writes=[qk])
                        evac.tile0 = i * 4
                        jd0 = JQ0 + 4 * i
                        jcs = [jc for jc in range(NCT) if 16 * P * jc + 31 <= OWN0 + 512 * i + 511]
                        for hh in range(4):
                            accs = [PS_A.next() for _ in range(4)]
                            for jc in jcs:
                                ps, psk = PS_S.next()
                                C.op('pe', lambda e: e.matmul(ps[:, 0:512], lhsT=kcT[:, g, jc * P:(jc + 1) * P], rhs=q_[:, hh, :], start=True, stop=True),
                                     reads=['kcT', qk], writes=[psk])
                                p_, pk = PT.next()
                                C.op('act', lambda e: e.activation(out=p_, in_=ps[:, 0:512], func=AF.Exp, scale=SCALE), reads=[psk], writes=[pk])
                                C.op('dve', lambda e: e.tensor_tensor(out=p_, in0=p_, in1=cmask[:, i, jc, :], op=ALU.mult), reads=[pk, 'at_cm'], writes=[pk])
                                for sub in range(4):
                                    pa, pak = accs[sub]
                                    C.op('pe', lambda e: e.matmul(pa[:, 0:AW], lhsT=p_[:, sub * P:(sub + 1) * P], rhs=vca[:, g, jc, :],
                                                                  start=(jc == jcs[0]), stop=(jc == jcs[-1])),
                                         reads=[pk, 'vca'], writes=[pak], inc=(jc == jcs[-1]))
                            evac(accs, hh, (4 * g + hh) * 3 + 0, True, True)
                        for sub in range(4):
                            a_, ak = TA.next(); b_, bk = TB.next()
                            r0 = i * 512 + sub * P
                            C.dma('sp', a_, t_topA[r0:r0 + P, :], writes=[ak])
                            C.dma('sp', b_, t_topB[r0:r0 + P, :], writes=[bk])
                            w_, wk_ = WV.next(); w2_, w2k = WR.next(); m_, mk = M8.next(); s_, sk = SEL.next()
                            C.op('dve', lambda e: e.tensor_tensor(out=w_, in0=imp[:, sub, :], in1=a_, op=ALU.mult), reads=[('at_imp', sub), ak], writes=[wk_])
                            C.op('dve', lambda e: e.tensor_tensor(out=w_, in0=w_, in1=b_, op=ALU.add), reads=[wk_, bk], writes=[wk_])
                            C.op('dve', lambda e: e.max(out=m_[:, 0:8], in_=w_), reads=[wk_], writes=[mk])
                            C.op('dve', lambda e: e.match_replace(out=w2_, in_to_replace=m_[:, 0:8], in_values=w_, imm_value=-1e30),
                                 reads=[wk_, mk], writes=[w2k])
                            C.op('dve', lambda e: e.max(out=m_[:, 8:16], in_=w2_), reads=[w2k], writes=[mk])
                            C.op('dve', lambda e: e.tensor_scalar(out=m_[:, 15:16], in0=m_[:, 15:16], scalar1=0.0, scalar2=None, op0=ALU.max),
                                 reads=[mk], writes=[mk])
                            C.op('dve', lambda e: e.tensor_scalar(out=w2_, in0=w_, scalar1=m_[:, 15:16], scalar2=None, op0=ALU.is_ge),
                                 reads=[wk_, mk], writes=[w2k])
                            C.op('dve', lambda e: e.tensor_scalar(out=s_, in0=w2_, scalar1=-NEGB, scalar2=NEGB, op0=ALU.mult, op1=ALU.add),
                                 reads=[w2k], writes=[sk])
                            p2, p2k = PSB.next()
                            C.op('pe', lambda e: e.transpose(out=p2[0:NB, 0:P], in_=s_, identity=ident[:]), reads=[sk, 'ident'], writes=[p2k])
                            C.op('act', lambda e: e.copy(out=nsT[0:NB, sub * P:(sub + 1) * P], in_=p2[0:NB, 0:P]), reads=[p2k], writes=['at_ns'])
                        for hh in range(4):
                            accs = [PS_A.next() for _ in range(4)]
                            for j in range(jd0 + 4):
                                ps, psk = PS_S.next()
                                C.op('pe', lambda e: e.matmul(ps[:, 0:512], lhsT=kS[:, j * P:(j + 1) * P], rhs=q_[:, hh, :], start=True, stop=False),
                                     reads=['at_kS', qk], writes=[psk], inc=False)
                                C.op('pe', lambda e: e.matmul(ps[:, 0:512], lhsT=Eb[0:NB, j * P:(j + 1) * P], rhs=nsT[0:NB, :], start=False, stop=True),
                                     reads=['at_E', 'at_ns'], writes=[psk])
                                p_, pk = PT.next()
                                C.op('act', lambda e: e.activation(out=p_, in_=ps[:, 0:512], func=AF.Exp, scale=SCALE), reads=[psk], writes=[pk])
                                if j >= jd0:
                                    C.op('dve', lambda e: e.tensor_tensor(out=p_, in0=p_, in1=sdiag[:, j - jd0, :], op=ALU.mult),
                                         reads=[pk, 'at_sd'], writes=[pk])
                                for sub in range(4):
                                    if j > jd0 + sub:
                                        continue
                                    pa, pak = accs[sub]
                                    C.op('pe', lambda e: e.matmul(pa[:, 0:129], lhsT=p_[:, sub * P:(sub + 1) * P], rhs=vS[:, j, :],
                                                                  start=(j == 0), stop=(j == jd0 + sub)),
                                         reads=[pk, 'at_vS'], writes=[pak], inc=(j == jd0 + sub))
                            evac(accs, hh, (4 * g + hh) * 3 + 1, False, False)
                        for hh in range(4):
                            accs = [PS_A.next() for _ in range(4)]
                            for k8 in range(8):
                                j = jd0 - 4 + k8
                                ps, psk = PS_S.next()
                                C.op('pe', lambda e: e.matmul(ps[:, 0:512], lhsT=kW[:, j * P:(j + 1) * P], rhs=q_[:, hh, :], start=True, stop=True),
                                     reads=['at_kW', qk], writes=[psk])
                                p_, pk = PT.next()
                                C.op('act', lambda e: e.activation(out=p_, in_=ps[:, 0:512], func=AF.Exp, scale=SCALE), reads=[psk], writes=[pk])
                                C.op('dve', lambda e: e.tensor_tensor(out=p_, in0=p_, in1=wmask[:, 0 if i == 0 else 1, k8, :], op=ALU.mult),
                                     reads=[pk, 'at_wm'], writes=[pk])
                                for sub in range(4):
                                    if not (sub <= k8 <= sub + 4):
                                        continue
                                    pa, pak = accs[sub]
                                    C.op('pe', lambda e: e.matmul(pa[:, 0:129], lhsT=p_[:, sub * P:(sub + 1) * P], rhs=vW[:, j, :],
                                                                  start=(k8 == sub), stop=(k8 == sub + 4)),
                                         reads=[pk, 'at_vW'], writes=[pak], inc=(k8 == sub + 4))
                            evac(accs, hh, (4 * g + hh) * 3 + 2, False, False)
                        o_, otk = OT.next()
                        for sub in range(4):
                            oks = [('at_o', sub, hh) for hh in range(4)]
                            C.op('act', lambda e: e.copy(out=obf[:, sub, :], in_=oacc[:, sub, :]), reads=oks, writes=[('at_ob', sub)])
                            p2, p2k = PSB.next()
                            for hh in range(4):
                                C.op('pe', lambda e: e.transpose(out=p2[:, hh * P:(hh + 1) * P], in_=obf[:, sub, hh * P:(hh + 1) * P], identity=ident[:]),
                                     reads=[('at_ob', sub), 'ident'], writes=[p2k], inc=(hh == 3))
                            evac_copy(sub, o_[:, :, sub * P:(sub + 1) * P], p2[:, 0:512].rearrange("p (h t) -> p h t", h=4), [p2k], [otk])
                        C.dma('sp', attnT_d[4 * g * P:(4 * g + 4) * P, i * 512:(i + 1) * 512].rearrange("(h p) t -> p h t", p=P), o_, reads=[otk])
            C.barrier()

        if stages >= 8:
            with ExitStack() as st:
                gg = sb(st, "g3_g", [P, 8, 512], BF16); GG = Ring("g3_g", gg, 8)
                t1 = sb(st, "g3_t1", [P, 2, 512], F32); T1 = Ring("g3_t1", t1, 2)
                t2 = sb(st, "g3_t2", [P, 2, 512], F32); T2 = Ring("g3_t2", t2, 2)
                ob = sb(st, "g3_o", [P, 2, 512], BF16); OB = Ring("g3_o", ob, 2)

                def pre(bi, ci, t0):
                    res = []
                    for ct in range(2):
                        row = bi * 256 + ct * P
                        ga, gak = GG.next(); gb_, gbk = GG.next()
                        C.dma('act', ga, gT_d[row:row + P, t0:t0 + 512], writes=[gak])
                        C.dma('act', gb_, gT_d[DM + row:DM + row + P, t0:t0 + 512], writes=[gbk])
                        res.append((ga, gak, gb_, gbk))
                    return res

                def epi(bi, ci, t0, pts, pr):
                    to = t0
                    for ct in range(2):
                        row = bi * 256 + ct * P
                        (pa, pak), (pb_, pbk) = pts[ct], pts[2 + ct]
                        ga, gak, gb_, gbk = pr[ct]
                        a_, ak = T1.next(); b_, bk = T2.next(); o_, ok = OB.next()
                        C.op('dve', lambda e: e.tensor_tensor(out=a_, in0=pa[:, 0:512], in1=ga, op=ALU.mult), reads=[pak, gak], writes=[ak])
                        C.op('dve', lambda e: e.tensor_tensor(out=b_, in0=pb_[:, 0:512], in1=gb_, op=ALU.mult), reads=[pbk, gbk], writes=[bk])
                        C.op('dve', lambda e: e.tensor_tensor(out=o_, in0=a_, in1=b_, op=ALU.add), reads=[ak, bk], writes=[ok])
                        C.dma('sp', mixT_d[row:row + P, to:to + 512], o_, reads=[ok])
                blocks = [[(attnT_d, 16, [(w_bra, 256 * b_, 256)]), (lruoT_d, NLT, [(w_brb, 256 * b_, 256)])] for b_ in range(DM // 256)]
                gemm("g3", 'b', blocks, 0, TOWN, 512, epi, resident=True, pre=pre)

        if stages >= 9:
            with ExitStack() as st:
                xr = sb(st, "g4_x", [P, 8, 256], F32); XR = Ring("g4_x", xr, 8)

                def pre(bi, ci, t0):
                    res = []
                    for j in range(4):
                        x_, xk = XR.next()
                        r0 = t0 + j * P
                        C.dma('act', x_, x_ctx[OWN0 + r0:OWN0 + r0 + P, bi * 256:(bi + 1) * 256], writes=[xk])
                        res.append((x_, xk))
                    return res

                def epi(bi, ci, t0, pts, pr):
                    for j, (pt, pk) in enumerate(pts):
                        x_, xk = pr[j]
                        r0 = t0 + j * P
                        C.op('dve', lambda e: e.tensor_tensor(out=x_, in0=pt[:, 0:256], in1=x_, op=ALU.add), reads=[pk, xk], writes=[xk])
                        C.dma('sp', x1_d[r0:r0 + P, bi * 256:(bi + 1) * 256], x_, reads=[xk])
                gemm("g4", 'a', [[(mixT_d, KT, [(w_out, 256 * b_, 256)])] for b_ in range(DM // 256)], 0, TOWN, 512, epi,
                     resident=True, pre=pre)
            phase_norm("n2", x1_d, TOWN, norm2_g, h2T_d)

        if stages >= 10:
            with ExitStack() as st:
                sg_ = sb(st, "g5_s", [P, 2, 512], F32); SG = Ring("g5_s", sg_, 2)
                ob = sb(st, "g5_o", [P, 3, 512], BF16); OB = Ring("g5_o", ob, 3)

                def epi(bi, ci, t0, pts):
                    row = bi * P
                    (pg, pgk), (pu, puk) = pts[0], pts[1]
                    s_, sk = SG.next(); o_, ok = OB.next()
                    C.op('act', lambda e: e.activation(out=s_, in_=pg[:, 0:512], func=AF.Silu), reads=[pgk], writes=[sk])
                    C.op('dve', lambda e: e.tensor_tensor(out=o_, in0=pu[:, 0:512], in1=s_, op=ALU.mult), reads=[puk, sk], writes=[ok])
                    C.dma('sp', actT_d[row:row + P, t0:t0 + 512], o_, reads=[ok])
                blocks = [[(h2T_d, KT, [(w_ffi, P * b_, P), (w_ffi, FH + P * b_, P)])] for b_ in range(FH // P)]
                gemm("g5", 'b', blocks, 0, TOWN, 512, epi, resident=True)
            with ExitStack() as st:
                xr = sb(st, "g6_x", [P, 4, 512], F32); XR = Ring("g6_x", xr, 4)

                def pre(bi, ci, t0):
                    res = []
                    for j in range(2):
                        x_, xk = XR.next()
                        r0 = t0 + j * P
                        C.dma('act', x_, x1_d[r0:r0 + P, bi * 512:(bi + 1) * 512], writes=[xk])
                        res.append((x_, xk))
                    return res

                def epi(bi, ci, t0, pts, pr):
                    for j, (pt, pk) in enumerate(pts):
                        x_, xk = pr[j]
                        r0 = t0 + j * P
                        C.op('dve', lambda e: e.tensor_tensor(out=x_, in0=pt[:, 0:512], in1=x_, op=ALU.add), reads=[pk, xk], writes=[xk])
                        C.dma('sp', out[r0:r0 + P, bi * 512:(bi + 1) * 512], x_, reads=[xk])
                gemm("g6", 'a', [[(actT_d, FH // P, [(w_ffo, 512 * b_, 512)])] for b_ in range(DM // 512)], 0, TOWN, 256, epi,
                     wslots=1, aslots=2, pre=pre)
        else:
            with ExitStack() as st:
                z = sb(st, "zz", [P, DM], F32)
                C.op('dve', lambda e: e.memset(z[:], 0.0), writes=['zz'])
                for i in range(TOWN // P):
                    C.dma('sp', out[i * P:(i + 1) * P, :], z[:], reads=['zz'])
        C.barrier()
    return nc


def make_tables(cfg, r):
    TCTX, TOWN = cfg['TCTX'], cfg['TOWN']
    NB = TCTX // 64; NCP = TCTX // 16; NC = NCP - 1; NCT = NCP // P
    NCHC = TCTX // 512; NCHO = TOWN // 512
    OWN0 = TCTX - TOWN
    pad = TCTX - TOWN * (r + 1)
    inv = 1.0 / (500000.0 ** (np.arange(16, dtype=np.float32) * 2.0 / 32.0))
    pos = (np.arange(TCTX) - pad).astype(np.float32)
    ang = pos[:, None] * inv[None, :].astype(np.float32)
    cs_tok = np.concatenate([np.cos(ang), np.sin(ang)], axis=1).astype(np.float32)
    cend = (np.arange(NCP) * 16 + 31 - pad).astype(np.float32)
    ang = cend[:, None] * inv[None, :].astype(np.float32)
    cs_cmp = np.concatenate([np.cos(ang), np.sin(ang)], axis=1).astype(np.float32)
    c_l = np.arange(P)[:, None, None, None]; i_ = np.arange(NCHO)[None, :, None, None]
    jc = np.arange(NCT)[None, None, :, None]; ql = np.arange(512)[None, None, None, :]
    c = jc * P + c_l; t = OWN0 + 512 * i_ + ql
    cmpmask = ((c < NC) & (16 * c >= pad) & (16 * c + 31 <= t)).astype(np.float32)
    cc = np.arange(NCP)[:, None]; nn = np.arange(NB)[None, :]
    overlap = ((16 * cc <= 64 * nn + 63) & (16 * cc + 31 >= 64 * nn) & (cc < NC)).astype(np.float32)
    E = (np.arange(TCTX)[None, :] // 64 == np.arange(NB)[:, None]).astype(np.float32)
    kl = np.arange(P)[:, None, None]; jj = np.arange(4)[None, :, None]; q2 = np.arange(512)[None, None, :]
    slcdiag = ((128 * jj + kl) <= q2).astype(np.float32)
    kl4 = np.arange(P)[:, None, None, None]; var = np.arange(2)[None, :, None, None]
    kt8 = np.arange(8)[None, None, :, None]; q4 = np.arange(512)[None, None, None, :]
    krel = -512 + 128 * kt8 + kl4
    wm = (krel <= q4) & (krel > q4 - 512)
    wm = np.broadcast_to(wm, (P, 2, 8, 512)).copy()
    wm[:, 0] &= np.broadcast_to((OWN0 + krel[:, 0] >= pad), (P, 8, 512))
    winmask = wm.astype(np.float32)
    tt = (OWN0 + np.arange(TOWN))[:, None]
    valid = (nn * 64 >= pad) & (nn * 64 <= tt)
    cur = tt // 64
    forced = ((nn == pad // 64) | (nn == cur) | (nn == cur - 1)) & valid
    topA = (valid & ~forced).astype(np.float32)
    fval = np.where(nn == cur, 3e30, np.where(nn == cur - 1, 2e30, 1e30))
    topB = np.where(forced, fval, np.where(valid, 0.0, -1.0)).astype(np.float32)
    vchunk = np.broadcast_to(((np.arange(NCHC) * 512) >= pad).astype(np.float32)[None, :], (P, NCHC)).copy()
    return dict(cs_tok=cs_tok, cs_cmp=cs_cmp, t_cmpmask=cmpmask, t_overlap=overlap, t_E=E, t_slcdiag=slcdiag,
                t_winmask=winmask, t_topA=topA, t_topB=topB, t_vchunk=vchunk)


def kernel(debug_outs=(), stages=99, **inputs):
    x = np.asarray(inputs["x"])
    B, S, DM = x.shape
    LW = DM // 2
    FH = np.asarray(inputs["w_ffn_out"]).shape[1]
    cfg = dict(DM=DM, TCTX=S, TOWN=S // 4, LW=LW, FH=FH, stages=stages)
    TOWN = cfg['TOWN']
    nc = build_program(cfg, debug_outs)
    names1 = ["norm1_g", "norm2_g", "q_norm_g", "conv_b", "lru_lambda"]
    shared = {}
    for k, v in inputs.items():
        if k == "x":
            continue
        a = np.asarray(v, dtype=np.float32)[0]
        shared[k] = np.ascontiguousarray(a)
    n_cores = B * 4
    in_maps = []
    for c in range(n_cores):
        b, r = c // 4, c % 4
        pad = S - TOWN * (r + 1)
        xc = np.zeros((S, DM), np.float32)
        xc[pad:] = x[b, :TOWN * (r + 1)]
        m = dict(shared)
        m["x_ctx"] = xc
        m.update(make_tables(cfg, r))
        in_maps.append(m)
    res = run_bass_kernel_spmd(nc, in_maps, core_ids=list(range(n_cores)))
    outp = np.zeros((B, S, DM), np.float32)
    for c in range(n_cores):
        b, r = c // 4, c % 4
        outp[b, r * TOWN:(r + 1) * TOWN] = res.results[c]["out"]
    if debug_outs:
        return outp, res.results
    return outp
```

```python
import math
from contextlib import ExitStack
import numpy as np
import concourse.bass as bass
import concourse.mybir as mybir
from concourse.bass_utils import run_bass_kernel_spmd

F32, BF16 = mybir.dt.float32, mybir.dt.bfloat16
AF = mybir.ActivationFunctionType
ALU = mybir.AluOpType
AX = mybir.AxisListType
NDMA = 16
P = 128
NEGB = -30000.0
EPS = 1e-6


class Ctx:
    def __init__(self, nc, es):
        self.nc = nc
        self.E = {'pe': nc.tensor, 'act': nc.scalar, 'dve': nc.vector, 'pool': nc.gpsimd, 'sp': nc.sync}
        self.sem = {k: es.enter_context(nc.semaphore('s_' + k)) for k in ('pe', 'act', 'dve', 'pool')}
        self.dsem = [es.enter_context(nc.semaphore('s_d%d' % i)) for i in range(NDMA)]
        self.cnt = {k: 0 for k in self.sem}
        self.pend = {k: False for k in self.sem}
        self.duse = [0] * NDMA
        self.di = 0
        self.waited = {k: {} for k in self.E}
        self.tiles = {}
        self.nwait = 0

    def _semh(self, key):
        return self.sem[key] if isinstance(key, str) else self.dsem[key[1]]

    def _wait(self, X, evs):
        need = {}
        for (k, v) in evs:
            if v > need.get(k, 0):
                need[k] = v
        for k, v in need.items():
            if X == 'pe' and k == 'pe':
                continue
            if self.waited[X].get(k, 0) >= v:
                continue
            self.E[X].wait_ge(self._semh(k), v)
            self.waited[X][k] = v
            self.nwait += 1

    def _deps(self, reads, writes):
        evs = []
        for k in reads:
            t = self.tiles.get(k)
            if t and t[0]:
                evs.append(t[0])
        for k in writes:
            t = self.tiles.get(k)
            if t:
                if t[0]:
                    evs.append(t[0])
                evs.extend(t[1].items())
        return evs

    def _record(self, ev, reads, writes):
        for k in reads:
            t = self.tiles.setdefault(k, [None, {}])
            if ev[1] > t[1].get(ev[0], 0):
                t[1][ev[0]] = ev[1]
        for k in writes:
            self.tiles[k] = [ev, {}]

    def op(self, X, emit, reads=(), writes=(), inc=True):
        self._wait(X, self._deps(reads, writes))
        ins = emit(self.E[X])
        if inc:
            self.cnt[X] += 1
            ins.then_inc(self.sem[X], 1)
            ev = (X, self.cnt[X])
            self.pend[X] = False
        else:
            ev = (X, self.cnt[X] + 1)
            self.pend[X] = True
        self._record(ev, reads, writes)
        return ins

    def dma(self, Q, out, in_, reads=(), writes=(), **kw):
        i = self.di % NDMA
        self.di += 1
        n = self.duse[i]
        evs = self._deps(reads, writes)
        if n > 0:
            evs.append((('d', i), 16 * n))
        self._wait(Q, evs)
        self.E[Q].dma_start(out=out, in_=in_, **kw).then_inc(self.dsem[i], 16)
        self.duse[i] = n + 1
        self._record((('d', i), 16 * (n + 1)), reads, writes)

    def barrier(self):
        for k in self.pend:
            assert not self.pend[k], k
        evs = [(k, self.cnt[k]) for k in self.cnt if self.cnt[k] > 0]
        evs += [(('d', i), 16 * self.duse[i]) for i in range(NDMA) if self.duse[i] > 0]
        for X in self.E:
            self._wait(X, evs)
        self.tiles = {}


class Ring:
    def __init__(self, name, t, n, base=0):
        self.name, self.t, self.n, self.i, self.base = name, t, n, 0, base

    def next(self):
        s = self.base + self.i % self.n
        self.i += 1
        return self.t[:, s], (self.name, s)


def bcast_rows(ap1d, n):
    return ap1d.partition_broadcast(P)


def build_program(cfg, debug_outs=()):
    DM, TCTX, TOWN, LW, FH = cfg['DM'], cfg['TCTX'], cfg['TOWN'], cfg['LW'], cfg['FH']
    KT = DM // P
    NB = TCTX // 64
    NCP = TCTX // 16
    NC = NCP - 1
    NCT = NCP // P
    NCHC = TCTX // 512
    NCHO = TOWN // 512
    OWN0 = TCTX - TOWN
    NLT = LW // P
    INW = 2048 + 3072 + 48 + 2 * LW + 2 * DM
    C_Q, C_KV, C_G = 0, 2048, 5120
    C_LX = 5168
    C_LY = C_LX + LW
    C_MG = C_LY + LW

    nc = bass.Bass("TRN2", target_bir_lowering=False)

    def din(name, shape, dt=F32):
        return nc.dram_tensor(name, list(shape), dt, kind="ExternalInput").ap()

    def dscr(name, shape, dt):
        kind = "ExternalOutput" if name in debug_outs else "Internal"
        return nc.dram_tensor(name, list(shape), dt, kind=kind).ap()

    x_ctx = din("x_ctx", [TCTX, DM])
    norm1_g = din("norm1_g", [DM]); norm2_g = din("norm2_g", [DM])
    w_in = din("w_in", [DM, INW])
    q_norm_g = din("q_norm_g", [128]); k_norm_g = din("k_norm_g", [3, 128])
    cmp_pos = din("cmp_pos", [2, 32, 128]); cmp_w1 = din("cmp_w1", [2, 4096, 256])
    cmp_b1 = din("cmp_b1", [2, 256]); cmp_w2 = din("cmp_w2", [2, 256, 128]); cmp_b2 = din("cmp_b2", [2, 128])
    conv_w = din("conv_w", [4, LW]); conv_b = din("conv_b", [LW])
    lru_wg = din("lru_w_gates", [2, NLT, 128, 128]); lru_bg = din("lru_b_gates", [2, LW])
    lru_lam = din("lru_lambda", [LW])
    w_bra = din("w_branch_a", [2048, DM]); w_brb = din("w_branch_b", [LW, DM])
    w_out = din("w_out", [DM, DM])
    w_ffi = din("w_ffn_in", [DM, 2 * FH]); w_ffo = din("w_ffn_out", [FH, DM])
    cs_tok = din("cs_tok", [TCTX, 32]); cs_cmp = din("cs_cmp", [NCP, 32])
    t_cmpmask = din("t_cmpmask", [P, NCHO, NCT, 512])
    t_overlap = din("t_overlap", [NCP, NB])
    t_E = din("t_E", [NB, TCTX])
    t_slcdiag = din("t_slcdiag", [P, 4, 512])
    t_winmask = din("t_winmask", [P, 2, 8, 512])
    t_topA = din("t_topA", [TOWN, NB]); t_topB = din("t_topB", [TOWN, NB])
    t_vchunk = din("t_vchunk", [P, NCHC])
    out = nc.dram_tensor("out", [TOWN, DM], F32, kind="ExternalOutput").ap()

    hT_d = dscr("hT_d", [DM, TCTX], BF16)
    kcmpT_d = dscr("kcmpT_d", [4, P, TCTX], BF16); vcmpT_d = dscr("vcmpT_d", [4, P, TCTX], BF16)
    kslcT_d = dscr("kslcT_d", [4, P, TCTX], BF16); kwinT_d = dscr("kwinT_d", [4, P, TCTX], BF16)
    vslc_d = dscr("vslc_d", [TCTX, 512], BF16); vwin_d = dscr("vwin_d", [TCTX, 512], BF16)
    hlru_d = dscr("hlru_d", [LW, TOWN], F32)
    qT_d = dscr("qT_d", [16, P, TOWN], BF16)
    lruoT_d = dscr("lruoT_d", [LW, TOWN], BF16)
    gT_d = dscr("gT_d", [2 * DM, TOWN], BF16)
    attnT_d = dscr("attnT_d", [2048, TOWN], BF16)
    mixT_d = dscr("mixT_d", [DM, TOWN], BF16)
    x1_d = dscr("x1_d", [TOWN, DM], F32)
    h2T_d = dscr("h2T_d", [DM, TOWN], BF16)
    actT_d = dscr("actT_d", [FH, TOWN], BF16)

    es = ExitStack()
    with es:
        C = Ctx(nc, es)

        def sb(st, name, shape, dt):
            return st.enter_context(nc.sbuf_tensor(name, list(shape), dt))[:]

        ident = sb(es, "ident", [P, P], BF16)
        identf = sb(es, "identf", [P, P], F32)
        gate_sb = sb(es, "gate_sb", [P, TOWN // P, 48], F32)
        psF = es.enter_context(nc.psum_tensor("psF", [P, 6, 512], F32))[:]
        psB = es.enter_context(nc.psum_tensor("psB", [P, 2, 1024], BF16))[:]
        PSF = Ring("psF", psF, 6)
        PSB = Ring("psB", psB, 2)

        C.op('pool', lambda e: e.memset(identf[:], 1.0), writes=['identf'])
        C.op('pool', lambda e: e.affine_select(out=identf[:], in_=identf[:], pattern=[[-1, P]],
                                               compare_op=ALU.is_equal, fill=0.0, base=0, channel_multiplier=1),
             reads=['identf'], writes=['identf'])
        C.op('dve', lambda e: e.tensor_copy(out=ident[:], in_=identf[:]), reads=['identf'], writes=['ident'])

        def phase_norm(tag, src_d, ntok, g_d, dst_d):
            with ExitStack() as st:
                gb = sb(st, tag + "gb", [P, DM], F32)
                xr = sb(st, tag + "x", [P, 2, DM], F32)
                junk = sb(st, tag + "junk", [P, DM], BF16)
                xn = sb(st, tag + "xn", [P, 2, DM], BF16)
                ss = sb(st, tag + "ss", [P, 4, 2], F32)
                stg = sb(st, tag + "stg", [P, 2, KT, 512], BF16)
                XR = Ring(tag + "x", xr, 2); XN = Ring(tag + "xn", xn, 2); SS = Ring(tag + "ss", ss, 4)
                STG = Ring(tag + "stg", stg, 2)
                C.dma('sp', gb[:], g_d.partition_broadcast(P), writes=[tag + 'gb'])
                for ch in range(ntok // 512):
                    sg, sgk = STG.next()
                    for sub in range(4):
                        t0 = ch * 512 + sub * P
                        xt, xk = XR.next()
                        C.dma('sp', xt, src_d[t0:t0 + P, :], writes=[xk])
                        s_, sk = SS.next()
                        C.op('act', lambda e: e.activation(out=junk[:], in_=xt, func=AF.Square, accum_out=s_[:, 0:1]),
                             reads=[xk], writes=[tag + 'junk', sk])
                        C.op('dve', lambda e: e.tensor_scalar(out=s_[:, 1:2], in0=s_[:, 0:1], scalar1=1.0 / DM, scalar2=EPS,
                                                              op0=ALU.mult, op1=ALU.add), reads=[sk], writes=[sk])
                        C.op('act', lambda e: e.sqrt(out=s_[:, 1:2], in_=s_[:, 1:2]), reads=[sk], writes=[sk])
                        C.op('dve', lambda e: e.reciprocal(out=s_[:, 1:2], in_=s_[:, 1:2]), reads=[sk], writes=[sk])
                        xb, xbk = XN.next()
                        C.op('dve', lambda e: e.scalar_tensor_tensor(out=xb, in0=xt, scalar=s_[:, 1:2], in1=gb[:],
                                                                     op0=ALU.mult, op1=ALU.mult),
                             reads=[xk, sk, tag + 'gb'], writes=[xbk])
                        nb8 = min(8, KT)
                        for k8 in range(KT // nb8):
                            pt, pk = PSB.next()
                            for j in range(nb8):
                                kt = k8 * nb8 + j
                                C.op('pe', lambda e: e.transpose(out=pt[:, j * P:(j + 1) * P], in_=xb[:, kt * P:(kt + 1) * P],
                                                                 identity=ident[:]),
                                     reads=[xbk, 'ident'], writes=[pk], inc=(j == nb8 - 1))
                            dst = sg[:, k8 * nb8:(k8 + 1) * nb8, sub * P:(sub + 1) * P]
                            src = pt[:, 0:nb8 * P].rearrange("p (k t) -> p k t", k=nb8)
                            eng = 'act' if (k8 % 2 == 0) else 'dve'
                            if eng == 'act':
                                C.op('act', lambda e: e.copy(out=dst, in_=src), reads=[pk], writes=[sgk])
                            else:
                                C.op('dve', lambda e: e.tensor_copy(out=dst, in_=src), reads=[pk], writes=[sgk])
                    dv = dst_d.rearrange("(k p) t -> p k t", p=P)
                    half = KT // 2
                    C.dma('sp', dv[:, 0:half, ch * 512:(ch + 1) * 512], sg[:, 0:half, :], reads=[sgk])
                    C.dma('sp', dv[:, half:KT, ch * 512:(ch + 1) * 512], sg[:, half:KT, :], reads=[sgk])
            C.barrier()

        def gemm(tag, form, colblocks, tok0, ntok, TC, epilogue, wslots=2, aslots=2, resident=False, pre=None, lq='act'):
            ktot = max(sum(g[1] for g in blk) for blk in colblocks)
            wmax = max(sum(pc[2] for pc in g[2]) for blk in colblocks for g in blk)
            nch = ntok // TC
            if resident:
                aslots = nch
            with ExitStack() as st:
                wb = sb(st, tag + "w", [P, wslots, ktot, wmax], BF16)
                ab = sb(st, tag + "a", [P, aslots, ktot, TC], BF16)
                WB = Ring(tag + "w", wb, wslots); AB = Ring(tag + "a", ab, aslots)
                work = [(bi, ci) for bi in range(len(colblocks)) for ci in range(nch)]
                wstate = {}
                astate = {}

                def load_w(bi):
                    if bi >= len(colblocks) or bi in wstate:
                        return
                    blk = colblocks[bi]
                    wt, wk = WB.next()
                    wkeys = []
                    k0 = 0
                    for (act_d, ktn, pieces) in blk:
                        c0 = 0
                        for (w_d, col0, width) in pieces:
                            wv = w_d.rearrange("(k p) n -> p k n", p=P)
                            step = 8
                            for ks in range(0, ktn, step):
                                ke = min(ktn, ks + step)
                                wkeys.append((wk, k0 + ks, c0))
                                C.dma('pool', wt[:, k0 + ks:k0 + ke, c0:c0 + width], wv[:, ks:ke, col0:col0 + width],
                                      writes=[wkeys[-1]])
                            c0 += width
                        k0 += ktn
                    wstate[bi] = (wt, wkeys)

                def load_a(idx):
                    if idx >= len(work) or idx in astate:
                        return
                    bi, ci = work[idx]
                    blk = colblocks[bi]
                    t0 = tok0 + ci * TC
                    if resident and bi > 0:
                        at, akeys, _ = astate[ci]
                    else:
                        at, ak = AB.next()
                        akeys = []
                        k0 = 0
                        for (act_d, ktn, pieces) in blk:
                            av = act_d.rearrange("(k p) t -> p k t", p=P)
                            step = 16
                            for ks in range(0, ktn, step):
                                ke = min(ktn, ks + step)
                                akeys.append((ak, k0 + ks))
                                C.dma(lq, at[:, k0 + ks:k0 + ke, :], av[:, ks:ke, t0:t0 + TC], writes=[akeys[-1]])
                            k0 += ktn
                    pr = pre(bi, ci, t0) if pre is not None else None
                    astate[idx] = (at, akeys, pr)

                load_w(0)
                load_a(0)
                for idx, (bi, ci) in enumerate(work):
                    blk = colblocks[bi]
                    t0 = tok0 + ci * TC
                    if ci == 0:
                        load_w(bi)
                        if wslots >= 2:
                            load_w(bi + 1)
                    load_a(idx + 1)
                    wt, wkeys = wstate[bi]
                    at, akeys, pr = astate[idx]
                    pts = []
                    if form == 'b':
                        k0 = 0
                        for (act_d, ktn, pieces) in blk:
                            wsum = sum(pc[2] for pc in pieces)
                            for ct in range(wsum // P):
                                pt, pk = PSF.next()
                                for kt in range(ktn):
                                    C.op('pe', lambda e: e.matmul(pt[:, 0:TC], lhsT=wt[:, k0 + kt, ct * P:(ct + 1) * P],
                                                                  rhs=at[:, k0 + kt, :], start=(kt == 0), stop=(kt == ktn - 1)),
                                         reads=wkeys + akeys, writes=[pk], inc=(kt == ktn - 1))
                                pts.append((pt, pk))
                            k0 += ktn
                    else:
                        nw = sum(pc[2] for pc in blk[0][2])
                        for sub in range(TC // P):
                            pt, pk = PSF.next()
                            for kt in range(ktot):
                                C.op('pe', lambda e: e.matmul(pt[:, 0:nw], lhsT=at[:, kt, sub * P:(sub + 1) * P],
                                                              rhs=wt[:, kt, 0:nw], start=(kt == 0), stop=(kt == ktot - 1)),
                                     reads=wkeys + akeys, writes=[pk], inc=(kt == ktot - 1))
                            pts.append((pt, pk))
                    if pre is not None:
                        epilogue(bi, ci, t0, pts, pr)
                    else:
                        epilogue(bi, ci, t0, pts)
                    if not resident:
                        del astate[idx]
                    if ci == nch - 1:
                        del wstate[bi]
            C.barrier()

        SCALE = 128.0 ** -0.5
        stages = cfg.get('stages', 99)

        def evac_copy(i, dst, src, rk, wk):
            if i % 2 == 0:
                C.op('act', lambda e: e.copy(out=dst, in_=src), reads=rk, writes=wk)
            else:
                C.op('dve', lambda e: e.tensor_copy(out=dst, in_=src), reads=rk, writes=wk)

        def normrope(T, xf, xfk, nh, gb_ap, gbk, cs, csk, outb, outk):
            sq, sqk = T['sq'].next(); stt, stk = T['st'].next(); tr, trk = T['tr'].next()
            W = nh * 128
            x3 = xf[:, 0:W].rearrange("p (h d) -> p h d", h=nh)
            o3 = outb[:, 0:W].rearrange("p (h d) -> p h d", h=nh)
            C.op('act', lambda e: e.activation(out=sq[:, 0:W], in_=xf[:, 0:W], func=AF.Square), reads=[xfk], writes=[sqk])
            C.op('dve', lambda e: e.tensor_reduce(out=stt[:, 0:nh], in_=sq[:, 0:W].rearrange("p (h d) -> p h d", h=nh),
                                                  axis=AX.X, op=ALU.add), reads=[sqk], writes=[stk])
            C.op('dve', lambda e: e.tensor_scalar(out=stt[:, 0:nh], in0=stt[:, 0:nh], scalar1=1.0 / 128, scalar2=EPS,
                                                  op0=ALU.mult, op1=ALU.add), reads=[stk], writes=[stk])
            C.op('act', lambda e: e.sqrt(out=stt[:, 0:nh], in_=stt[:, 0:nh]), reads=[stk], writes=[stk])
            C.op('dve', lambda e: e.reciprocal(out=stt[:, 0:nh], in_=stt[:, 0:nh]), reads=[stk], writes=[stk])
            C.op('dve', lambda e: e.tensor_tensor(out=x3, in0=x3, in1=stt[:, 0:nh].unsqueeze(2).to_broadcast([P, nh, 128]),
                                                  op=ALU.mult), reads=[xfk, stk], writes=[xfk])
            C.op('dve', lambda e: e.tensor_tensor(out=x3, in0=x3, in1=gb_ap.unsqueeze(1).to_broadcast([P, nh, 128]),
                                                  op=ALU.mult), reads=[xfk, gbk], writes=[xfk])
            C.op('act', lambda e: e.copy(out=outb[:, 0:W], in_=xf[:, 0:W]), reads=[xfk], writes=[outk])
            x1 = x3[:, :, 0:16]; x2 = x3[:, :, 16:32]
            cb = cs[:, 0:16].unsqueeze(1).to_broadcast([P, nh, 16]); sbb = cs[:, 16:32].unsqueeze(1).to_broadcast([P, nh, 16])
            C.op('dve', lambda e: e.tensor_tensor(out=tr[:, 0, 0:nh, :], in0=x1, in1=cb, op=ALU.mult), reads=[xfk, csk], writes=[trk])
            C.op('dve', lambda e: e.tensor_tensor(out=tr[:, 1, 0:nh, :], in0=x2, in1=sbb, op=ALU.mult), reads=[xfk, csk], writes=[trk])
            C.op('dve', lambda e: e.tensor_tensor(out=tr[:, 2, 0:nh, :], in0=x2, in1=cb, op=ALU.mult), reads=[xfk, csk], writes=[trk])
            C.op('dve', lambda e: e.tensor_tensor(out=tr[:, 3, 0:nh, :], in0=x1, in1=sbb, op=ALU.mult), reads=[xfk, csk], writes=[trk])
            C.op('dve', lambda e: e.tensor_tensor(out=o3[:, :, 0:16], in0=tr[:, 0, 0:nh, :], in1=tr[:, 1, 0:nh, :], op=ALU.subtract),
                 reads=[trk], writes=[outk])
            C.op('dve', lambda e: e.tensor_tensor(out=o3[:, :, 16:32], in0=tr[:, 2, 0:nh, :], in1=tr[:, 3, 0:nh, :], op=ALU.add),
                 reads=[trk], writes=[outk])

        def nr_temps(st, tag):
            sq = sb(st, tag + "sq", [P, 2, 512], F32); stt = sb(st, tag + "st", [P, 2, 8], F32)
            tr = sb(st, tag + "tr", [P, 2, 4, 4, 16], F32)
            return dict(sq=Ring(tag + "sq", sq, 2), st=Ring(tag + "st", stt, 2), tr=Ring(tag + "tr", tr, 2))

        phase_norm("n1", x_ctx, TCTX, norm1_g, hT_d)

        if stages >= 1:
            with ExitStack() as st:
                stg = sb(st, "g1a_stg", [P, 2, 4, 512], BF16); STG = Ring("g1a_stg", stg, 2)

                def epi(bi, ci, t0, pts):
                    sg, sgk = STG.next()
                    for j, (pt, pk) in enumerate(pts):
                        evac_copy(j, sg[:, j, :], pt[:, 0:512], [pk], [sgk])
                    dst = (kcmpT_d if bi == 0 else vcmpT_d)[:, :, t0:t0 + 512].rearrange("g p t -> p g t")
                    C.dma('sp', dst, sg, reads=[sgk])
                gemm("g1a", 'b', [[(hT_d, KT, [(w_in, 2048, 512)])], [(hT_d, KT, [(w_in, 2560, 512)])]], 0, TCTX, 512, epi)

        if stages >= 2:
            with ExitStack() as st:
                T = nr_temps(st, "g1b")
                gkb = sb(st, "g1b_gk", [P, 2, 128], F32)
                xf = sb(st, "g1b_xf", [P, 2, 512], F32); XF = Ring("g1b_xf", xf, 2)
                kb = sb(st, "g1b_kb", [P, 2, 512], BF16); KB = Ring("g1b_kb", kb, 2)
                cst = sb(st, "g1b_cs", [P, TCTX // P, 32], F32)
                C.dma('sp', cst, cs_tok.rearrange("(t p) c -> p t c", p=P), writes=['g1b_cs'])
                stg = sb(st, "g1b_stg", [P, 2, 4, 512], BF16); STG = Ring("g1b_stg", stg, 2)
                C.dma('sp', gkb[:, 0, :], k_norm_g[1].partition_broadcast(P), writes=['g1b_gk'])
                C.dma('sp', gkb[:, 1, :], k_norm_g[2].partition_broadcast(P), writes=['g1b_gk'])

                def epi(bi, ci, t0, pts):
                    sg, sgk = STG.next()
                    if bi in (1, 3):
                        for j, (pt, pk) in enumerate(pts):
                            evac_copy(j, sg[:, j, :], pt[:, 0:512], [pk], [sgk])
                        dst = (vslc_d if bi == 1 else vwin_d)[t0:t0 + 512, :].rearrange("(s p) c -> p s c", p=P)
                        C.dma('sp', dst, sg, reads=[sgk])
                    else:
                        for j, (pt, pk) in enumerate(pts):
                            x_, xk = XF.next()
                            C.op('act', lambda e: e.copy(out=x_, in_=pt[:, 0:512]), reads=[pk], writes=[xk])
                            c_, ck = cst[:, t0 // P + j, :], 'g1b_cs'
                            k_, kk = KB.next()
                            normrope(T, x_, xk, 4, gkb[:, 0 if bi == 0 else 1, :], 'g1b_gk', c_, ck, k_, kk)
                            p2, p2k = PSB.next()
                            for g in range(4):
                                C.op('pe', lambda e: e.transpose(out=p2[:, g * P:(g + 1) * P], in_=k_[:, g * P:(g + 1) * P],
                                                                 identity=ident[:]), reads=[kk, 'ident'], writes=[p2k], inc=(g == 3))
                            evac_copy(j, sg[:, :, j * P:(j + 1) * P], p2[:, 0:512].rearrange("p (g t) -> p g t", g=4), [p2k], [sgk])
                        dst = (kslcT_d if bi == 0 else kwinT_d)[:, :, t0:t0 + 512].rearrange("g p t -> p g t")
                        C.dma('sp', dst, sg, reads=[sgk])
                gemm("g1b", 'a', [[(hT_d, KT, [(w_in, 3072 + 512 * b_, 512)])] for b_ in range(4)], 0, TCTX, 512, epi)

        if stages >= 3:
            with ExitStack() as st:
                cw = sb(st, "l_cw", [P, 4, NLT], F32); cbias = sb(st, "l_cb", [P, NLT], F32)
                bg = sb(st, "l_bg", [P, 2, NLT], F32); lam = sb(st, "l_lam", [P, NLT], F32)
                sc = sb(st, "l_sc", [P, NLT], F32); vch = sb(st, "l_vch", [P, NCHC], F32)
                wg = sb(st, "l_wg", [P, 2, NLT, 128], BF16)
                xprev = sb(st, "l_xprev", [P, NLT, 4], F32); hprev = sb(st, "l_hprev", [P, NLT], F32)
                xbuf = sb(st, "l_xbuf", [P, 4, 516], F32); XB = Ring("l_xbuf", xbuf, 4)
                xc = sb(st, "l_xc", [P, 4, 512], F32); XC = Ring("l_xc", xc, 4)
                xcb = sb(st, "l_xcb", [P, 4, 512], BF16); XCB = Ring("l_xcb", xcb, 4)
                rr = sb(st, "l_r", [P, 4, 512], F32); RR = Ring("l_r", rr, 4)
                ii = sb(st, "l_i", [P, 4, 512], F32); II = Ring("l_i", ii, 4)
                aa = sb(st, "l_a", [P, 4, 512], F32); AA = Ring("l_a", aa, 4)
                mm = sb(st, "l_m", [P, 4, 512], F32); MM = Ring("l_m", mm, 4)
                hh_ = sb(st, "l_h", [P, 4, 512], F32); HH = Ring("l_h", hh_, 4)
                C.dma('sp', cw, conv_w.rearrange("k (t p) -> p k t", p=P), writes=['l_par'], allow_slow_non_contiguous=True)
                C.dma('sp', cbias, conv_b.rearrange("(t p) -> p t", p=P), writes=['l_par'], allow_slow_non_contiguous=True)
                C.dma('sp', bg, lru_bg.rearrange("k (t p) -> p k t", p=P), writes=['l_par'], allow_slow_non_contiguous=True)
                C.dma('sp', lam, lru_lam.rearrange("(t p) -> p t", p=P), writes=['l_par'], allow_slow_non_contiguous=True)
                C.dma('sp', vch, t_vchunk[:, :], writes=['l_par'])
                C.dma('pool', wg, lru_wg.rearrange("k t c d -> c k t d"), writes=['l_wg'])
                C.op('act', lambda e: e.activation(out=sc, in_=lam, func=AF.Exp, scale=-1.0), reads=['l_par'], writes=['l_sc'])
                C.op('act', lambda e: e.activation(out=sc, in_=sc, func=AF.Ln, bias=1.0), reads=['l_sc'], writes=['l_sc'])
                C.op('dve', lambda e: e.tensor_scalar(out=sc, in0=sc, scalar1=-8.0, scalar2=None, op0=ALU.mult), reads=['l_sc'], writes=['l_sc'])
                C.op('dve', lambda e: e.memset(xprev, 0.0), writes=[('l_xprev', c_) for c_ in range(NLT)])
                C.op('dve', lambda e: e.memset(hprev, 0.0), writes=[('l_hprev', c_) for c_ in range(NLT)])
                BW = min(512, LW)

                def epi(bi, ci, t0, pts):
                    n = len(pts)
                    J = range(n)
                    cts = [bi * (BW // P) + j for j in J]
                    xbs = [XB.next() for j in J]; xcs = [XC.next() for j in J]; xcbs = [XCB.next() for j in J]
                    rs_ = [RR.next() for j in J]; is_ = [II.next() for j in J]; as_ = [AA.next() for j in J]
                    ms_ = [MM.next() for j in J]; hs_ = [HH.next() for j in J]
                    for j in J:
                        C.op('act', lambda e: e.copy(out=xbs[j][0][:, 3:515], in_=pts[j][0][:, 0:512]), reads=[pts[j][1]], writes=[xbs[j][1]])
                    for j in J:
                        C.op('dve', lambda e: e.tensor_copy(out=xbs[j][0][:, 0:3], in_=xprev[:, cts[j], 0:3]), reads=[('l_xprev', cts[j]), xbs[j][1]], writes=[xbs[j][1]])
                    for j in J:
                        C.op('dve', lambda e: e.tensor_scalar(out=xcs[j][0], in0=xbs[j][0][:, 3:515], scalar1=cw[:, 3, cts[j]:cts[j] + 1],
                                                              scalar2=cbias[:, cts[j]:cts[j] + 1], op0=ALU.mult, op1=ALU.add),
                             reads=[xbs[j][1], 'l_par'], writes=[xcs[j][1]])
                    for k in range(3):
                        for j in J:
                            C.op('dve', lambda e: e.scalar_tensor_tensor(out=xcs[j][0], in0=xbs[j][0][:, k:k + 512], scalar=cw[:, k, cts[j]:cts[j] + 1],
                                                                         in1=xcs[j][0], op0=ALU.mult, op1=ALU.add),
                                 reads=[xbs[j][1], xcs[j][1], 'l_par'], writes=[xcs[j][1]])
                    for j in J:
                        C.op('dve', lambda e: e.tensor_copy(out=xprev[:, cts[j], 0:3], in_=xbs[j][0][:, 512:515]), reads=[xbs[j][1]], writes=[('l_xprev', cts[j])])
                    for j in J:
                        C.op('act', lambda e: e.copy(out=xcbs[j][0], in_=xcs[j][0]), reads=[xcs[j][1]], writes=[xcbs[j][1]])
                    for j0 in range(0, n, 2):
                        JJ = range(j0, min(n, j0 + 2))
                        prs = {}; pis = {}
                        for j in JJ:
                            prs[j] = PSF.next(); pis[j] = PSF.next()
                            C.op('pe', lambda e: e.matmul(prs[j][0][:, 0:512], lhsT=wg[:, 0, cts[j], :], rhs=xcbs[j][0], start=True, stop=True),
                                 reads=['l_wg', xcbs[j][1]], writes=[prs[j][1]])
                            C.op('pe', lambda e: e.matmul(pis[j][0][:, 0:512], lhsT=wg[:, 1, cts[j], :], rhs=xcbs[j][0], start=True, stop=True),
                                 reads=['l_wg', xcbs[j][1]], writes=[pis[j][1]])
                        for j in JJ:
                            C.op('act', lambda e: e.activation(out=rs_[j][0], in_=prs[j][0][:, 0:512], func=AF.Sigmoid, bias=bg[:, 0, cts[j]:cts[j] + 1]),
                                 reads=[prs[j][1], 'l_par'], writes=[rs_[j][1]])
                        for j in JJ:
                            C.op('act', lambda e: e.activation(out=is_[j][0], in_=pis[j][0][:, 0:512], func=AF.Sigmoid, bias=bg[:, 1, cts[j]:cts[j] + 1]),
                                 reads=[pis[j][1], 'l_par'], writes=[is_[j][1]])
                    for j in J:
                        C.op('act', lambda e: e.activation(out=as_[j][0], in_=rs_[j][0], func=AF.Exp, scale=sc[:, cts[j]:cts[j] + 1]),
                             reads=[rs_[j][1], 'l_sc'], writes=[as_[j][1]])
                    for j in J:
                        C.op('dve', lambda e: e.tensor_tensor(out=ms_[j][0], in0=as_[j][0], in1=as_[j][0], op=ALU.mult), reads=[as_[j][1]], writes=[ms_[j][1]])
                    for j in J:
                        C.op('act', lambda e: e.activation(out=ms_[j][0], in_=ms_[j][0], func=AF.Sqrt, scale=-1.0, bias=1.0), reads=[ms_[j][1]], writes=[ms_[j][1]])
                    for j in J:
                        C.op('dve', lambda e: e.tensor_tensor(out=ms_[j][0], in0=ms_[j][0], in1=is_[j][0], op=ALU.mult), reads=[ms_[j][1], is_[j][1]], writes=[ms_[j][1]])
                    for j in J:
                        C.op('dve', lambda e: e.scalar_tensor_tensor(out=ms_[j][0], in0=ms_[j][0], scalar=vch[:, ci:ci + 1], in1=xcs[j][0],
                                                                     op0=ALU.mult, op1=ALU.mult), reads=[ms_[j][1], xcs[j][1], 'l_par'], writes=[ms_[j][1]])
                    for j in J:
                        C.op('dve', lambda e: e.tensor_tensor_scan(out=hs_[j][0], data0=as_[j][0], data1=ms_[j][0], initial=hprev[:, cts[j]:cts[j] + 1],
                                                                   op0=ALU.mult, op1=ALU.add), reads=[as_[j][1], ms_[j][1], ('l_hprev', cts[j])], writes=[hs_[j][1]])
                    for j in J:
                        C.op('dve', lambda e: e.tensor_copy(out=hprev[:, cts[j]:cts[j] + 1], in_=hs_[j][0][:, 511:512]), reads=[hs_[j][1]], writes=[('l_hprev', cts[j])])
                    if t0 >= OWN0:
                        for j in J:
                            C.dma('sp', hlru_d[cts[j] * P:(cts[j] + 1) * P, t0 - OWN0:t0 - OWN0 + 512], hs_[j][0], reads=[hs_[j][1]])
                gemm("g1c", 'b', [[(hT_d, KT, [(w_in, C_LX + BW * b_, BW)])] for b_ in range(LW // BW)], 0, TCTX, 512, epi)

        if stages >= 4:
            with ExitStack() as st:
                T = nr_temps(st, "g2a")
                gq = sb(st, "g2a_gq", [P, 128], F32)
                xf = sb(st, "g2a_xf", [P, 2, 512], F32); XF = Ring("g2a_xf", xf, 2)
                kb = sb(st, "g2a_kb", [P, 2, 512], BF16); KB = Ring("g2a_kb", kb, 2)
                cst = sb(st, "g2a_cs", [P, TOWN // P, 32], F32)
                C.dma('sp', cst, cs_tok[OWN0:TCTX, :].rearrange("(t p) c -> p t c", p=P), writes=['g2a_cs'])
                stg = sb(st, "g2a_stg", [P, 2, 4, 512], BF16); STG = Ring("g2a_stg", stg, 2)
                C.dma('sp', gq, q_norm_g.partition_broadcast(P), writes=['g2a_gq'])

                def epi(bi, ci, t0, pts):
                    if bi == 4:
                        for j, (pt, pk) in enumerate(pts):
                            tile_i = (t0 - OWN0) // P + j
                            C.op('act', lambda e: e.activation(out=gate_sb[:, tile_i, :], in_=pt[:, 0:48], func=AF.Sigmoid),
                                 reads=[pk], writes=['gate_sb'])
                        return
                    sg, sgk = STG.next()
                    for j, (pt, pk) in enumerate(pts):
                        x_, xk = XF.next()
                        C.op('act', lambda e: e.copy(out=x_, in_=pt[:, 0:512]), reads=[pk], writes=[xk])
                        c_, ck = cst[:, (t0 - OWN0) // P + j, :], 'g2a_cs'
                        k_, kk = KB.next()
                        normrope(T, x_, xk, 4, gq, 'g2a_gq', c_, ck, k_, kk)
                        p2, p2k = PSB.next()
                        for g in range(4):
                            C.op('pe', lambda e: e.transpose(out=p2[:, g * P:(g + 1) * P], in_=k_[:, g * P:(g + 1) * P],
                                                             identity=ident[:]), reads=[kk, 'ident'], writes=[p2k], inc=(g == 3))
                        evac_copy(j, sg[:, :, j * P:(j + 1) * P], p2[:, 0:512].rearrange("p (g t) -> p g t", g=4), [p2k], [sgk])
                    dst = qT_d[4 * bi:4 * bi + 4, :, t0 - OWN0:t0 - OWN0 + 512].rearrange("g p t -> p g t")
                    C.dma('sp', dst, sg, reads=[sgk])
                blocks = [[(hT_d, KT, [(w_in, 512 * b_, 512)])] for b_ in range(4)] + [[(hT_d, KT, [(w_in, C_G, 48)])]]
                gemm("g2a", 'a', blocks, OWN0, TOWN, 512, epi)

        if stages >= 5:
            with ExitStack() as st:
                yx = sb(st, "g2b_y", [P, 2, 512], F32); YX = Ring("g2b_y", yx, 2)
                uu = sb(st, "g2b_u", [P, 2, 512], F32); UU = Ring("g2b_u", uu, 2)
                hl = sb(st, "g2b_h", [P, 8, 512], F32); HL = Ring("g2b_h", hl, 8)
                ob = sb(st, "g2b_o", [P, 3, 512], BF16); OB = Ring("g2b_o", ob, 3)
                BW = min(256, LW)
                nlb = LW // BW

                def pre(bi, ci, t0):
                    if bi >= nlb:
                        return None
                    to = t0 - OWN0
                    res = []
                    for j in range(BW // P):
                        ct = bi * (BW // P) + j
                        h_, hk = HL.next()
                        C.dma('act', h_, hlru_d[ct * P:(ct + 1) * P, to:to + 512], writes=[hk])
                        res.append((h_, hk))
                    return res

                def epi(bi, ci, t0, pts, pr):
                    to = t0 - OWN0
                    for j, (pt, pk) in enumerate(pts):
                        o_, ok = OB.next()
                        if bi < nlb:
                            ct = bi * (BW // P) + j
                            y_, yk = YX.next(); u_, uk = UU.next(); h_, hk = pr[j]
                            C.op('act', lambda e: e.copy(out=y_, in_=pt[:, 0:512]), reads=[pk], writes=[yk])
                            C.op('dve', lambda e: e.tensor_tensor(out=u_, in0=y_, in1=y_, op=ALU.mult), reads=[yk], writes=[uk])
                            C.op('dve', lambda e: e.tensor_scalar(out=u_, in0=u_, scalar1=0.044715, scalar2=1.0, op0=ALU.mult, op1=ALU.add),
                                 reads=[uk], writes=[uk])
                            C.op('dve', lambda e: e.tensor_tensor(out=u_, in0=u_, in1=y_, op=ALU.mult), reads=[uk, yk], writes=[uk])
                            C.op('act', lambda e: e.activation(out=u_, in_=u_, func=AF.Sigmoid, scale=1.5957691216057308), reads=[uk], writes=[uk])
                            C.op('dve', lambda e: e.tensor_tensor(out=u_, in0=u_, in1=y_, op=ALU.mult), reads=[uk, yk], writes=[uk])
                            C.op('dve', lambda e: e.tensor_tensor(out=o_, in0=u_, in1=h_, op=ALU.mult), reads=[uk, hk], writes=[ok])
                            C.dma('sp', lruoT_d[ct * P:(ct + 1) * P, to:to + 512], o_, reads=[ok])
                        else:
                            row = (bi - nlb) * 256 + j * P
                            C.op('act', lambda e: e.activation(out=o_, in_=pt[:, 0:512], func=AF.Sigmoid), reads=[pk], writes=[ok])
                            C.dma('sp', gT_d[row:row + P, to:to + 512], o_, reads=[ok])
                blocks = [[(hT_d, KT, [(w_in, C_LY + BW * b_, BW)])] for b_ in range(nlb)]
                blocks += [[(hT_d, KT, [(w_in, C_MG + 256 * b_, 256)])] for b_ in range(2 * DM // 256)]
                gemm("g2b", 'b', blocks, OWN0, TOWN, 512, epi, resident=True, pre=pre)

        AW = 129 + NB
        if stages >= 6:
            kcT = sb(es, "kcT", [P, 4, NCP], BF16)
            vca = sb(es, "vca", [P, 4, NCT, AW], BF16)
            with ExitStack() as st:
                T = nr_temps(st, "cp")
                w1 = sb(st, "cp_w1", [P, 2, 32, 256], BF16); posT = sb(st, "cp_pos", [P, 2, 32], BF16)
                b1 = sb(st, "cp_b1", [P, 2, 2], F32); w2 = sb(st, "cp_w2", [P, 2, 2, 128], BF16)
                b2b = sb(st, "cp_b2", [P, 2, 128], F32); gk = sb(st, "cp_gk", [P, 128], F32)
                csc = sb(st, "cp_cs", [P, NCT, 32], F32)
                src = sb(st, "cp_src", [P, 2, TCTX], BF16); SRC = Ring("cp_src", src, 2)
                hx = sb(st, "cp_hx", [P, 2, 512], F32); HX = Ring("cp_hx", hx, 2)
                hu = sb(st, "cp_hu", [P, 2, 512], F32); HU = Ring("cp_hu", hu, 2)
                hid = sb(st, "cp_hid", [P, 2, 2, NCP], BF16); HID = Ring("cp_hid", hid, 2)
                hb = sb(st, "cp_hb", [P, 2, 2], F32)
                xf = sb(st, "cp_xf", [P, 2, 512], F32); XF = Ring("cp_xf", xf, 2)
                kb = sb(st, "cp_kb", [P, 2, 512], BF16); KB = Ring("cp_kb", kb, 2)
                for kv in range(2):
                    C.dma('pool', w1[:, kv], cmp_w1[kv].rearrange("(l d) h -> d l h", d=128), writes=['cp_w1'])
                    C.dma('pool', posT[:, kv], cmp_pos[kv].rearrange("l d -> d l"), writes=['cp_pos'], allow_slow_non_contiguous=True)
                    C.dma('sp', b1[:, kv], cmp_b1[kv].rearrange("(t p) -> p t", p=P), writes=['cp_b1'], allow_slow_non_contiguous=True)
                    C.dma('pool', w2[:, kv], cmp_w2[kv].rearrange("(t p) d -> p t d", p=P), writes=['cp_w2'])
                    C.dma('sp', b2b[:, kv], cmp_b2[kv].partition_broadcast(P), writes=['cp_b2'])
                C.dma('sp', gk, k_norm_g[0].partition_broadcast(P), writes=['cp_gk'])
                C.dma('sp', csc, cs_cmp.rearrange("(t p) c -> p t c", p=P), writes=['cp_cs'])
                for g in range(4):
                    C.dma('pool', vca[:, g, :, 129:129 + NB], t_overlap.rearrange("(t p) n -> p t n", p=P), writes=['vca'])
                C.op('dve', lambda e: e.memset(vca[:, :, :, 128:129], 1.0), reads=[], writes=['vca'])
                C.op('dve', lambda e: e.memset(hid, 0.0), writes=[('cp_hid', 0), ('cp_hid', 1)])
                for kv in range(2):
                    for ht in range(2):
                        pt, pk = PSF.next()
                        for l in range(32):
                            C.op('pe', lambda e: e.matmul(pt[:, 0:1], lhsT=w1[:, kv, l, ht * P:(ht + 1) * P], rhs=posT[:, kv, l:l + 1],
                                                          start=(l == 0), stop=(l == 31)),
                                 reads=['cp_w1', 'cp_pos'], writes=[pk], inc=(l == 31))
                        C.op('dve', lambda e: e.tensor_tensor(out=hb[:, kv, ht:ht + 1], in0=pt[:, 0:1], in1=b1[:, kv, ht:ht + 1], op=ALU.add),
                             reads=[pk, 'cp_b1'], writes=['cp_hb'])
                for g in range(4):
                    for kv in range(2):
                        s_, sk = SRC.next()
                        C.dma('sp', s_, (kcmpT_d if kv == 0 else vcmpT_d)[g], writes=[sk])
                        sv = s_.rearrange("p (c s) -> p c s", s=16)
                        hd, hdk = HID.next()
                        for ht in range(2):
                            pt, pk = PSF.next()
                            for l in range(32):
                                rhs = sv[:, 0:NC, l] if l < 16 else sv[:, 1:NC + 1, l - 16]
                                C.op('pe', lambda e: e.matmul(pt[:, 0:NC], lhsT=w1[:, kv, l, ht * P:(ht + 1) * P], rhs=rhs,
                                                              start=(l == 0), stop=(l == 31)),
                                     reads=['cp_w1', sk], writes=[pk], inc=(l == 31))
                            x_, xk = HX.next(); u_, uk = HU.next()
                            C.op('act', lambda e: e.activation(out=x_[:, 0:NC], in_=pt[:, 0:NC], func=AF.Identity, bias=hb[:, kv, ht:ht + 1]),
                                 reads=[pk, 'cp_hb'], writes=[xk])
                            C.op('dve', lambda e: e.tensor_tensor(out=u_[:, 0:NC], in0=x_[:, 0:NC], in1=x_[:, 0:NC], op=ALU.mult), reads=[xk], writes=[uk])
                            C.op('dve', lambda e: e.tensor_scalar(out=u_[:, 0:NC], in0=u_[:, 0:NC], scalar1=0.044715, scalar2=1.0,
                                                                  op0=ALU.mult, op1=ALU.add), reads=[uk], writes=[uk])
                            C.op('dve', lambda e: e.tensor_tensor(out=u_[:, 0:NC], in0=u_[:, 0:NC], in1=x_[:, 0:NC], op=ALU.mult), reads=[uk, xk], writes=[uk])
                            C.op('act', lambda e: e.activation(out=u_[:, 0:NC], in_=u_[:, 0:NC], func=AF.Sigmoid, scale=1.5957691216057308),
                                 reads=[uk], writes=[uk])
                            C.op('dve', lambda e: e.tensor_tensor(out=hd[:, ht, 0:NC], in0=u_[:, 0:NC], in1=x_[:, 0:NC], op=ALU.mult),
                                 reads=[uk, xk], writes=[hdk])
                        for ct in range(NCT):
                            pt, pk = PSF.next()
                            for ht in range(2):
                                C.op('pe', lambda e: e.matmul(pt[:, 0:128], lhsT=hd[:, ht, ct * P:(ct + 1) * P], rhs=w2[:, kv, ht, :],
                                                              start=(ht == 0), stop=(ht == 1)), reads=[hdk, 'cp_w2'], writes=[pk], inc=(ht == 1))
                            if kv == 1:
                                C.op('dve', lambda e: e.tensor_tensor(out=vca[:, g, ct, 0:128], in0=pt[:, 0:128], in1=b2b[:, 1, :], op=ALU.add),
                                     reads=[pk, 'cp_b2'], writes=['vca'])
                            else:
                                x_, xk = XF.next(); k_, kk = KB.next()
                                C.op('dve', lambda e: e.tensor_tensor(out=x_[:, 0:128], in0=pt[:, 0:128], in1=b2b[:, 0, :], op=ALU.add),
                                     reads=[pk, 'cp_b2'], writes=[xk])
                                normrope(T, x_, xk, 1, gk, 'cp_gk', csc[:, ct, :], 'cp_cs', k_, kk)
                                p2, p2k = PSB.next()
                                C.op('pe', lambda e: e.transpose(out=p2[:, 0:P], in_=k_[:, 0:P], identity=ident[:]), reads=[kk, 'ident'], writes=[p2k])
                                C.op('act', lambda e: e.copy(out=kcT[:, g, ct * P:(ct + 1) * P], in_=p2[:, 0:P]), reads=[p2k], writes=['kcT'])
            C.barrier()

        if stages >= 7:
            with ExitStack() as st:
                Eb = sb(st, "at_E", [P, TCTX], BF16)
                cmask = sb(st, "at_cm", [P, NCHO, NCT, 512], BF16)
                sdiag = sb(st, "at_sd", [P, 4, 512], BF16); wmask = sb(st, "at_wm", [P, 2, 8, 512], BF16)
                kS = sb(st, "at_kS", [P, TCTX], BF16); kW = sb(st, "at_kW", [P, TCTX], BF16)
                NKT = TCTX // P
                vS = sb(st, "at_vS", [P, NKT, 129], BF16); vW = sb(st, "at_vW", [P, NKT, 129], BF16)
                qt = sb(st, "at_q", [P, 2, 4, 512], BF16); QT = Ring("at_q", qt, 2)
                pT = sb(st, "at_p", [P, 3, 512], BF16); PT = Ring("at_p", pT, 3)
                nsT = sb(st, "at_ns", [P, 512], BF16)
                tA = sb(st, "at_tA", [P, 2, NB], F32); TA = Ring("at_tA", tA, 2)
                tB = sb(st, "at_tB", [P, 2, NB], F32); TB = Ring("at_tB", tB, 2)
                imp = sb(st, "at_imp", [P, 4, NB], F32)
                wv = sb(st, "at_wv", [P, 2, NB], F32); WV = Ring("at_wv", wv, 2)
                wr = sb(st, "at_wr", [P, 2, NB], F32); WR = Ring("at_wr", wr, 2)
                m8 = sb(st, "at_m8", [P, 2, 16], F32); M8 = Ring("at_m8", m8, 2)
                selb = sb(st, "at_sel", [P, 2, NB], BF16); SEL = Ring("at_sel", selb, 2)
                rs = sb(st, "at_rs", [P, 4, 2], F32); RS = Ring("at_rs", rs, 4)
                oacc = sb(st, "at_o", [P, 4, 512], F32)
                obf = sb(st, "at_ob", [P, 4, 512], BF16)
                oT = sb(st, "at_oT", [P, 2, 4, 512], BF16); OT = Ring("at_oT", oT, 2)
                PS_S = Ring("psF", psF, 2, base=0); PS_A = Ring("psF", psF, 4, base=2)
                C.dma('pool', Eb[0:NB, :], t_E[:, :], writes=['at_E'])
                C.dma('pool', cmask, t_cmpmask, writes=['at_cm'])
                C.dma('pool', sdiag, t_slcdiag, writes=['at_sd'])
                C.dma('pool', wmask, t_winmask, writes=['at_wm'])
                C.op('dve', lambda e: e.memset(vS[:, :, 128:129], 1.0), writes=['at_vS'])
                C.op('dve', lambda e: e.memset(vW[:, :, 128:129], 1.0), writes=['at_vW'])
                JQ0 = OWN0 // P

                def pipeline(items, fqk, fpv):
                    prev = None
                    for it in items:
                        cur = fqk(it)
                        if prev is not None:
                            fpv(prev[0], *prev[1])
                        prev = (it, cur)
                    if prev is not None:
                        fpv(prev[0], *prev[1])

                def evac(accs, hh, gidx, first, with_imp):
                    for sub in range(4):
                        pa, pak = accs[sub]
                        r_, rk = RS.next()
                        tile_i = None
                        C.op('dve', lambda e: e.tensor_scalar(out=r_[:, 0:1], in0=pa[:, 128:129], scalar1=1e-30, scalar2=None, op0=ALU.max),
                             reads=[pak], writes=[rk])
                        C.op('dve', lambda e: e.reciprocal(out=r_[:, 0:1], in_=r_[:, 0:1]), reads=[rk], writes=[rk])
                        C.op('dve', lambda e: e.tensor_tensor(out=r_[:, 1:2], in0=r_[:, 0:1], in1=gate_sb[:, evac.tile0 + sub, gidx:gidx + 1], op=ALU.mult),
                             reads=[rk, 'gate_sb'], writes=[rk])
                        od = oacc[:, sub, hh * P:(hh + 1) * P]
                        ok = ('at_o', sub, hh)
                        if first:
                            C.op('dve', lambda e: e.tensor_scalar(out=od, in0=pa[:, 0:128], scalar1=r_[:, 1:2], scalar2=None, op0=ALU.mult),
                                 reads=[pak, rk], writes=[ok])
                        else:
                            C.op('dve', lambda e: e.scalar_tensor_tensor(out=od, in0=pa[:, 0:128], scalar=r_[:, 1:2], in1=od, op0=ALU.mult, op1=ALU.add),
                                 reads=[pak, rk, ok], writes=[ok])
                        if with_imp:
                            ik = ('at_imp', sub)
                            if hh == 0:
                                C.op('dve', lambda e: e.tensor_scalar(out=imp[:, sub, :], in0=pa[:, 129:129 + NB], scalar1=r_[:, 0:1], scalar2=None, op0=ALU.mult),
                                     reads=[pak, rk], writes=[ik])
                            else:
                                C.op('dve', lambda e: e.scalar_tensor_tensor(out=imp[:, sub, :], in0=pa[:, 129:129 + NB], scalar=r_[:, 0:1], in1=imp[:, sub, :],
                                                                             op0=ALU.mult, op1=ALU.add), reads=[pak, rk, ik], writes=[ik])

                for g in range(4):
                    C.dma('sp', kS, kslcT_d[g], writes=['at_kS'])
                    C.dma('sp', kW, kwinT_d[g], writes=['at_kW'])
                    C.dma('sp', vS[:, :, 0:128], vslc_d[:, g * P:(g + 1) * P].rearrange("(j p) d -> p j d", p=P), writes=['at_vS'])
                    C.dma('sp', vW[:, :, 0:128], vwin_d[:, g * P:(g + 1) * P].rearrange("(j p) d -> p j d", p=P), writes=['at_vW'])
                    for i in range(NCHO):
                        q_, qk = QT.next()
                        C.dma('sp', q_, qT_d[4 * g:4 * g + 4, :, i * 512:(i + 1) * 512].rearrange("h p t -> p h t"), writes=[qk])
                        evac.tile0 = i * 4
                        jd0 = JQ0 + 4 * i
                        jcs = [jc for jc in range(NCT) if 16 * P * jc + 31 <= OWN0 + 512 * i + 511]
                        for hh in range(4):
                            accs = [PS_A.next() for _ in range(4)]

                            def qk_c(jc):
                                ps, psk = PS_S.next()
                                C.op('pe', lambda e: e.matmul(ps[:, 0:512], lhsT=kcT[:, g, jc * P:(jc + 1) * P], rhs=q_[:, hh, :], start=True, stop=True),
                                     reads=['kcT', qk], writes=[psk])
                                return ps, psk

                            def pv_c(jc, ps, psk):
                                p_, pk = PT.next()
                                C.op('act', lambda e: e.activation(out=p_, in_=ps[:, 0:512], func=AF.Exp, scale=SCALE), reads=[psk], writes=[pk])
                                C.op('dve', lambda e: e.tensor_tensor(out=p_, in0=p_, in1=cmask[:, i, jc, :], op=ALU.mult), reads=[pk, 'at_cm'], writes=[pk])
                                for sub in range(4):
                                    pa, pak = accs[sub]
                                    C.op('pe', lambda e: e.matmul(pa[:, 0:AW], lhsT=p_[:, sub * P:(sub + 1) * P], rhs=vca[:, g, jc, :],
                                                                  start=(jc == jcs[0]), stop=(jc == jcs[-1])),
                                         reads=[pk, 'vca'], writes=[pak], inc=(jc == jcs[-1]))
                            pipeline(jcs, qk_c, pv_c)
                            evac(accs, hh, (4 * g + hh) * 3 + 0, True, True)
                        for sub in range(4):
                            a_, ak = TA.next(); b_, bk = TB.next()
                            r0 = i * 512 + sub * P
                            C.dma('sp', a_, t_topA[r0:r0 + P, :], writes=[ak])
                            C.dma('sp', b_, t_topB[r0:r0 + P, :], writes=[bk])
                            w_, wk_ = WV.next(); w2_, w2k = WR.next(); m_, mk = M8.next(); s_, sk = SEL.next()
                            C.op('dve', lambda e: e.tensor_tensor(out=w_, in0=imp[:, sub, :], in1=a_, op=ALU.mult), reads=[('at_imp', sub), ak], writes=[wk_])
                            C.op('dve', lambda e: e.tensor_tensor(out=w_, in0=w_, in1=b_, op=ALU.add), reads=[wk_, bk], writes=[wk_])
                            C.op('dve', lambda e: e.max(out=m_[:, 0:8], in_=w_), reads=[wk_], writes=[mk])
                            C.op('dve', lambda e: e.match_replace(out=w2_, in_to_replace=m_[:, 0:8], in_values=w_, imm_value=-1e30),
                                 reads=[wk_, mk], writes=[w2k])
                            C.op('dve', lambda e: e.max(out=m_[:, 8:16], in_=w2_), reads=[w2k], writes=[mk])
                            C.op('dve', lambda e: e.tensor_scalar(out=m_[:, 15:16], in0=m_[:, 15:16], scalar1=0.0, scalar2=None, op0=ALU.max),
                                 reads=[mk], writes=[mk])
                            C.op('dve', lambda e: e.tensor_scalar(out=w2_, in0=w_, scalar1=m_[:, 15:16], scalar2=None, op0=ALU.is_ge),
                                 reads=[wk_, mk], writes=[w2k])
                            C.op('dve', lambda e: e.tensor_scalar(out=s_, in0=w2_, scalar1=-NEGB, scalar2=NEGB, op0=ALU.mult, op1=ALU.add),
                                 reads=[w2k], writes=[sk])
                            p2, p2k = PSB.next()
                            C.op('pe', lambda e: e.transpose(out=p2[0:NB, 0:P], in_=s_, identity=ident[:]), reads=[sk, 'ident'], writes=[p2k])
                            C.op('act', lambda e: e.copy(out=nsT[0:NB, sub * P:(sub + 1) * P], in_=p2[0:NB, 0:P]), reads=[p2k], writes=['at_ns'])
                        for hh in range(4):
                            accs = [PS_A.next() for _ in range(4)]

                            def qk_s(j):
                                ps, psk = PS_S.next()
                                C.op('pe', lambda e: e.matmul(ps[:, 0:512], lhsT=kS[:, j * P:(j + 1) * P], rhs=q_[:, hh, :], start=True, stop=False),
                                     reads=['at_kS', qk], writes=[psk], inc=False)
                                C.op('pe', lambda e: e.matmul(ps[:, 0:512], lhsT=Eb[0:NB, j * P:(j + 1) * P], rhs=nsT[0:NB, :], start=False, stop=True),
                                     reads=['at_E', 'at_ns'], writes=[psk])
                                return ps, psk

                            def pv_s(j, ps, psk):
                                p_, pk = PT.next()
                                C.op('act', lambda e: e.activation(out=p_, in_=ps[:, 0:512], func=AF.Exp, scale=SCALE), reads=[psk], writes=[pk])
                                if j >= jd0:
                                    C.op('dve', lambda e: e.tensor_tensor(out=p_, in0=p_, in1=sdiag[:, j - jd0, :], op=ALU.mult),
                                         reads=[pk, 'at_sd'], writes=[pk])
                                for sub in range(4):
                                    if j > jd0 + sub:
                                        continue
                                    pa, pak = accs[sub]
                                    C.op('pe', lambda e: e.matmul(pa[:, 0:129], lhsT=p_[:, sub * P:(sub + 1) * P], rhs=vS[:, j, :],
                                                                  start=(j == 0), stop=(j == jd0 + sub)),
                                         reads=[pk, 'at_vS'], writes=[pak], inc=(j == jd0 + sub))
                            pipeline(list(range(jd0 + 4)), qk_s, pv_s)
                            evac(accs, hh, (4 * g + hh) * 3 + 1, False, False)
                        for hh in range(4):
                            accs = [PS_A.next() for _ in range(4)]

                            def qk_w(k8):
                                j = jd0 - 4 + k8
                                ps, psk = PS_S.next()
                                C.op('pe', lambda e: e.matmul(ps[:, 0:512], lhsT=kW[:, j * P:(j + 1) * P], rhs=q_[:, hh, :], start=True, stop=True),
                                     reads=['at_kW', qk], writes=[psk])
                                return ps, psk

                            def pv_w(k8, ps, psk):
                                j = jd0 - 4 + k8
                                p_, pk = PT.next()
                                C.op('act', lambda e: e.activation(out=p_, in_=ps[:, 0:512], func=AF.Exp, scale=SCALE), reads=[psk], writes=[pk])
                                C.op('dve', lambda e: e.tensor_tensor(out=p_, in0=p_, in1=wmask[:, 0 if i == 0 else 1, k8, :], op=ALU.mult),
                                     reads=[pk, 'at_wm'], writes=[pk])
                                for sub in range(4):
                                    if not (sub <= k8 <= sub + 4):
                                        continue
                                    pa, pak = accs[sub]
                                    C.op('pe', lambda e: e.matmul(pa[:, 0:129], lhsT=p_[:, sub * P:(sub + 1) * P], rhs=vW[:, j, :],
                                                                  start=(k8 == sub), stop=(k8 == sub + 4)),
                                         reads=[pk, 'at_vW'], writes=[pak], inc=(k8 == sub + 4))
                            pipeline(list(range(8)), qk_w, pv_w)
                            evac(accs, hh, (4 * g + hh) * 3 + 2, False, False)
                        o_, otk = OT.next()
                        for sub in range(4):
                            oks = [('at_o', sub, hh) for hh in range(4)]
                            C.op('act', lambda e: e.copy(out=obf[:, sub, :], in_=oacc[:, sub, :]), reads=oks, writes=[('at_ob', sub)])
                            p2, p2k = PSB.next()
                            for hh in range(4):
                                C.op('pe', lambda e: e.transpose(out=p2[:, hh * P:(hh + 1) * P], in_=obf[:, sub, hh * P:(hh + 1) * P], identity=ident[:]),
                                     reads=[('at_ob', sub), 'ident'], writes=[p2k], inc=(hh == 3))
                            evac_copy(sub, o_[:, :, sub * P:(sub + 1) * P], p2[:, 0:512].rearrange("p (h t) -> p h t", h=4), [p2k], [otk])
                        C.dma('sp', attnT_d[4 * g * P:(4 * g + 4) * P, i * 512:(i + 1) * 512].rearrange("(h p) t -> p h t", p=P), o_, reads=[otk])
            C.barrier()

        if stages >= 8:
            with ExitStack() as st:
                gg = sb(st, "g3_g", [P, 8, 512], BF16); GG = Ring("g3_g", gg, 8)
                t1 = sb(st, "g3_t1", [P, 2, 512], F32); T1 = Ring("g3_t1", t1, 2)
                t2 = sb(st, "g3_t2", [P, 2, 512], F32); T2 = Ring("g3_t2", t2, 2)
                ob = sb(st, "g3_o", [P, 2, 512], BF16); OB = Ring("g3_o", ob, 2)

                def pre(bi, ci, t0):
                    res = []
                    for ct in range(2):
                        row = bi * 256 + ct * P
                        ga, gak = GG.next(); gb_, gbk = GG.next()
                        C.dma('act', ga, gT_d[row:row + P, t0:t0 + 512], writes=[gak])
                        C.dma('act', gb_, gT_d[DM + row:DM + row + P, t0:t0 + 512], writes=[gbk])
                        res.append((ga, gak, gb_, gbk))
                    return res

                def epi(bi, ci, t0, pts, pr):
                    to = t0
                    for ct in range(2):
                        row = bi * 256 + ct * P
                        (pa, pak), (pb_, pbk) = pts[ct], pts[2 + ct]
                        ga, gak, gb_, gbk = pr[ct]
                        a_, ak = T1.next(); b_, bk = T2.next(); o_, ok = OB.next()
                        C.op('dve', lambda e: e.tensor_tensor(out=a_, in0=pa[:, 0:512], in1=ga, op=ALU.mult), reads=[pak, gak], writes=[ak])
                        C.op('dve', lambda e: e.tensor_tensor(out=b_, in0=pb_[:, 0:512], in1=gb_, op=ALU.mult), reads=[pbk, gbk], writes=[bk])
                        C.op('dve', lambda e: e.tensor_tensor(out=o_, in0=a_, in1=b_, op=ALU.add), reads=[ak, bk], writes=[ok])
                        C.dma('sp', mixT_d[row:row + P, to:to + 512], o_, reads=[ok])
                blocks = [[(attnT_d, 16, [(w_bra, 256 * b_, 256)]), (lruoT_d, NLT, [(w_brb, 256 * b_, 256)])] for b_ in range(DM // 256)]
                gemm("g3", 'b', blocks, 0, TOWN, 512, epi, resident=True, pre=pre)

        if stages >= 9:
            with ExitStack() as st:
                xr = sb(st, "g4_x", [P, 8, 256], F32); XR = Ring("g4_x", xr, 8)

                def pre(bi, ci, t0):
                    res = []
                    for j in range(4):
                        x_, xk = XR.next()
                        r0 = t0 + j * P
                        C.dma('act', x_, x_ctx[OWN0 + r0:OWN0 + r0 + P, bi * 256:(bi + 1) * 256], writes=[xk])
                        res.append((x_, xk))
                    return res

                def epi(bi, ci, t0, pts, pr):
                    for j, (pt, pk) in enumerate(pts):
                        x_, xk = pr[j]
                        r0 = t0 + j * P
                        C.op('dve', lambda e: e.tensor_tensor(out=x_, in0=pt[:, 0:256], in1=x_, op=ALU.add), reads=[pk, xk], writes=[xk])
                        C.dma('sp', x1_d[r0:r0 + P, bi * 256:(bi + 1) * 256], x_, reads=[xk])
                gemm("g4", 'a', [[(mixT_d, KT, [(w_out, 256 * b_, 256)])] for b_ in range(DM // 256)], 0, TOWN, 512, epi,
                     resident=True, pre=pre)
            phase_norm("n2", x1_d, TOWN, norm2_g, h2T_d)

        if stages >= 10:
            with ExitStack() as st:
                sg_ = sb(st, "g5_s", [P, 2, 512], F32); SG = Ring("g5_s", sg_, 2)
                ob = sb(st, "g5_o", [P, 3, 512], BF16); OB = Ring("g5_o", ob, 3)

                def epi(bi, ci, t0, pts):
                    row = bi * P
                    (pg, pgk), (pu, puk) = pts[0], pts[1]
                    s_, sk = SG.next(); o_, ok = OB.next()
                    C.op('act', lambda e: e.activation(out=s_, in_=pg[:, 0:512], func=AF.Silu), reads=[pgk], writes=[sk])
                    C.op('dve', lambda e: e.tensor_tensor(out=o_, in0=pu[:, 0:512], in1=s_, op=ALU.mult), reads=[puk, sk], writes=[ok])
                    C.dma('sp', actT_d[row:row + P, t0:t0 + 512], o_, reads=[ok])
                blocks = [[(h2T_d, KT, [(w_ffi, P * b_, P), (w_ffi, FH + P * b_, P)])] for b_ in range(FH // P)]
                gemm("g5", 'b', blocks, 0, TOWN, 512, epi, resident=True)
            with ExitStack() as st:
                xr = sb(st, "g6_x", [P, 4, 512], F32); XR = Ring("g6_x", xr, 4)

                def pre(bi, ci, t0):
                    res = []
                    for j in range(2):
                        x_, xk = XR.next()
                        r0 = t0 + j * P
                        C.dma('act', x_, x1_d[r0:r0 + P, bi * 512:(bi + 1) * 512], writes=[xk])
                        res.append((x_, xk))
                    return res

                def epi(bi, ci, t0, pts, pr):
                    for j, (pt, pk) in enumerate(pts):
                        x_, xk = pr[j]
                        r0 = t0 + j * P
                        C.op('dve', lambda e: e.tensor_tensor(out=x_, in0=pt[:, 0:512], in1=x_, op=ALU.add), reads=[pk, xk], writes=[xk])
                        C.dma('sp', out[r0:r0 + P, bi * 512:(bi + 1) * 512], x_, reads=[xk])
                gemm("g6", 'a', [[(actT_d, FH // P, [(w_ffo, 512 * b_, 512)])] for b_ in range(DM // 512)], 0, TOWN, 256, epi,
                     wslots=1, aslots=2, pre=pre)
        else:
            with ExitStack() as st:
                z = sb(st, "zz", [P, DM], F32)
                C.op('dve', lambda e: e.memset(z[:], 0.0), writes=['zz'])
                for i in range(TOWN // P):
                    C.dma('sp', out[i * P:(i + 1) * P, :], z[:], reads=['zz'])
        C.barrier()
    return nc


def make_tables(cfg, r):
    TCTX, TOWN = cfg['TCTX'], cfg['TOWN']
    NB = TCTX // 64; NCP = TCTX // 16; NC = NCP - 1; NCT = NCP // P
    NCHC = TCTX // 512; NCHO = TOWN // 512
    OWN0 = TCTX - TOWN
    pad = TCTX - TOWN * (r + 1)
    inv = 1.0 / (500000.0 ** (np.arange(16, dtype=np.float32) * 2.0 / 32.0))
    pos = (np.arange(TCTX) - pad).astype(np.float32)
    ang = pos[:, None] * inv[None, :].astype(np.float32)
    cs_tok = np.concatenate([np.cos(ang), np.sin(ang)], axis=1).astype(np.float32)
    cend = (np.arange(NCP) * 16 + 31 - pad).astype(np.float32)
    ang = cend[:, None] * inv[None, :].astype(np.float32)
    cs_cmp = np.concatenate([np.cos(ang), np.sin(ang)], axis=1).astype(np.float32)
    c_l = np.arange(P)[:, None, None, None]; i_ = np.arange(NCHO)[None, :, None, None]
    jc = np.arange(NCT)[None, None, :, None]; ql = np.arange(512)[None, None, None, :]
    c = jc * P + c_l; t = OWN0 + 512 * i_ + ql
    cmpmask = ((c < NC) & (16 * c >= pad) & (16 * c + 31 <= t)).astype(np.float32)
    cc = np.arange(NCP)[:, None]; nn = np.arange(NB)[None, :]
    overlap = ((16 * cc <= 64 * nn + 63) & (16 * cc + 31 >= 64 * nn) & (cc < NC)).astype(np.float32)
    E = (np.arange(TCTX)[None, :] // 64 == np.arange(NB)[:, None]).astype(np.float32)
    kl = np.arange(P)[:, None, None]; jj = np.arange(4)[None, :, None]; q2 = np.arange(512)[None, None, :]
    slcdiag = ((128 * jj + kl) <= q2).astype(np.float32)
    kl4 = np.arange(P)[:, None, None, None]; var = np.arange(2)[None, :, None, None]
    kt8 = np.arange(8)[None, None, :, None]; q4 = np.arange(512)[None, None, None, :]
    krel = -512 + 128 * kt8 + kl4
    wm = (krel <= q4) & (krel > q4 - 512)
    wm = np.broadcast_to(wm, (P, 2, 8, 512)).copy()
    wm[:, 0] &= np.broadcast_to((OWN0 + krel[:, 0] >= pad), (P, 8, 512))
    winmask = wm.astype(np.float32)
    tt = (OWN0 + np.arange(TOWN))[:, None]
    valid = (nn * 64 >= pad) & (nn * 64 <= tt)
    cur = tt // 64
    forced = ((nn == pad // 64) | (nn == cur) | (nn == cur - 1)) & valid
    topA = (valid & ~forced).astype(np.float32)
    fval = np.where(nn == cur, 3e30, np.where(nn == cur - 1, 2e30, 1e30))
    topB = np.where(forced, fval, np.where(valid, 0.0, -1.0)).astype(np.float32)
    vchunk = np.broadcast_to(((np.arange(NCHC) * 512) >= pad).astype(np.float32)[None, :], (P, NCHC)).copy()
    return dict(cs_tok=cs_tok, cs_cmp=cs_cmp, t_cmpmask=cmpmask, t_overlap=overlap, t_E=E, t_slcdiag=slcdiag,
                t_winmask=winmask, t_topA=topA, t_topB=topB, t_vchunk=vchunk)


def kernel(debug_outs=(), stages=99, **inputs):
    x = np.asarray(inputs["x"])
    B, S, DM = x.shape
    LW = DM // 2
    FH = np.asarray(inputs["w_ffn_out"]).shape[1]
    cfg = dict(DM=DM, TCTX=S, TOWN=S // 4, LW=LW, FH=FH, stages=stages)
    TOWN = cfg['TOWN']
    nc = build_program(cfg, debug_outs)
    names1 = ["norm1_g", "norm2_g", "q_norm_g", "conv_b", "lru_lambda"]
    shared = {}
    for k, v in inputs.items():
        if k == "x":
            continue
        a = np.asarray(v, dtype=np.float32)[0]
        shared[k] = np.ascontiguousarray(a)
    n_cores = B * 4
    in_maps = []
    for c in range(n_cores):
        b, r = c // 4, c % 4
        pad = S - TOWN * (r + 1)
        xc = np.zeros((S, DM), np.float32)
        xc[pad:] = x[b, :TOWN * (r + 1)]
        m = dict(shared)
        m["x_ctx"] = xc
        m.update(make_tables(cfg, r))
        in_maps.append(m)
    res = run_bass_kernel_spmd(nc, in_maps, core_ids=list(range(n_cores)))
    outp = np.zeros((B, S, DM), np.float32)
    for c in range(n_cores):
        b, r = c // 4, c % 4
        outp[b, r * TOWN:(r + 1) * TOWN] = res.results[c]["out"]
    if debug_outs:
        return outp, res.results
    return outp
```

```python
import math
from contextlib import ExitStack
import numpy as np
import concourse.bass as bass
import concourse.mybir as mybir
from concourse.bass_utils import run_bass_kernel_spmd

F32, BF16 = mybir.dt.float32, mybir.dt.bfloat16
AF = mybir.ActivationFunctionType
ALU = mybir.AluOpType
AX = mybir.AxisListType
NDMA = 16
P = 128
NEGB = -30000.0
EPS = 1e-6


class Ctx:
    def __init__(self, nc, es):
        self.nc = nc
        self.E = {'pe': nc.tensor, 'act': nc.scalar, 'dve': nc.vector, 'pool': nc.gpsimd, 'sp': nc.sync}
        self.sem = {k: es.enter_context(nc.semaphore('s_' + k)) for k in ('pe', 'act', 'dve', 'pool')}
        self.dsem = [es.enter_context(nc.semaphore('s_d%d' % i)) for i in range(NDMA)]
        self.cnt = {k: 0 for k in self.sem}
        self.pend = {k: False for k in self.sem}
        self.duse = [0] * NDMA
        self.di = 0
        self.waited = {k: {} for k in self.E}
        self.tiles = {}
        self.nwait = 0

    def _semh(self, key):
        return self.sem[key] if isinstance(key, str) else self.dsem[key[1]]

    def _wait(self, X, evs):
        need = {}
        for (k, v) in evs:
            if v > need.get(k, 0):
                need[k] = v
        for k, v in need.items():
            if X == 'pe' and k == 'pe':
                continue
            if self.waited[X].get(k, 0) >= v:
                continue
            self.E[X].wait_ge(self._semh(k), v)
            self.waited[X][k] = v
            self.nwait += 1

    def _deps(self, reads, writes):
        evs = []
        for k in reads:
            t = self.tiles.get(k)
            if t and t[0]:
                evs.append(t[0])
        for k in writes:
            t = self.tiles.get(k)
            if t:
                if t[0]:
                    evs.append(t[0])
                evs.extend(t[1].items())
        return evs

    def _record(self, ev, reads, writes):
        for k in reads:
            t = self.tiles.setdefault(k, [None, {}])
            if ev[1] > t[1].get(ev[0], 0):
                t[1][ev[0]] = ev[1]
        for k in writes:
            self.tiles[k] = [ev, {}]

    def op(self, X, emit, reads=(), writes=(), inc=True):
        self._wait(X, self._deps(reads, writes))
        ins = emit(self.E[X])
        if inc:
            self.cnt[X] += 1
            ins.then_inc(self.sem[X], 1)
            ev = (X, self.cnt[X])
            self.pend[X] = False
        else:
            ev = (X, self.cnt[X] + 1)
            self.pend[X] = True
        self._record(ev, reads, writes)
        return ins

    def dma(self, Q, out, in_, reads=(), writes=(), **kw):
        i = self.di % NDMA
        self.di += 1
        n = self.duse[i]
        evs = self._deps(reads, writes)
        if n > 0:
            evs.append((('d', i), 16 * n))
        self._wait(Q, evs)
        self.E[Q].dma_start(out=out, in_=in_, **kw).then_inc(self.dsem[i], 16)
        self.duse[i] = n + 1
        self._record((('d', i), 16 * (n + 1)), reads, writes)

    def barrier(self):
        for k in self.pend:
            assert not self.pend[k], k
        evs = [(k, self.cnt[k]) for k in self.cnt if self.cnt[k] > 0]
        evs += [(('d', i), 16 * self.duse[i]) for i in range(NDMA) if self.duse[i] > 0]
        for X in self.E:
            self._wait(X, evs)
        self.tiles = {}


class Ring:
    def __init__(self, name, t, n, base=0):
        self.name, self.t, self.n, self.i, self.base = name, t, n, 0, base

    def next(self):
        s = self.base + self.i % self.n
        self.i += 1
        return self.t[:, s], (self.name, s)


def bcast_rows(ap1d, n):
    return ap1d.partition_broadcast(P)


def build_program(cfg, debug_outs=()):
    DM, TCTX, TOWN, LW, FH = cfg['DM'], cfg['TCTX'], cfg['TOWN'], cfg['LW'], cfg['FH']
    KT = DM // P
    NB = TCTX // 64
    NCP = TCTX // 16
    NC = NCP - 1
    NCT = NCP // P
    NCHC = TCTX // 512
    NCHO = TOWN // 512
    OWN0 = TCTX - TOWN
    NLT = LW // P
    INW = 2048 + 3072 + 48 + 2 * LW + 2 * DM
    C_Q, C_KV, C_G = 0, 2048, 5120
    C_LX = 5168
    C_LY = C_LX + LW
    C_MG = C_LY + LW

    nc = bass.Bass("TRN2", target_bir_lowering=False)

    def din(name, shape, dt=F32):
        return nc.dram_tensor(name, list(shape), dt, kind="ExternalInput").ap()

    def dscr(name, shape, dt):
        kind = "ExternalOutput" if name in debug_outs else "Internal"
        return nc.dram_tensor(name, list(shape), dt, kind=kind).ap()

    x_ctx = din("x_ctx", [TCTX, DM])
    norm1_g = din("norm1_g", [DM]); norm2_g = din("norm2_g", [DM])
    w_in = din("w_in", [DM, INW])
    q_norm_g = din("q_norm_g", [128]); k_norm_g = din("k_norm_g", [3, 128])
    cmp_pos = din("cmp_pos", [2, 32, 128]); cmp_w1 = din("cmp_w1", [2, 4096, 256])
    cmp_b1 = din("cmp_b1", [2, 256]); cmp_w2 = din("cmp_w2", [2, 256, 128]); cmp_b2 = din("cmp_b2", [2, 128])
    conv_w = din("conv_w", [4, LW]); conv_b = din("conv_b", [LW])
    lru_wg = din("lru_w_gates", [2, NLT, 128, 128]); lru_bg = din("lru_b_gates", [2, LW])
    lru_lam = din("lru_lambda", [LW])
    w_bra = din("w_branch_a", [2048, DM]); w_brb = din("w_branch_b", [LW, DM])
    w_out = din("w_out", [DM, DM])
    w_ffi = din("w_ffn_in", [DM, 2 * FH]); w_ffo = din("w_ffn_out", [FH, DM])
    cs_tok = din("cs_tok", [TCTX, 32]); cs_cmp = din("cs_cmp", [NCP, 32])
    t_cmpmask = din("t_cmpmask", [P, NCHO, NCT, 512])
    t_overlap = din("t_overlap", [NCP, NB])
    t_E = din("t_E", [NB, TCTX])
    t_slcdiag = din("t_slcdiag", [P, 4, 512])
    t_winmask = din("t_winmask", [P, 2, 8, 512])
    t_topA = din("t_topA", [TOWN, NB]); t_topB = din("t_topB", [TOWN, NB])
    t_vchunk = din("t_vchunk", [P, NCHC])
    out = nc.dram_tensor("out", [TOWN, DM], F32, kind="ExternalOutput").ap()

    hT_d = dscr("hT_d", [TCTX // 512, P, KT, 512], BF16)
    kcmpT_d = dscr("kcmpT_d", [4, P, TCTX], BF16); vcmpT_d = dscr("vcmpT_d", [4, P, TCTX], BF16)
    kslcT_d = dscr("kslcT_d", [4, P, TCTX], BF16); kwinT_d = dscr("kwinT_d", [4, P, TCTX], BF16)
    vslc_d = dscr("vslc_d", [TCTX, 512], BF16); vwin_d = dscr("vwin_d", [TCTX, 512], BF16)
    hlru_d = dscr("hlru_d", [LW, TOWN], F32)
    qT_d = dscr("qT_d", [16, P, TOWN], BF16)
    lruoT_d = dscr("lruoT_d", [LW, TOWN], BF16)
    gT_d = dscr("gT_d", [2 * DM, TOWN], BF16)
    attnT_d = dscr("attnT_d", [2048, TOWN], BF16)
    mixT_d = dscr("mixT_d", [DM, TOWN], BF16)
    x1_d = dscr("x1_d", [TOWN, DM], F32)
    h2T_d = dscr("h2T_d", [TOWN // 512, P, KT, 512], BF16)
    actT_d = dscr("actT_d", [TOWN // 256, P, FH // P, 256], BF16)

    es = ExitStack()
    with es:
        C = Ctx(nc, es)

        def sb(st, name, shape, dt):
            return st.enter_context(nc.sbuf_tensor(name, list(shape), dt))[:]

        ident = sb(es, "ident", [P, P], BF16)
        identf = sb(es, "identf", [P, P], F32)
        gate_sb = sb(es, "gate_sb", [P, TOWN // P, 48], F32)
        psF = es.enter_context(nc.psum_tensor("psF", [P, 6, 512], F32))[:]
        psB = es.enter_context(nc.psum_tensor("psB", [P, 2, 1024], BF16))[:]
        PSF = Ring("psF", psF, 6)
        PSB = Ring("psB", psB, 2)

        C.op('pool', lambda e: e.memset(identf[:], 1.0), writes=['identf'])
        C.op('pool', lambda e: e.affine_select(out=identf[:], in_=identf[:], pattern=[[-1, P]],
                                               compare_op=ALU.is_equal, fill=0.0, base=0, channel_multiplier=1),
             reads=['identf'], writes=['identf'])
        C.op('dve', lambda e: e.tensor_copy(out=ident[:], in_=identf[:]), reads=['identf'], writes=['ident'])

        def phase_norm(tag, src_d, ntok, g_d, dst_d):
            with ExitStack() as st:
                gb = sb(st, tag + "gb", [P, DM], F32)
                xr = sb(st, tag + "x", [P, 2, DM], F32)
                junk = sb(st, tag + "junk", [P, DM], BF16)
                xn = sb(st, tag + "xn", [P, 2, DM], BF16)
                ss = sb(st, tag + "ss", [P, 4, 2], F32)
                stg = sb(st, tag + "stg", [P, 2, KT, 512], BF16)
                XR = Ring(tag + "x", xr, 2); XN = Ring(tag + "xn", xn, 2); SS = Ring(tag + "ss", ss, 4)
                STG = Ring(tag + "stg", stg, 2)
                C.dma('sp', gb[:], g_d.partition_broadcast(P), writes=[tag + 'gb'])
                for ch in range(ntok // 512):
                    sg, sgk = STG.next()
                    for sub in range(4):
                        t0 = ch * 512 + sub * P
                        xt, xk = XR.next()
                        C.dma('sp', xt, src_d[t0:t0 + P, :], writes=[xk])
                        s_, sk = SS.next()
                        C.op('act', lambda e: e.activation(out=junk[:], in_=xt, func=AF.Square, accum_out=s_[:, 0:1]),
                             reads=[xk], writes=[tag + 'junk', sk])
                        C.op('dve', lambda e: e.tensor_scalar(out=s_[:, 1:2], in0=s_[:, 0:1], scalar1=1.0 / DM, scalar2=EPS,
                                                              op0=ALU.mult, op1=ALU.add), reads=[sk], writes=[sk])
                        C.op('act', lambda e: e.sqrt(out=s_[:, 1:2], in_=s_[:, 1:2]), reads=[sk], writes=[sk])
                        C.op('dve', lambda e: e.reciprocal(out=s_[:, 1:2], in_=s_[:, 1:2]), reads=[sk], writes=[sk])
                        xb, xbk = XN.next()
                        C.op('dve', lambda e: e.scalar_tensor_tensor(out=xb, in0=xt, scalar=s_[:, 1:2], in1=gb[:],
                                                                     op0=ALU.mult, op1=ALU.mult),
                             reads=[xk, sk, tag + 'gb'], writes=[xbk])
                        nb8 = min(8, KT)
                        for k8 in range(KT // nb8):
                            pt, pk = PSB.next()
                            for j in range(nb8):
                                kt = k8 * nb8 + j
                                C.op('pe', lambda e: e.transpose(out=pt[:, j * P:(j + 1) * P], in_=xb[:, kt * P:(kt + 1) * P],
                                                                 identity=ident[:]),
                                     reads=[xbk, 'ident'], writes=[pk], inc=(j == nb8 - 1))
                            dst = sg[:, k8 * nb8:(k8 + 1) * nb8, sub * P:(sub + 1) * P]
                            src = pt[:, 0:nb8 * P].rearrange("p (k t) -> p k t", k=nb8)
                            eng = 'act' if (k8 % 2 == 0) else 'dve'
                            if eng == 'act':
                                C.op('act', lambda e: e.copy(out=dst, in_=src), reads=[pk], writes=[sgk])
                            else:
                                C.op('dve', lambda e: e.tensor_copy(out=dst, in_=src), reads=[pk], writes=[sgk])
                    half = KT // 2
                    C.dma('sp', dst_d[ch][:, 0:half, :], sg[:, 0:half, :], reads=[sgk])
                    C.dma('sp', dst_d[ch][:, half:KT, :], sg[:, half:KT, :], reads=[sgk])
            C.barrier()

        def gemm(tag, form, colblocks, tok0, ntok, TC, epilogue, wslots=2, aslots=2, resident=False, pre=None, lq='act', astep=32):
            ktot = max(sum(g[1] for g in blk) for blk in colblocks)
            wmax = max(sum(pc[2] for pc in g[2]) for blk in colblocks for g in blk)
            nch = ntok // TC
            if resident:
                aslots = nch
            with ExitStack() as st:
                wb = sb(st, tag + "w", [P, wslots, ktot, wmax], BF16)
                ab = sb(st, tag + "a", [P, aslots, ktot, TC], BF16)
                WB = Ring(tag + "w", wb, wslots); AB = Ring(tag + "a", ab, aslots)
                work = [(bi, ci) for bi in range(len(colblocks)) for ci in range(nch)]
                wstate = {}
                astate = {}

                def load_w(bi):
                    if bi >= len(colblocks) or bi in wstate:
                        return
                    blk = colblocks[bi]
                    wt, wk = WB.next()
                    wkeys = []
                    k0 = 0
                    for (act_d, ktn, pieces) in blk:
                        c0 = 0
                        for (w_d, col0, width) in pieces:
                            wv = w_d.rearrange("(k p) n -> p k n", p=P)
                            step = 8
                            for ks in range(0, ktn, step):
                                ke = min(ktn, ks + step)
                                wkeys.append((wk, k0 + ks, c0))
                                C.dma('pool', wt[:, k0 + ks:k0 + ke, c0:c0 + width], wv[:, ks:ke, col0:col0 + width],
                                      writes=[wkeys[-1]])
                            c0 += width
                        k0 += ktn
                    wstate[bi] = (wt, wkeys)

                def load_a(idx):
                    if idx >= len(work) or idx in astate:
                        return
                    bi, ci = work[idx]
                    blk = colblocks[bi]
                    t0 = tok0 + ci * TC
                    if resident and bi > 0:
                        at, akeys, _ = astate[ci]
                    else:
                        at, ak = AB.next()
                        akeys = []
                        k0 = 0
                        for (act_d, ktn, pieces) in blk:
                            step = 16
                            if callable(act_d):
                                step = astep
                            else:
                                av = act_d.rearrange("(k p) t -> p k t", p=P)
                            for ks in range(0, ktn, step):
                                ke = min(ktn, ks + step)
                                akeys.append((ak, k0 + ks))
                                src = act_d(ks, ke, t0) if callable(act_d) else av[:, ks:ke, t0:t0 + TC]
                                C.dma(lq, at[:, k0 + ks:k0 + ke, :], src, writes=[akeys[-1]])
                            k0 += ktn
                    pr = pre(bi, ci, t0) if pre is not None else None
                    astate[idx] = (at, akeys, pr)

                load_w(0)
                load_a(0)
                for idx, (bi, ci) in enumerate(work):
                    blk = colblocks[bi]
                    t0 = tok0 + ci * TC
                    if ci == 0:
                        load_w(bi)
                        if wslots >= 2:
                            load_w(bi + 1)
                    load_a(idx + 1)
                    wt, wkeys = wstate[bi]
                    at, akeys, pr = astate[idx]
                    pts = []
                    if form == 'b':
                        k0 = 0
                        for (act_d, ktn, pieces) in blk:
                            wsum = sum(pc[2] for pc in pieces)
                            for ct in range(wsum // P):
                                pt, pk = PSF.next()
                                for kt in range(ktn):
                                    C.op('pe', lambda e: e.matmul(pt[:, 0:TC], lhsT=wt[:, k0 + kt, ct * P:(ct + 1) * P],
                                                                  rhs=at[:, k0 + kt, :], start=(kt == 0), stop=(kt == ktn - 1)),
                                         reads=wkeys + akeys, writes=[pk], inc=(kt == ktn - 1))
                                pts.append((pt, pk))
                            k0 += ktn
                    else:
                        nw = sum(pc[2] for pc in blk[0][2])
                        for sub in range(TC // P):
                            pt, pk = PSF.next()
                            for kt in range(ktot):
                                C.op('pe', lambda e: e.matmul(pt[:, 0:nw], lhsT=at[:, kt, sub * P:(sub + 1) * P],
                                                              rhs=wt[:, kt, 0:nw], start=(kt == 0), stop=(kt == ktot - 1)),
                                     reads=wkeys + akeys, writes=[pk], inc=(kt == ktot - 1))
                            pts.append((pt, pk))
                    if pre is not None:
                        epilogue(bi, ci, t0, pts, pr)
                    else:
                        epilogue(bi, ci, t0, pts)
                    if not resident:
                        del astate[idx]
                    if ci == nch - 1:
                        del wstate[bi]
            C.barrier()

        SCALE = 128.0 ** -0.5

        def hT_src(ks, ke, t0):
            return hT_d[t0 // 512][:, ks:ke, :]

        def h2T_src(ks, ke, t0):
            return h2T_d[t0 // 512][:, ks:ke, :]

        def actT_src(ks, ke, t0):
            return actT_d[t0 // 256][:, ks:ke, :]
        stages = cfg.get('stages', 99)

        def evac_copy(i, dst, src, rk, wk):
            if i % 2 == 0:
                C.op('act', lambda e: e.copy(out=dst, in_=src), reads=rk, writes=wk)
            else:
                C.op('dve', lambda e: e.tensor_copy(out=dst, in_=src), reads=rk, writes=wk)

        def normrope(T, xf, xfk, nh, gb_ap, gbk, cs, csk, outb, outk):
            sq, sqk = T['sq'].next(); stt, stk = T['st'].next(); tr, trk = T['tr'].next()
            W = nh * 128
            x3 = xf[:, 0:W].rearrange("p (h d) -> p h d", h=nh)
            o3 = outb[:, 0:W].rearrange("p (h d) -> p h d", h=nh)
            C.op('act', lambda e: e.activation(out=sq[:, 0:W], in_=xf[:, 0:W], func=AF.Square), reads=[xfk], writes=[sqk])
            C.op('dve', lambda e: e.tensor_reduce(out=stt[:, 0:nh], in_=sq[:, 0:W].rearrange("p (h d) -> p h d", h=nh),
                                                  axis=AX.X, op=ALU.add), reads=[sqk], writes=[stk])
            C.op('dve', lambda e: e.tensor_scalar(out=stt[:, 0:nh], in0=stt[:, 0:nh], scalar1=1.0 / 128, scalar2=EPS,
                                                  op0=ALU.mult, op1=ALU.add), reads=[stk], writes=[stk])
            C.op('act', lambda e: e.sqrt(out=stt[:, 0:nh], in_=stt[:, 0:nh]), reads=[stk], writes=[stk])
            C.op('dve', lambda e: e.reciprocal(out=stt[:, 0:nh], in_=stt[:, 0:nh]), reads=[stk], writes=[stk])
            C.op('dve', lambda e: e.tensor_tensor(out=x3, in0=x3, in1=stt[:, 0:nh].unsqueeze(2).to_broadcast([P, nh, 128]),
                                                  op=ALU.mult), reads=[xfk, stk], writes=[xfk])
            C.op('dve', lambda e: e.tensor_tensor(out=x3, in0=x3, in1=gb_ap.unsqueeze(1).to_broadcast([P, nh, 128]),
                                                  op=ALU.mult), reads=[xfk, gbk], writes=[xfk])
            C.op('act', lambda e: e.copy(out=outb[:, 0:W], in_=xf[:, 0:W]), reads=[xfk], writes=[outk])
            x1 = x3[:, :, 0:16]; x2 = x3[:, :, 16:32]
            cb = cs[:, 0:16].unsqueeze(1).to_broadcast([P, nh, 16]); sbb = cs[:, 16:32].unsqueeze(1).to_broadcast([P, nh, 16])
            C.op('dve', lambda e: e.tensor_tensor(out=tr[:, 0, 0:nh, :], in0=x1, in1=cb, op=ALU.mult), reads=[xfk, csk], writes=[trk])
            C.op('dve', lambda e: e.tensor_tensor(out=tr[:, 1, 0:nh, :], in0=x2, in1=sbb, op=ALU.mult), reads=[xfk, csk], writes=[trk])
            C.op('dve', lambda e: e.tensor_tensor(out=tr[:, 2, 0:nh, :], in0=x2, in1=cb, op=ALU.mult), reads=[xfk, csk], writes=[trk])
            C.op('dve', lambda e: e.tensor_tensor(out=tr[:, 3, 0:nh, :], in0=x1, in1=sbb, op=ALU.mult), reads=[xfk, csk], writes=[trk])
            C.op('dve', lambda e: e.tensor_tensor(out=o3[:, :, 0:16], in0=tr[:, 0, 0:nh, :], in1=tr[:, 1, 0:nh, :], op=ALU.subtract),
                 reads=[trk], writes=[outk])
            C.op('dve', lambda e: e.tensor_tensor(out=o3[:, :, 16:32], in0=tr[:, 2, 0:nh, :], in1=tr[:, 3, 0:nh, :], op=ALU.add),
                 reads=[trk], writes=[outk])

        def nr_temps(st, tag):
            sq = sb(st, tag + "sq", [P, 2, 512], F32); stt = sb(st, tag + "st", [P, 2, 8], F32)
            tr = sb(st, tag + "tr", [P, 2, 4, 4, 16], F32)
            return dict(sq=Ring(tag + "sq", sq, 2), st=Ring(tag + "st", stt, 2), tr=Ring(tag + "tr", tr, 2))

        phase_norm("n1", x_ctx, TCTX, norm1_g, hT_d)

        if stages >= 1:
            with ExitStack() as st:
                stg = sb(st, "g1a_stg", [P, 2, 4, 512], BF16); STG = Ring("g1a_stg", stg, 2)

                def epi(bi, ci, t0, pts):
                    sg, sgk = STG.next()
                    for j, (pt, pk) in enumerate(pts):
                        evac_copy(j, sg[:, j, :], pt[:, 0:512], [pk], [sgk])
                    dst = (kcmpT_d if bi == 0 else vcmpT_d)[:, :, t0:t0 + 512].rearrange("g p t -> p g t")
                    C.dma('sp', dst, sg, reads=[sgk])
                gemm("g1a", 'b', [[(hT_src, KT, [(w_in, 2048, 512)])], [(hT_src, KT, [(w_in, 2560, 512)])]], 0, TCTX, 512, epi)

        if stages >= 2:
            with ExitStack() as st:
                T = nr_temps(st, "g1b")
                gkb = sb(st, "g1b_gk", [P, 2, 128], F32)
                xf = sb(st, "g1b_xf", [P, 2, 512], F32); XF = Ring("g1b_xf", xf, 2)
                kb = sb(st, "g1b_kb", [P, 2, 512], BF16); KB = Ring("g1b_kb", kb, 2)
                cst = sb(st, "g1b_cs", [P, TCTX // P, 32], F32)
                C.dma('sp', cst, cs_tok.rearrange("(t p) c -> p t c", p=P), writes=['g1b_cs'])
                stg = sb(st, "g1b_stg", [P, 2, 4, 512], BF16); STG = Ring("g1b_stg", stg, 2)
                C.dma('sp', gkb[:, 0, :], k_norm_g[1].partition_broadcast(P), writes=['g1b_gk'])
                C.dma('sp', gkb[:, 1, :], k_norm_g[2].partition_broadcast(P), writes=['g1b_gk'])

                epi_base = [0]

                def epi(bi, ci, t0, pts):
                    bi = bi + epi_base[0]
                    sg, sgk = STG.next()
                    if bi in (1, 3):
                        for j, (pt, pk) in enumerate(pts):
                            evac_copy(j, sg[:, j, :], pt[:, 0:512], [pk], [sgk])
                        dst = (vslc_d if bi == 1 else vwin_d)[t0:t0 + 512, :].rearrange("(s p) c -> p s c", p=P)
                        C.dma('sp', dst, sg, reads=[sgk])
                    else:
                        for j, (pt, pk) in enumerate(pts):
                            x_, xk = XF.next()
                            C.op('act', lambda e: e.copy(out=x_, in_=pt[:, 0:512]), reads=[pk], writes=[xk])
                            c_, ck = cst[:, t0 // P + j, :], 'g1b_cs'
                            k_, kk = KB.next()
                            normrope(T, x_, xk, 4, gkb[:, 0 if bi == 0 else 1, :], 'g1b_gk', c_, ck, k_, kk)
                            p2, p2k = PSB.next()
                            for g in range(4):
                                C.op('pe', lambda e: e.transpose(out=p2[:, g * P:(g + 1) * P], in_=k_[:, g * P:(g + 1) * P],
                                                                 identity=ident[:]), reads=[kk, 'ident'], writes=[p2k], inc=(g == 3))
                            evac_copy(j, sg[:, :, j * P:(j + 1) * P], p2[:, 0:512].rearrange("p (g t) -> p g t", g=4), [p2k], [sgk])
                        dst = (kslcT_d if bi == 0 else kwinT_d)[:, :, t0:t0 + 512].rearrange("g p t -> p g t")
                        C.dma('sp', dst, sg, reads=[sgk])
                gemm("g1b", 'a', [[(hT_src, KT, [(w_in, 3072 + 512 * b_, 512)])] for b_ in range(2)], 0, TCTX, 512, epi)
                epi_base[0] = 2
                gemm("g1bw", 'a', [[(hT_src, KT, [(w_in, 3072 + 512 * b_, 512)])] for b_ in range(2, 4)], OWN0 - 512, TOWN + 512, 512, epi)

        if stages >= 3:
            with ExitStack() as st:
                cw = sb(st, "l_cw", [P, 4, NLT], F32); cbias = sb(st, "l_cb", [P, NLT], F32)
                bg = sb(st, "l_bg", [P, 2, NLT], F32); lam = sb(st, "l_lam", [P, NLT], F32)
                sc = sb(st, "l_sc", [P, NLT], F32); vch = sb(st, "l_vch", [P, NCHC], F32)
                wg = sb(st, "l_wg", [P, 2, NLT, 128], BF16)
                xprev = sb(st, "l_xprev", [P, NLT, 4], F32); hprev = sb(st, "l_hprev", [P, NLT], F32)
                xbuf = sb(st, "l_xbuf", [P, 4, 516], F32); XB = Ring("l_xbuf", xbuf, 4)
                xc = sb(st, "l_xc", [P, 4, 512], F32); XC = Ring("l_xc", xc, 4)
                xcb = sb(st, "l_xcb", [P, 4, 512], BF16); XCB = Ring("l_xcb", xcb, 4)
                rr = sb(st, "l_r", [P, 4, 512], F32); RR = Ring("l_r", rr, 4)
                ii = sb(st, "l_i", [P, 4, 512], F32); II = Ring("l_i", ii, 4)
                aa = sb(st, "l_a", [P, 4, 512], F32); AA = Ring("l_a", aa, 4)
                mm = sb(st, "l_m", [P, 4, 512], F32); MM = Ring("l_m", mm, 4)
                hh_ = sb(st, "l_h", [P, 4, 512], F32); HH = Ring("l_h", hh_, 4)
                C.dma('sp', cw, conv_w.rearrange("k (t p) -> p k t", p=P), writes=['l_par'], allow_slow_non_contiguous=True)
                C.dma('sp', cbias, conv_b.rearrange("(t p) -> p t", p=P), writes=['l_par'], allow_slow_non_contiguous=True)
                C.dma('sp', bg, lru_bg.rearrange("k (t p) -> p k t", p=P), writes=['l_par'], allow_slow_non_contiguous=True)
                C.dma('sp', lam, lru_lam.rearrange("(t p) -> p t", p=P), writes=['l_par'], allow_slow_non_contiguous=True)
                C.dma('sp', vch, t_vchunk[:, :], writes=['l_par'])
                C.dma('pool', wg, lru_wg.rearrange("k t c d -> c k t d"), writes=['l_wg'])
                C.op('act', lambda e: e.activation(out=sc, in_=lam, func=AF.Exp, scale=-1.0), reads=['l_par'], writes=['l_sc'])
                C.op('act', lambda e: e.activation(out=sc, in_=sc, func=AF.Ln, bias=1.0), reads=['l_sc'], writes=['l_sc'])
                C.op('dve', lambda e: e.tensor_scalar(out=sc, in0=sc, scalar1=-8.0, scalar2=None, op0=ALU.mult), reads=['l_sc'], writes=['l_sc'])
                C.op('dve', lambda e: e.memset(xprev, 0.0), writes=[('l_xprev', c_) for c_ in range(NLT)])
                C.op('dve', lambda e: e.memset(hprev, 0.0), writes=[('l_hprev', c_) for c_ in range(NLT)])
                BW = min(512, LW)

                def epi(bi, ci, t0, pts):
                    n = len(pts)
                    J = range(n)
                    cts = [bi * (BW // P) + j for j in J]
                    xbs = [XB.next() for j in J]; xcs = [XC.next() for j in J]; xcbs = [XCB.next() for j in J]
                    rs_ = [RR.next() for j in J]; is_ = [II.next() for j in J]; as_ = [AA.next() for j in J]
                    ms_ = [MM.next() for j in J]; hs_ = [HH.next() for j in J]
                    for j in J:
                        C.op('act', lambda e: e.copy(out=xbs[j][0][:, 3:515], in_=pts[j][0][:, 0:512]), reads=[pts[j][1]], writes=[xbs[j][1]])
                    for j in J:
                        C.op('dve', lambda e: e.tensor_copy(out=xbs[j][0][:, 0:3], in_=xprev[:, cts[j], 0:3]), reads=[('l_xprev', cts[j]), xbs[j][1]], writes=[xbs[j][1]])
                    for j in J:
                        C.op('dve', lambda e: e.tensor_scalar(out=xcs[j][0], in0=xbs[j][0][:, 3:515], scalar1=cw[:, 3, cts[j]:cts[j] + 1],
                                                              scalar2=cbias[:, cts[j]:cts[j] + 1], op0=ALU.mult, op1=ALU.add),
                             reads=[xbs[j][1], 'l_par'], writes=[xcs[j][1]])
                    for k in range(3):
                        for j in J:
                            C.op('dve', lambda e: e.scalar_tensor_tensor(out=xcs[j][0], in0=xbs[j][0][:, k:k + 512], scalar=cw[:, k, cts[j]:cts[j] + 1],
                                                                         in1=xcs[j][0], op0=ALU.mult, op1=ALU.add),
                                 reads=[xbs[j][1], xcs[j][1], 'l_par'], writes=[xcs[j][1]])
                    for j in J:
                        C.op('dve', lambda e: e.tensor_copy(out=xprev[:, cts[j], 0:3], in_=xbs[j][0][:, 512:515]), reads=[xbs[j][1]], writes=[('l_xprev', cts[j])])
                    for j in J:
                        C.op('act', lambda e: e.copy(out=xcbs[j][0], in_=xcs[j][0]), reads=[xcs[j][1]], writes=[xcbs[j][1]])
                    for j0 in range(0, n, 2):
                        JJ = range(j0, min(n, j0 + 2))
                        prs = {}; pis = {}
                        for j in JJ:
                            prs[j] = PSF.next(); pis[j] = PSF.next()
                            C.op('pe', lambda e: e.matmul(prs[j][0][:, 0:512], lhsT=wg[:, 0, cts[j], :], rhs=xcbs[j][0], start=True, stop=True),
                                 reads=['l_wg', xcbs[j][1]], writes=[prs[j][1]])
                            C.op('pe', lambda e: e.matmul(pis[j][0][:, 0:512], lhsT=wg[:, 1, cts[j], :], rhs=xcbs[j][0], start=True, stop=True),
                                 reads=['l_wg', xcbs[j][1]], writes=[pis[j][1]])
                        for j in JJ:
                            C.op('act', lambda e: e.activation(out=rs_[j][0], in_=prs[j][0][:, 0:512], func=AF.Sigmoid, bias=bg[:, 0, cts[j]:cts[j] + 1]),
                                 reads=[prs[j][1], 'l_par'], writes=[rs_[j][1]])
                        for j in JJ:
                            C.op('act', lambda e: e.activation(out=is_[j][0], in_=pis[j][0][:, 0:512], func=AF.Sigmoid, bias=bg[:, 1, cts[j]:cts[j] + 1]),
                                 reads=[pis[j][1], 'l_par'], writes=[is_[j][1]])
                    for j in J:
                        C.op('act', lambda e: e.activation(out=as_[j][0], in_=rs_[j][0], func=AF.Exp, scale=sc[:, cts[j]:cts[j] + 1]),
                             reads=[rs_[j][1], 'l_sc'], writes=[as_[j][1]])
                    for j in J:
                        C.op('dve', lambda e: e.tensor_tensor(out=ms_[j][0], in0=as_[j][0], in1=as_[j][0], op=ALU.mult), reads=[as_[j][1]], writes=[ms_[j][1]])
                    for j in J:
                        C.op('act', lambda e: e.activation(out=ms_[j][0], in_=ms_[j][0], func=AF.Sqrt, scale=-1.0, bias=1.0), reads=[ms_[j][1]], writes=[ms_[j][1]])
                    for j in J:
                        C.op('dve', lambda e: e.tensor_tensor(out=ms_[j][0], in0=ms_[j][0], in1=is_[j][0], op=ALU.mult), reads=[ms_[j][1], is_[j][1]], writes=[ms_[j][1]])
                    for j in J:
                        C.op('dve', lambda e: e.scalar_tensor_tensor(out=ms_[j][0], in0=ms_[j][0], scalar=vch[:, ci:ci + 1], in1=xcs[j][0],
                                                                     op0=ALU.mult, op1=ALU.mult), reads=[ms_[j][1], xcs[j][1], 'l_par'], writes=[ms_[j][1]])
                    for j in J:
                        C.op('dve', lambda e: e.tensor_tensor_scan(out=hs_[j][0], data0=as_[j][0], data1=ms_[j][0], initial=hprev[:, cts[j]:cts[j] + 1],
                                                                   op0=ALU.mult, op1=ALU.add), reads=[as_[j][1], ms_[j][1], ('l_hprev', cts[j])], writes=[hs_[j][1]])
                    for j in J:
                        C.op('dve', lambda e: e.tensor_copy(out=hprev[:, cts[j]:cts[j] + 1], in_=hs_[j][0][:, 511:512]), reads=[hs_[j][1]], writes=[('l_hprev', cts[j])])
                    if t0 >= OWN0:
                        for j in J:
                            C.dma('sp', hlru_d[cts[j] * P:(cts[j] + 1) * P, t0 - OWN0:t0 - OWN0 + 512], hs_[j][0], reads=[hs_[j][1]])
                gemm("g1c", 'b', [[(hT_src, KT, [(w_in, C_LX + BW * b_, BW)])] for b_ in range(LW // BW)], 0, TCTX, 512, epi)

        if stages >= 4:
            with ExitStack() as st:
                T = nr_temps(st, "g2a")
                gq = sb(st, "g2a_gq", [P, 128], F32)
                xf = sb(st, "g2a_xf", [P, 2, 512], F32); XF = Ring("g2a_xf", xf, 2)
                kb = sb(st, "g2a_kb", [P, 2, 512], BF16); KB = Ring("g2a_kb", kb, 2)
                cst = sb(st, "g2a_cs", [P, TOWN // P, 32], F32)
                C.dma('sp', cst, cs_tok[OWN0:TCTX, :].rearrange("(t p) c -> p t c", p=P), writes=['g2a_cs'])
                stg = sb(st, "g2a_stg", [P, 2, 4, 512], BF16); STG = Ring("g2a_stg", stg, 2)
                C.dma('sp', gq, q_norm_g.partition_broadcast(P), writes=['g2a_gq'])

                def epi(bi, ci, t0, pts):
                    if bi == 4:
                        for j, (pt, pk) in enumerate(pts):
                            tile_i = (t0 - OWN0) // P + j
                            C.op('act', lambda e: e.activation(out=gate_sb[:, tile_i, :], in_=pt[:, 0:48], func=AF.Sigmoid),
                                 reads=[pk], writes=['gate_sb'])
                        return
                    sg, sgk = STG.next()
                    for j, (pt, pk) in enumerate(pts):
                        x_, xk = XF.next()
                        C.op('act', lambda e: e.copy(out=x_, in_=pt[:, 0:512]), reads=[pk], writes=[xk])
                        c_, ck = cst[:, (t0 - OWN0) // P + j, :], 'g2a_cs'
                        k_, kk = KB.next()
                        normrope(T, x_, xk, 4, gq, 'g2a_gq', c_, ck, k_, kk)
                        p2, p2k = PSB.next()
                        for g in range(4):
                            C.op('pe', lambda e: e.transpose(out=p2[:, g * P:(g + 1) * P], in_=k_[:, g * P:(g + 1) * P],
                                                             identity=ident[:]), reads=[kk, 'ident'], writes=[p2k], inc=(g == 3))
                        evac_copy(j, sg[:, :, j * P:(j + 1) * P], p2[:, 0:512].rearrange("p (g t) -> p g t", g=4), [p2k], [sgk])
                    dst = qT_d[4 * bi:4 * bi + 4, :, t0 - OWN0:t0 - OWN0 + 512].rearrange("g p t -> p g t")
                    C.dma('sp', dst, sg, reads=[sgk])
                blocks = [[(hT_src, KT, [(w_in, 512 * b_, 512)])] for b_ in range(4)] + [[(hT_src, KT, [(w_in, C_G, 48)])]]
                gemm("g2a", 'a', blocks, OWN0, TOWN, 512, epi)

        if stages >= 5:
            with ExitStack() as st:
                yx = sb(st, "g2b_y", [P, 2, 512], F32); YX = Ring("g2b_y", yx, 2)
                uu = sb(st, "g2b_u", [P, 2, 512], F32); UU = Ring("g2b_u", uu, 2)
                hl = sb(st, "g2b_h", [P, 8, 512], F32); HL = Ring("g2b_h", hl, 8)
                ob = sb(st, "g2b_o", [P, 3, 512], BF16); OB = Ring("g2b_o", ob, 3)
                BW = min(256, LW)
                nlb = LW // BW

                def pre(bi, ci, t0):
                    if bi >= nlb:
                        return None
                    to = t0 - OWN0
                    res = []
                    for j in range(BW // P):
                        ct = bi * (BW // P) + j
                        h_, hk = HL.next()
                        C.dma('act', h_, hlru_d[ct * P:(ct + 1) * P, to:to + 512], writes=[hk])
                        res.append((h_, hk))
                    return res

                def epi(bi, ci, t0, pts, pr):
                    to = t0 - OWN0
                    for j, (pt, pk) in enumerate(pts):
                        o_, ok = OB.next()
                        if bi < nlb:
                            ct = bi * (BW // P) + j
                            y_, yk = YX.next(); u_, uk = UU.next(); h_, hk = pr[j]
                            C.op('act', lambda e: e.copy(out=y_, in_=pt[:, 0:512]), reads=[pk], writes=[yk])
                            C.op('dve', lambda e: e.tensor_tensor(out=u_, in0=y_, in1=y_, op=ALU.mult), reads=[yk], writes=[uk])
                            C.op('dve', lambda e: e.tensor_scalar(out=u_, in0=u_, scalar1=0.044715, scalar2=1.0, op0=ALU.mult, op1=ALU.add),
                                 reads=[uk], writes=[uk])
                            C.op('dve', lambda e: e.tensor_tensor(out=u_, in0=u_, in1=y_, op=ALU.mult), reads=[uk, yk], writes=[uk])
                            C.op('act', lambda e: e.activation(out=u_, in_=u_, func=AF.Sigmoid, scale=1.5957691216057308), reads=[uk], writes=[uk])
                            C.op('dve', lambda e: e.tensor_tensor(out=u_, in0=u_, in1=y_, op=ALU.mult), reads=[uk, yk], writes=[uk])
                            C.op('dve', lambda e: e.tensor_tensor(out=o_, in0=u_, in1=h_, op=ALU.mult), reads=[uk, hk], writes=[ok])
                            C.dma('sp', lruoT_d[ct * P:(ct + 1) * P, to:to + 512], o_, reads=[ok])
                        else:
                            row = (bi - nlb) * 256 + j * P
                            C.op('act', lambda e: e.activation(out=o_, in_=pt[:, 0:512], func=AF.Sigmoid), reads=[pk], writes=[ok])
                            C.dma('sp', gT_d[row:row + P, to:to + 512], o_, reads=[ok])
                blocks = [[(hT_src, KT, [(w_in, C_LY + BW * b_, BW)])] for b_ in range(nlb)]
                blocks += [[(hT_src, KT, [(w_in, C_MG + 256 * b_, 256)])] for b_ in range(2 * DM // 256)]
                gemm("g2b", 'b', blocks, OWN0, TOWN, 512, epi, resident=True, pre=pre)

        AW = 129 + NB
        if stages >= 6:
            kcT = sb(es, "kcT", [P, 4, NCP], BF16)
            vca = sb(es, "vca", [P, 4, NCT, AW], BF16)
            with ExitStack() as st:
                T = nr_temps(st, "cp")
                w1 = sb(st, "cp_w1", [P, 2, 32, 256], BF16); posT = sb(st, "cp_pos", [P, 2, 32], BF16)
                b1 = sb(st, "cp_b1", [P, 2, 2], F32); w2 = sb(st, "cp_w2", [P, 2, 2, 128], BF16)
                b2b = sb(st, "cp_b2", [P, 2, 128], F32); gk = sb(st, "cp_gk", [P, 128], F32)
                csc = sb(st, "cp_cs", [P, NCT, 32], F32)
                src = sb(st, "cp_src", [P, 2, TCTX], BF16); SRC = Ring("cp_src", src, 2)
                hx = sb(st, "cp_hx", [P, 2, 512], F32); HX = Ring("cp_hx", hx, 2)
                hu = sb(st, "cp_hu", [P, 2, 512], F32); HU = Ring("cp_hu", hu, 2)
                hid = sb(st, "cp_hid", [P, 2, 2, NCP], BF16); HID = Ring("cp_hid", hid, 2)
                hb = sb(st, "cp_hb", [P, 2, 2], F32)
                xf = sb(st, "cp_xf", [P, 2, 512], F32); XF = Ring("cp_xf", xf, 2)
                kb = sb(st, "cp_kb", [P, 2, 512], BF16); KB = Ring("cp_kb", kb, 2)
                for kv in range(2):
                    C.dma('pool', w1[:, kv], cmp_w1[kv].rearrange("(l d) h -> d l h", d=128), writes=['cp_w1'])
                    C.dma('pool', posT[:, kv], cmp_pos[kv].rearrange("l d -> d l"), writes=['cp_pos'], allow_slow_non_contiguous=True)
                    C.dma('sp', b1[:, kv], cmp_b1[kv].rearrange("(t p) -> p t", p=P), writes=['cp_b1'], allow_slow_non_contiguous=True)
                    C.dma('pool', w2[:, kv], cmp_w2[kv].rearrange("(t p) d -> p t d", p=P), writes=['cp_w2'])
                    C.dma('sp', b2b[:, kv], cmp_b2[kv].partition_broadcast(P), writes=['cp_b2'])
                C.dma('sp', gk, k_norm_g[0].partition_broadcast(P), writes=['cp_gk'])
                C.dma('sp', csc, cs_cmp.rearrange("(t p) c -> p t c", p=P), writes=['cp_cs'])
                for g in range(4):
                    C.dma('pool', vca[:, g, :, 129:129 + NB], t_overlap.rearrange("(t p) n -> p t n", p=P), writes=['vca'])
                C.op('dve', lambda e: e.memset(vca[:, :, :, 128:129], 1.0), reads=[], writes=['vca'])
                C.op('dve', lambda e: e.memset(hid, 0.0), writes=[('cp_hid', 0), ('cp_hid', 1)])
                for kv in range(2):
                    for ht in range(2):
                        pt, pk = PSF.next()
                        for l in range(32):
                            C.op('pe', lambda e: e.matmul(pt[:, 0:1], lhsT=w1[:, kv, l, ht * P:(ht + 1) * P], rhs=posT[:, kv, l:l + 1],
                                                          start=(l == 0), stop=(l == 31)),
                                 reads=['cp_w1', 'cp_pos'], writes=[pk], inc=(l == 31))
                        C.op('dve', lambda e: e.tensor_tensor(out=hb[:, kv, ht:ht + 1], in0=pt[:, 0:1], in1=b1[:, kv, ht:ht + 1], op=ALU.add),
                             reads=[pk, 'cp_b1'], writes=['cp_hb'])
                for g in range(4):
                    for kv in range(2):
                        s_, sk = SRC.next()
                        C.dma('sp', s_, (kcmpT_d if kv == 0 else vcmpT_d)[g], writes=[sk])
                        sv = s_.rearrange("p (c s) -> p c s", s=16)
                        hd, hdk = HID.next()
                        for ht in range(2):
                            pt, pk = PSF.next()
                            for l in range(32):
                                rhs = sv[:, 0:NC, l] if l < 16 else sv[:, 1:NC + 1, l - 16]
                                C.op('pe', lambda e: e.matmul(pt[:, 0:NC], lhsT=w1[:, kv, l, ht * P:(ht + 1) * P], rhs=rhs,
                                                              start=(l == 0), stop=(l == 31)),
                                     reads=['cp_w1', sk], writes=[pk], inc=(l == 31))
                            x_, xk = HX.next(); u_, uk = HU.next()
                            C.op('act', lambda e: e.activation(out=x_[:, 0:NC], in_=pt[:, 0:NC], func=AF.Identity, bias=hb[:, kv, ht:ht + 1]),
                                 reads=[pk, 'cp_hb'], writes=[xk])
                            C.op('dve', lambda e: e.tensor_tensor(out=u_[:, 0:NC], in0=x_[:, 0:NC], in1=x_[:, 0:NC], op=ALU.mult), reads=[xk], writes=[uk])
                            C.op('dve', lambda e: e.tensor_scalar(out=u_[:, 0:NC], in0=u_[:, 0:NC], scalar1=0.044715, scalar2=1.0,
                                                                  op0=ALU.mult, op1=ALU.add), reads=[uk], writes=[uk])
                            C.op('dve', lambda e: e.tensor_tensor(out=u_[:, 0:NC], in0=u_[:, 0:NC], in1=x_[:, 0:NC], op=ALU.mult), reads=[uk, xk], writes=[uk])
                            C.op('act', lambda e: e.activation(out=u_[:, 0:NC], in_=u_[:, 0:NC], func=AF.Sigmoid, scale=1.5957691216057308),
                                 reads=[uk], writes=[uk])
                            C.op('dve', lambda e: e.tensor_tensor(out=hd[:, ht, 0:NC], in0=u_[:, 0:NC], in1=x_[:, 0:NC], op=ALU.mult),
                                 reads=[uk, xk], writes=[hdk])
                        for ct in range(NCT):
                            pt, pk = PSF.next()
                            for ht in range(2):
                                C.op('pe', lambda e: e.matmul(pt[:, 0:128], lhsT=hd[:, ht, ct * P:(ct + 1) * P], rhs=w2[:, kv, ht, :],
                                                              start=(ht == 0), stop=(ht == 1)), reads=[hdk, 'cp_w2'], writes=[pk], inc=(ht == 1))
                            if kv == 1:
                                C.op('dve', lambda e: e.tensor_tensor(out=vca[:, g, ct, 0:128], in0=pt[:, 0:128], in1=b2b[:, 1, :], op=ALU.add),
                                     reads=[pk, 'cp_b2'], writes=['vca'])
                            else:
                                x_, xk = XF.next(); k_, kk = KB.next()
                                C.op('dve', lambda e: e.tensor_tensor(out=x_[:, 0:128], in0=pt[:, 0:128], in1=b2b[:, 0, :], op=ALU.add),
                                     reads=[pk, 'cp_b2'], writes=[xk])
                                normrope(T, x_, xk, 1, gk, 'cp_gk', csc[:, ct, :], 'cp_cs', k_, kk)
                                p2, p2k = PSB.next()
                                C.op('pe', lambda e: e.transpose(out=p2[:, 0:P], in_=k_[:, 0:P], identity=ident[:]), reads=[kk, 'ident'], writes=[p2k])
                                C.op('act', lambda e: e.copy(out=kcT[:, g, ct * P:(ct + 1) * P], in_=p2[:, 0:P]), reads=[p2k], writes=['kcT'])
            C.barrier()

        if stages >= 7:
            with ExitStack() as st:
                Eb = sb(st, "at_E", [P, TCTX], BF16)
                cmask = sb(st, "at_cm", [P, NCHO, NCT, 512], BF16)
                sdiag = sb(st, "at_sd", [P, 4, 512], BF16); wmask = sb(st, "at_wm", [P, 2, 8, 512], BF16)
                kS = sb(st, "at_kS", [P, TCTX], BF16); kW = sb(st, "at_kW", [P, TCTX], BF16)
                NKT = TCTX // P
                vS = sb(st, "at_vS", [P, NKT, 129], BF16); vW = sb(st, "at_vW", [P, NKT, 129], BF16)
                qt = sb(st, "at_q", [P, 2, 4, 512], BF16); QT = Ring("at_q", qt, 2)
                pT = sb(st, "at_p", [P, 3, 512], BF16); PT = Ring("at_p", pT, 3)
                nsT = sb(st, "at_ns", [P, 512], BF16)
                tA = sb(st, "at_tA", [P, 2, NB], F32); TA = Ring("at_tA", tA, 2)
                tB = sb(st, "at_tB", [P, 2, NB], F32); TB = Ring("at_tB", tB, 2)
                imp = sb(st, "at_imp", [P, 4, NB], F32)
                wv = sb(st, "at_wv", [P, 2, NB], F32); WV = Ring("at_wv", wv, 2)
                wr = sb(st, "at_wr", [P, 2, NB], F32); WR = Ring("at_wr", wr, 2)
                m8 = sb(st, "at_m8", [P, 2, 16], F32); M8 = Ring("at_m8", m8, 2)
                selb = sb(st, "at_sel", [P, 2, NB], BF16); SEL = Ring("at_sel", selb, 2)
                rs = sb(st, "at_rs", [P, 4, 2], F32); RS = Ring("at_rs", rs, 4)
                oacc = sb(st, "at_o", [P, 4, 512], F32)
                obf = sb(st, "at_ob", [P, 4, 512], BF16)
                oT = sb(st, "at_oT", [P, 2, 4, 512], BF16); OT = Ring("at_oT", oT, 2)
                PS_S = Ring("psF", psF, 2, base=0); PS_A = Ring("psF", psF, 4, base=2)
                C.dma('pool', Eb[0:NB, :], t_E[:, :], writes=['at_E'])
                C.dma('pool', cmask, t_cmpmask, writes=['at_cm'])
                C.dma('pool', sdiag, t_slcdiag, writes=['at_sd'])
                C.dma('pool', wmask, t_winmask, writes=['at_wm'])
                C.op('dve', lambda e: e.memset(vS[:, :, 128:129], 1.0), writes=['at_vS'])
                C.op('dve', lambda e: e.memset(vW[:, :, 128:129], 1.0), writes=['at_vW'])
                JQ0 = OWN0 // P

                def pipeline(items, fqk, fpv):
                    prev = None
                    for it in items:
                        cur = fqk(it)
                        if prev is not None:
                            fpv(prev[0], *prev[1])
                        prev = (it, cur)
                    if prev is not None:
                        fpv(prev[0], *prev[1])

                def evac(accs, hh, gidx, first, with_imp):
                    for sub in range(4):
                        pa, pak = accs[sub]
                        r_, rk = RS.next()
                        tile_i = None
                        C.op('dve', lambda e: e.tensor_scalar(out=r_[:, 0:1], in0=pa[:, 128:129], scalar1=1e-30, scalar2=None, op0=ALU.max),
                             reads=[pak], writes=[rk])
                        C.op('dve', lambda e: e.reciprocal(out=r_[:, 0:1], in_=r_[:, 0:1]), reads=[rk], writes=[rk])
                        C.op('dve', lambda e: e.tensor_tensor(out=r_[:, 1:2], in0=r_[:, 0:1], in1=gate_sb[:, evac.tile0 + sub, gidx:gidx + 1], op=ALU.mult),
                             reads=[rk, 'gate_sb'], writes=[rk])
                        od = oacc[:, sub, hh * P:(hh + 1) * P]
                        ok = ('at_o', sub, hh)
                        if first:
                            C.op('dve', lambda e: e.tensor_scalar(out=od, in0=pa[:, 0:128], scalar1=r_[:, 1:2], scalar2=None, op0=ALU.mult),
                                 reads=[pak, rk], writes=[ok])
                        else:
                            C.op('dve', lambda e: e.scalar_tensor_tensor(out=od, in0=pa[:, 0:128], scalar=r_[:, 1:2], in1=od, op0=ALU.mult, op1=ALU.add),
                                 reads=[pak, rk, ok], writes=[ok])
                        if with_imp:
                            ik = ('at_imp', sub)
                            if hh == 0:
                                C.op('dve', lambda e: e.tensor_scalar(out=imp[:, sub, :], in0=pa[:, 129:129 + NB], scalar1=r_[:, 0:1], scalar2=None, op0=ALU.mult),
                                     reads=[pak, rk], writes=[ik])
                            else:
                                C.op('dve', lambda e: e.scalar_tensor_tensor(out=imp[:, sub, :], in0=pa[:, 129:129 + NB], scalar=r_[:, 0:1], in1=imp[:, sub, :],
                                                                             op0=ALU.mult, op1=ALU.add), reads=[pak, rk, ik], writes=[ik])

                for g in range(4):
                    C.dma('sp', kS, kslcT_d[g], writes=['at_kS'])
                    C.dma('sp', kW, kwinT_d[g], writes=['at_kW'])
                    C.dma('sp', vS[:, :, 0:128], vslc_d[:, g * P:(g + 1) * P].rearrange("(j p) d -> p j d", p=P), writes=['at_vS'])
                    C.dma('sp', vW[:, :, 0:128], vwin_d[:, g * P:(g + 1) * P].rearrange("(j p) d -> p j d", p=P), writes=['at_vW'])
                    for i in range(NCHO):
                        q_, qk = QT.next()
                        C.dma('sp', q_, qT_d[4 * g:4 * g + 4, :, i * 512:(i + 1) * 512].rearrange("h p t -> p h t"), writes=[qk])
                        evac.tile0 = i * 4
                        jd0 = JQ0 + 4 * i
                        jcs = [jc for jc in range(NCT) if 16 * P * jc + 31 <= OWN0 + 512 * i + 511]
                        for hh in range(4):
                            accs = [PS_A.next() for _ in range(4)]

                            def qk_c(jc):
                                ps, psk = PS_S.next()
                                C.op('pe', lambda e: e.matmul(ps[:, 0:512], lhsT=kcT[:, g, jc * P:(jc + 1) * P], rhs=q_[:, hh, :], start=True, stop=True),
                                     reads=['kcT', qk], writes=[psk])
                                return ps, psk

                            def pv_c(jc, ps, psk):
                                p_, pk = PT.next()
                                C.op('act', lambda e: e.activation(out=p_, in_=ps[:, 0:512], func=AF.Exp, scale=SCALE), reads=[psk], writes=[pk])
                                C.op('dve', lambda e: e.tensor_tensor(out=p_, in0=p_, in1=cmask[:, i, jc, :], op=ALU.mult), reads=[pk, 'at_cm'], writes=[pk])
                                for sub in range(4):
                                    pa, pak = accs[sub]
                                    C.op('pe', lambda e: e.matmul(pa[:, 0:AW], lhsT=p_[:, sub * P:(sub + 1) * P], rhs=vca[:, g, jc, :],
                                                                  start=(jc == jcs[0]), stop=(jc == jcs[-1])),
                                         reads=[pk, 'vca'], writes=[pak], inc=(jc == jcs[-1]))
                            pipeline(jcs, qk_c, pv_c)
                            evac(accs, hh, (4 * g + hh) * 3 + 0, True, True)
                        for sub in range(4):
                            a_, ak = TA.next(); b_, bk = TB.next()
                            r0 = i * 512 + sub * P
                            C.dma('sp', a_, t_topA[r0:r0 + P, :], writes=[ak])
                            C.dma('sp', b_, t_topB[r0:r0 + P, :], writes=[bk])
                            w_, wk_ = WV.next(); w2_, w2k = WR.next(); m_, mk = M8.next(); s_, sk = SEL.next()
                            C.op('dve', lambda e: e.tensor_tensor(out=w_, in0=imp[:, sub, :], in1=a_, op=ALU.mult), reads=[('at_imp', sub), ak], writes=[wk_])
                            C.op('dve', lambda e: e.tensor_tensor(out=w_, in0=w_, in1=b_, op=ALU.add), reads=[wk_, bk], writes=[wk_])
                            C.op('dve', lambda e: e.max(out=m_[:, 0:8], in_=w_), reads=[wk_], writes=[mk])
                            C.op('dve', lambda e: e.match_replace(out=w2_, in_to_replace=m_[:, 0:8], in_values=w_, imm_value=-1e30),
                                 reads=[wk_, mk], writes=[w2k])
                            C.op('dve', lambda e: e.max(out=m_[:, 8:16], in_=w2_), reads=[w2k], writes=[mk])
                            C.op('dve', lambda e: e.tensor_scalar(out=m_[:, 15:16], in0=m_[:, 15:16], scalar1=0.0, scalar2=None, op0=ALU.max),
                                 reads=[mk], writes=[mk])
                            C.op('dve', lambda e: e.tensor_scalar(out=w2_, in0=w_, scalar1=m_[:, 15:16], scalar2=None, op0=ALU.is_ge),
                                 reads=[wk_, mk], writes=[w2k])
                            C.op('dve', lambda e: e.tensor_scalar(out=s_, in0=w2_, scalar1=-NEGB, scalar2=NEGB, op0=ALU.mult, op1=ALU.add),
                                 reads=[w2k], writes=[sk])
                            p2, p2k = PSB.next()
                            C.op('pe', lambda e: e.transpose(out=p2[0:NB, 0:P], in_=s_, identity=ident[:]), reads=[sk, 'ident'], writes=[p2k])
                            C.op('act', lambda e: e.copy(out=nsT[0:NB, sub * P:(sub + 1) * P], in_=p2[0:NB, 0:P]), reads=[p2k], writes=['at_ns'])
                        for hh in range(4):
                            accs = [PS_A.next() for _ in range(4)]

                            def qk_s(j):
                                ps, psk = PS_S.next()
                                C.op('pe', lambda e: e.matmul(ps[:, 0:512], lhsT=kS[:, j * P:(j + 1) * P], rhs=q_[:, hh, :], start=True, stop=False),
                                     reads=['at_kS', qk], writes=[psk], inc=False)
                                C.op('pe', lambda e: e.matmul(ps[:, 0:512], lhsT=Eb[0:NB, j * P:(j + 1) * P], rhs=nsT[0:NB, :], start=False, stop=True),
                                     reads=['at_E', 'at_ns'], writes=[psk])
                                return ps, psk

                            def pv_s(j, ps, psk):
                                p_, pk = PT.next()
                                C.op('act', lambda e: e.activation(out=p_, in_=ps[:, 0:512], func=AF.Exp, scale=SCALE), reads=[psk], writes=[pk])
                                if j >= jd0:
                                    C.op('dve', lambda e: e.tensor_tensor(out=p_, in0=p_, in1=sdiag[:, j - jd0, :], op=ALU.mult),
                                         reads=[pk, 'at_sd'], writes=[pk])
                                for sub in range(4):
                                    if j > jd0 + sub:
                                        continue
                                    pa, pak = accs[sub]
                                    C.op('pe', lambda e: e.matmul(pa[:, 0:129], lhsT=p_[:, sub * P:(sub + 1) * P], rhs=vS[:, j, :],
                                                                  start=(j == 0), stop=(j == jd0 + sub)),
                                         reads=[pk, 'at_vS'], writes=[pak], inc=(j == jd0 + sub))
                            pipeline(list(range(jd0 + 4)), qk_s, pv_s)
                            evac(accs, hh, (4 * g + hh) * 3 + 1, False, False)
                        for hh in range(4):
                            accs = [PS_A.next() for _ in range(4)]

                            def qk_w(k8):
                                j = jd0 - 4 + k8
                                ps, psk = PS_S.next()
                                C.op('pe', lambda e: e.matmul(ps[:, 0:512], lhsT=kW[:, j * P:(j + 1) * P], rhs=q_[:, hh, :], start=True, stop=True),
                                     reads=['at_kW', qk], writes=[psk])
                                return ps, psk

                            def pv_w(k8, ps, psk):
                                j = jd0 - 4 + k8
                                p_, pk = PT.next()
                                C.op('act', lambda e: e.activation(out=p_, in_=ps[:, 0:512], func=AF.Exp, scale=SCALE), reads=[psk], writes=[pk])
                                C.op('dve', lambda e: e.tensor_tensor(out=p_, in0=p_, in1=wmask[:, 0 if i == 0 else 1, k8, :], op=ALU.mult),
                                     reads=[pk, 'at_wm'], writes=[pk])
                                for sub in range(4):
                                    if not (sub <= k8 <= sub + 4):
                                        continue
                                    pa, pak = accs[sub]
                                    C.op('pe', lambda e: e.matmul(pa[:, 0:129], lhsT=p_[:, sub * P:(sub + 1) * P], rhs=vW[:, j, :],
                                                                  start=(k8 == sub), stop=(k8 == sub + 4)),
                                         reads=[pk, 'at_vW'], writes=[pak], inc=(k8 == sub + 4))
                            pipeline(list(range(8)), qk_w, pv_w)
                            evac(accs, hh, (4 * g + hh) * 3 + 2, False, False)
                        o_, otk = OT.next()
                        for sub in range(4):
                            oks = [('at_o', sub, hh) for hh in range(4)]
                            C.op('act', lambda e: e.copy(out=obf[:, sub, :], in_=oacc[:, sub, :]), reads=oks, writes=[('at_ob', sub)])
                            p2, p2k = PSB.next()
                            for hh in range(4):
                                C.op('pe', lambda e: e.transpose(out=p2[:, hh * P:(hh + 1) * P], in_=obf[:, sub, hh * P:(hh + 1) * P], identity=ident[:]),
                                     reads=[('at_ob', sub), 'ident'], writes=[p2k], inc=(hh == 3))
                            evac_copy(sub, o_[:, :, sub * P:(sub + 1) * P], p2[:, 0:512].rearrange("p (h t) -> p h t", h=4), [p2k], [otk])
                        C.dma('sp', attnT_d[4 * g * P:(4 * g + 4) * P, i * 512:(i + 1) * 512].rearrange("(h p) t -> p h t", p=P), o_, reads=[otk])
            C.barrier()

        if stages >= 8:
            with ExitStack() as st:
                gg = sb(st, "g3_g", [P, 8, 512], BF16); GG = Ring("g3_g", gg, 8)
                t1 = sb(st, "g3_t1", [P, 2, 512], F32); T1 = Ring("g3_t1", t1, 2)
                t2 = sb(st, "g3_t2", [P, 2, 512], F32); T2 = Ring("g3_t2", t2, 2)
                ob = sb(st, "g3_o", [P, 2, 512], BF16); OB = Ring("g3_o", ob, 2)

                def pre(bi, ci, t0):
                    res = []
                    for ct in range(2):
                        row = bi * 256 + ct * P
                        ga, gak = GG.next(); gb_, gbk = GG.next()
                        C.dma('act', ga, gT_d[row:row + P, t0:t0 + 512], writes=[gak])
                        C.dma('act', gb_, gT_d[DM + row:DM + row + P, t0:t0 + 512], writes=[gbk])
                        res.append((ga, gak, gb_, gbk))
                    return res

                def epi(bi, ci, t0, pts, pr):
                    to = t0
                    for ct in range(2):
                        row = bi * 256 + ct * P
                        (pa, pak), (pb_, pbk) = pts[ct], pts[2 + ct]
                        ga, gak, gb_, gbk = pr[ct]
                        a_, ak = T1.next(); b_, bk = T2.next(); o_, ok = OB.next()
                        C.op('dve', lambda e: e.tensor_tensor(out=a_, in0=pa[:, 0:512], in1=ga, op=ALU.mult), reads=[pak, gak], writes=[ak])
                        C.op('dve', lambda e: e.tensor_tensor(out=b_, in0=pb_[:, 0:512], in1=gb_, op=ALU.mult), reads=[pbk, gbk], writes=[bk])
                        C.op('dve', lambda e: e.tensor_tensor(out=o_, in0=a_, in1=b_, op=ALU.add), reads=[ak, bk], writes=[ok])
                        C.dma('sp', mixT_d[row:row + P, to:to + 512], o_, reads=[ok])
                blocks = [[(attnT_d, 16, [(w_bra, 256 * b_, 256)]), (lruoT_d, NLT, [(w_brb, 256 * b_, 256)])] for b_ in range(DM // 256)]
                gemm("g3", 'b', blocks, 0, TOWN, 512, epi, resident=True, pre=pre)

        if stages >= 9:
            with ExitStack() as st:
                xr = sb(st, "g4_x", [P, 8, 256], F32); XR = Ring("g4_x", xr, 8)

                def pre(bi, ci, t0):
                    res = []
                    for j in range(4):
                        x_, xk = XR.next()
                        r0 = t0 + j * P
                        C.dma('act', x_, x_ctx[OWN0 + r0:OWN0 + r0 + P, bi * 256:(bi + 1) * 256], writes=[xk])
                        res.append((x_, xk))
                    return res

                def epi(bi, ci, t0, pts, pr):
                    for j, (pt, pk) in enumerate(pts):
                        x_, xk = pr[j]
                        r0 = t0 + j * P
                        C.op('dve', lambda e: e.tensor_tensor(out=x_, in0=pt[:, 0:256], in1=x_, op=ALU.add), reads=[pk, xk], writes=[xk])
                        C.dma('sp', x1_d[r0:r0 + P, bi * 256:(bi + 1) * 256], x_, reads=[xk])
                gemm("g4", 'a', [[(mixT_d, KT, [(w_out, 256 * b_, 256)])] for b_ in range(DM // 256)], 0, TOWN, 512, epi,
                     resident=True, pre=pre)
            phase_norm("n2", x1_d, TOWN, norm2_g, h2T_d)

        if stages >= 10:
            with ExitStack() as st:
                sg_ = sb(st, "g5_s", [P, 2, 512], F32); SG = Ring("g5_s", sg_, 2)
                ob = sb(st, "g5_o", [P, 3, 512], BF16); OB = Ring("g5_o", ob, 3)

                def epi(bi, ci, t0, pts):
                    row = bi * P
                    (pg, pgk), (pu, puk) = pts[0], pts[1]
                    s_, sk = SG.next(); o_, ok = OB.next()
                    C.op('act', lambda e: e.activation(out=s_, in_=pg[:, 0:512], func=AF.Silu), reads=[pgk], writes=[sk])
                    C.op('dve', lambda e: e.tensor_tensor(out=o_, in0=pu[:, 0:512], in1=s_, op=ALU.mult), reads=[puk, sk], writes=[ok])
                    for hf in range(2):
                        C.dma('sp', actT_d[t0 // 256 + hf][:, bi, :], o_[:, hf * 256:(hf + 1) * 256], reads=[ok])
                blocks = [[(h2T_src, KT, [(w_ffi, P * b_, P), (w_ffi, FH + P * b_, P)])] for b_ in range(FH // P)]
                gemm("g5", 'b', blocks, 0, TOWN, 512, epi, resident=True)
            with ExitStack() as st:
                xr = sb(st, "g6_x", [P, 4, 512], F32); XR = Ring("g6_x", xr, 4)

                def pre(bi, ci, t0):
                    res = []
                    for j in range(2):
                        x_, xk = XR.next()
                        r0 = t0 + j * P
                        C.dma('act', x_, x1_d[r0:r0 + P, bi * 512:(bi + 1) * 512], writes=[xk])
                        res.append((x_, xk))
                    return res

                def epi(bi, ci, t0, pts, pr):
                    for j, (pt, pk) in enumerate(pts):
                        x_, xk = pr[j]
                        r0 = t0 + j * P
                        C.op('dve', lambda e: e.tensor_tensor(out=x_, in0=pt[:, 0:512], in1=x_, op=ALU.add), reads=[pk, xk], writes=[xk])
                        C.dma('sp', out[r0:r0 + P, bi * 512:(bi + 1) * 512], x_, reads=[xk])
                gemm("g6", 'a', [[(actT_src, FH // P, [(w_ffo, 512 * b_, 512)])] for b_ in range(DM // 512)], 0, TOWN, 256, epi,
                     wslots=1, aslots=2, pre=pre)
        else:
            with ExitStack() as st:
                z = sb(st, "zz", [P, DM], F32)
                C.op('dve', lambda e: e.memset(z[:], 0.0), writes=['zz'])
                for i in range(TOWN // P):
                    C.dma('sp', out[i * P:(i + 1) * P, :], z[:], reads=['zz'])
        C.barrier()
    return nc


def make_tables(cfg, r):
    TCTX, TOWN = cfg['TCTX'], cfg['TOWN']
    NB = TCTX // 64; NCP = TCTX // 16; NC = NCP - 1; NCT = NCP // P
    NCHC = TCTX // 512; NCHO = TOWN // 512
    OWN0 = TCTX - TOWN
    pad = TCTX - TOWN * (r + 1)
    inv = 1.0 / (500000.0 ** (np.arange(16, dtype=np.float32) * 2.0 / 32.0))
    pos = (np.arange(TCTX) - pad).astype(np.float32)
    ang = pos[:, None] * inv[None, :].astype(np.float32)
    cs_tok = np.concatenate([np.cos(ang), np.sin(ang)], axis=1).astype(np.float32)
    cend = (np.arange(NCP) * 16 + 31 - pad).astype(np.float32)
    ang = cend[:, None] * inv[None, :].astype(np.float32)
    cs_cmp = np.concatenate([np.cos(ang), np.sin(ang)], axis=1).astype(np.float32)
    c_l = np.arange(P)[:, None, None, None]; i_ = np.arange(NCHO)[None, :, None, None]
    jc = np.arange(NCT)[None, None, :, None]; ql = np.arange(512)[None, None, None, :]
    c = jc * P + c_l; t = OWN0 + 512 * i_ + ql
    cmpmask = ((c < NC) & (16 * c >= pad) & (16 * c + 31 <= t)).astype(np.float32)
    cc = np.arange(NCP)[:, None]; nn = np.arange(NB)[None, :]
    overlap = ((16 * cc <= 64 * nn + 63) & (16 * cc + 31 >= 64 * nn) & (cc < NC)).astype(np.float32)
    E = (np.arange(TCTX)[None, :] // 64 == np.arange(NB)[:, None]).astype(np.float32)
    kl = np.arange(P)[:, None, None]; jj = np.arange(4)[None, :, None]; q2 = np.arange(512)[None, None, :]
    slcdiag = ((128 * jj + kl) <= q2).astype(np.float32)
    kl4 = np.arange(P)[:, None, None, None]; var = np.arange(2)[None, :, None, None]
    kt8 = np.arange(8)[None, None, :, None]; q4 = np.arange(512)[None, None, None, :]
    krel = -512 + 128 * kt8 + kl4
    wm = (krel <= q4) & (krel > q4 - 512)
    wm = np.broadcast_to(wm, (P, 2, 8, 512)).copy()
    wm[:, 0] &= np.broadcast_to((OWN0 + krel[:, 0] >= pad), (P, 8, 512))
    winmask = wm.astype(np.float32)
    tt = (OWN0 + np.arange(TOWN))[:, None]
    valid = (nn * 64 >= pad) & (nn * 64 <= tt)
    cur = tt // 64
    forced = ((nn == pad // 64) | (nn == cur) | (nn == cur - 1)) & valid
    topA = (valid & ~forced).astype(np.float32)
    fval = np.where(nn == cur, 3e30, np.where(nn == cur - 1, 2e30, 1e30))
    topB = np.where(forced, fval, np.where(valid, 0.0, -1.0)).astype(np.float32)
    vchunk = np.broadcast_to(((np.arange(NCHC) * 512) >= pad).astype(np.float32)[None, :], (P, NCHC)).copy()
    return dict(cs_tok=cs_tok, cs_cmp=cs_cmp, t_cmpmask=cmpmask, t_overlap=overlap, t_E=E, t_slcdiag=slcdiag,
                t_winmask=winmask, t_topA=topA, t_topB=topB, t_vchunk=vchunk)


def kernel(debug_outs=(), stages=99, **inputs):
    x = np.asarray(inputs["x"])
    B, S, DM = x.shape
    LW = DM // 2
    FH = np.asarray(inputs["w_ffn_out"]).shape[1]
    cfg = dict(DM=DM, TCTX=S, TOWN=S // 4, LW=LW, FH=FH, stages=stages)
    TOWN = cfg['TOWN']
    nc = build_program(cfg, debug_outs)
    names1 = ["norm1_g", "norm2_g", "q_norm_g", "conv_b", "lru_lambda"]
    shared = {}
    for k, v in inputs.items():
        if k == "x":
            continue
        a = np.asarray(v, dtype=np.float32)[0]
        shared[k] = np.ascontiguousarray(a)
    n_cores = B * 4
    in_maps = []
    for c in range(n_cores):
        b, r = c // 4, c % 4
        pad = S - TOWN * (r + 1)
        xc = np.zeros((S, DM), np.float32)
        xc[pad:] = x[b, :TOWN * (r + 1)]
        m = dict(shared)
        m["x_ctx"] = xc
        m.update(make_tables(cfg, r))
        in_maps.append(m)
    res = run_bass_kernel_spmd(nc, in_maps, core_ids=list(range(n_cores)))
    outp = np.zeros((B, S, DM), np.float32)
    for c in range(n_cores):
        b, r = c // 4, c % 4
        outp[b, r * TOWN:(r + 1) * TOWN] = res.results[c]["out"]
    if debug_outs:
        return outp, res.results
    return outp
```
